# Optimizing a Trainium2 kernel written in Bass

```python
import jax, jax.numpy as jnp
from jax import lax
import numpy as np

D_MODEL = 2048
BATCH = 4
SEQ = 8192
DEPTH = 1

GRID_W = 64
EPS = 1e-6
ROPE_THETA = 10000.0
DA_HEADS = 4
DA_HEAD_DIM = 128
DA_WIDTH = DA_HEADS * 2 * DA_HEAD_DIM
Q_BLOCK = 128
NA_HEADS = 8
NA_HEAD_DIM = 128
NA_WIDTH = NA_HEADS * NA_HEAD_DIM
NA_KH_MAX = 8
NA_KW = 16
D_FF = 4 * D_MODEL
N_BRANCHES = 2
IN_SPLITS = [DA_WIDTH, DA_WIDTH, DA_WIDTH, NA_WIDTH, NA_WIDTH, NA_WIDTH, N_BRANCHES * D_MODEL]
IN_COLS = sum(IN_SPLITS)

kernel_name = "hybrid_diffattn_neighattn_gated_encoder"


def rms_norm(x, w):
    xf = x.astype(jnp.float32)
    y = xf * lax.rsqrt(jnp.mean(xf * xf, axis=-1, keepdims=True) + EPS)
    return (y * w.astype(jnp.float32)).astype(x.dtype)


def rope_tables(seq, dim):
    inv = 1.0 / (ROPE_THETA ** (jnp.arange(0, dim, 2, dtype=jnp.float32) / dim))
    ang = jnp.arange(seq, dtype=jnp.float32)[:, None] * inv[None, :]
    return jnp.cos(ang), jnp.sin(ang)


def apply_rope(x, cos, sin):
    xf = x.astype(jnp.float32)
    x1, x2 = jnp.split(xf, 2, axis=-1)
    c = cos[:, None, None, :]
    s = sin[:, None, None, :]
    out = jnp.concatenate([x1 * c - x2 * s, x2 * c + x1 * s], axis=-1)
    return out.astype(x.dtype)


def diff_attention(q, k, v, lam):
    B, S, H, _, d = q.shape
    nb = S // Q_BLOCK
    qb = q.reshape(B, nb, Q_BLOCK, H, 2, d).transpose(1, 0, 2, 3, 4, 5)
    scale = d ** -0.5

    def block(qi):
        s = jnp.einsum('bqhcd,bkhcd->bhcqk', qi, k, preferred_element_type=jnp.float32) * scale
        p = jax.nn.softmax(s, axis=-1)
        p = p[:, :, 0] - lam * p[:, :, 1]
        return jnp.einsum('bhqk,bkhe->bqhe', p.astype(v.dtype), v)

    o = lax.map(block, qb)
    return o.transpose(1, 0, 2, 3, 4).reshape(B, S, H, 2 * d)


def neighbourhood_attention(q, k, v, rpb):
    B, S, H, d = q.shape
    rows = S // GRID_W
    kh = min(NA_KH_MAX, rows)
    qg = q.reshape(B, rows, GRID_W, H, d).transpose(1, 0, 2, 3, 4)
    kg = k.reshape(B, rows, GRID_W, H, d)
    vg = v.reshape(B, rows, GRID_W, H, d)
    col = jnp.arange(GRID_W)
    col_start = jnp.clip(col - NA_KW // 2, 0, GRID_W - NA_KW)
    col_idx = col_start[:, None] + jnp.arange(NA_KW)[None, :]
    dc = col_idx - col[:, None] + (NA_KW - 1)
    row_ids = jnp.arange(rows)
    row_start = jnp.clip(row_ids - kh // 2, 0, rows - kh)
    scale = d ** -0.5

    def row_block(args):
        qr, r, rs = args
        kb = lax.dynamic_slice_in_dim(kg, rs, kh, axis=1)
        vb = lax.dynamic_slice_in_dim(vg, rs, kh, axis=1)
        kw = kb[:, :, col_idx]
        vw = vb[:, :, col_idx]
        dr = rs + jnp.arange(kh) - r + (NA_KH_MAX - 1)
        bias = rpb[:, dr[None, :, None], dc[:, None, :]]
        s = jnp.einsum('bqhd,bnqwhd->bhqnw', qr, kw, preferred_element_type=jnp.float32) * scale
        s = s + bias.astype(jnp.float32)[None]
        p = jax.nn.softmax(s, axis=(-2, -1))
        return jnp.einsum('bhqnw,bnqwhd->bqhd', p.astype(v.dtype), vw)

    o = lax.map(row_block, (qg, row_ids, row_start))
    return o.transpose(1, 0, 2, 3, 4).reshape(B, S, H, d)


def setup_inputs(seed: int = 0) -> dict:
    key = jax.random.key(seed)
    ks = jax.random.split(key, 20)
    f32 = jnp.float32

    def nrm(k, shape, scale):
        return jax.random.normal(k, shape, f32) * scale

    def gain(k, n):
        return 1.0 + 0.05 * jax.random.normal(k, (DEPTH, n), f32)

    return {
        "x": jax.random.normal(ks[0], (BATCH, SEQ, D_MODEL), f32),
        "w_in": nrm(ks[1], (DEPTH, D_MODEL, IN_COLS), D_MODEL ** -0.5),
        "w_branch_a": nrm(ks[2], (DEPTH, DA_WIDTH, D_MODEL), DA_WIDTH ** -0.5),
        "w_branch_b": nrm(ks[3], (DEPTH, NA_WIDTH, D_MODEL), NA_WIDTH ** -0.5),
        "w_out": nrm(ks[4], (DEPTH, D_MODEL, D_MODEL), D_MODEL ** -0.5),
        "norm_mix_pre": gain(ks[5], D_MODEL),
        "norm_mix_post": gain(ks[6], D_MODEL),
        "norm_mlp_pre": gain(ks[7], D_MODEL),
        "norm_mlp_post": gain(ks[8], D_MODEL),
        "lam_q1": nrm(ks[9], (DEPTH, DA_HEAD_DIM), 0.1),
        "lam_k1": nrm(ks[10], (DEPTH, DA_HEAD_DIM), 0.1),
        "lam_q2": nrm(ks[11], (DEPTH, DA_HEAD_DIM), 0.1),
        "lam_k2": nrm(ks[12], (DEPTH, DA_HEAD_DIM), 0.1),
        "subln_w": gain(ks[13], 2 * DA_HEAD_DIM),
        "na_rpb": nrm(ks[14], (DEPTH, NA_HEADS, 2 * NA_KH_MAX - 1, 2 * NA_KW - 1), 0.1),
        "w_up": nrm(ks[15], (DEPTH, D_MODEL, D_FF), D_MODEL ** -0.5),
        "w_down": nrm(ks[16], (DEPTH, D_FF, D_MODEL), D_FF ** -0.5),
    }


def reference(x, w_in, w_branch_a, w_branch_b, w_out, norm_mix_pre, norm_mix_post,
              norm_mlp_pre, norm_mlp_post, lam_q1, lam_k1, lam_q2, lam_k2, subln_w,
              na_rpb, w_up, w_down):
    B, S, _ = x.shape
    cos, sin = rope_tables(S, DA_HEAD_DIM)
    split_at = [int(c) for c in np.cumsum(IN_SPLITS)[:-1]]
    for l in range(DEPTH):
        lambda_init = 0.8 - 0.6 * float(np.exp(-0.3 * l))
        h = rms_norm(x, norm_mix_pre[l])
        proj = jnp.einsum('bsd,dc->bsc', h, w_in[l])
        qa, ka, va, qn, kn, vn, gates = jnp.split(proj, split_at, axis=-1)
        qa = apply_rope(qa.reshape(B, S, DA_HEADS, 2, DA_HEAD_DIM), cos, sin)
        ka = apply_rope(ka.reshape(B, S, DA_HEADS, 2, DA_HEAD_DIM), cos, sin)
        va = va.reshape(B, S, DA_HEADS, 2 * DA_HEAD_DIM)
        lam = (jnp.exp(jnp.sum(lam_q1[l].astype(jnp.float32) * lam_k1[l].astype(jnp.float32)))
               - jnp.exp(jnp.sum(lam_q2[l].astype(jnp.float32) * lam_k2[l].astype(jnp.float32)))
               + lambda_init)
        oa = diff_attention(qa, ka, va, lam)
        oa = (rms_norm(oa, subln_w[l]) * (1.0 - lambda_init)).reshape(B, S, DA_WIDTH)
        on = neighbourhood_attention(qn.reshape(B, S, NA_HEADS, NA_HEAD_DIM),
                                     kn.reshape(B, S, NA_HEADS, NA_HEAD_DIM),
                                     vn.reshape(B, S, NA_HEADS, NA_HEAD_DIM),
                                     na_rpb[l]).reshape(B, S, NA_WIDTH)
        g_a, g_b = jnp.split(jax.nn.sigmoid(gates), N_BRANCHES, axis=-1)
        mixed = (g_a * jnp.einsum('bsc,cd->bsd', oa, w_branch_a[l])
                 + g_b * jnp.einsum('bsc,cd->bsd', on, w_branch_b[l]))
        y = jnp.einsum('bsd,de->bse', mixed, w_out[l])
        x = x + rms_norm(y, norm_mix_post[l])
        h = rms_norm(x, norm_mlp_pre[l])
        u = jnp.square(jax.nn.relu(jnp.einsum('bsd,df->bsf', h, w_up[l])))
        x = x + rms_norm(jnp.einsum('bsf,fd->bsd', u, w_down[l]), norm_mlp_post[l])
    return x
```

```python
import numpy as np
import ml_dtypes
import concourse.bass as bass
import concourse.mybir as mybir
from concourse.bass_utils import run_bass_kernel_spmd

F32 = mybir.dt.float32
BF16 = mybir.dt.bfloat16
AF = mybir.ActivationFunctionType
ALU = mybir.AluOpType
AX = mybir.AxisListType

D = 2048
DFF = 8192
INC = 10240
EPS = 1e-6
GRID_W = 64
NEG = -30000.0
LAMBDA_INIT = 0.8 - 0.6 * 1.0
SCALE = 128 ** -0.5


class Buf:
    __slots__ = ("name", "last_w", "readers", "sem", "cnt", "base", "excl")

    def __init__(self, name):
        self.name = name
        self.excl = False
        self.last_w = None
        self.readers = []
        self.sem = None
        self.cnt = 0
        self.base = ()


class Op:
    __slots__ = ("eng", "fn", "deps", "is_dma", "sem", "val", "waited")

    def __init__(self, eng, fn):
        self.eng = eng
        self.fn = fn
        self.deps = []
        self.is_dma = False
        self.sem = None
        self.val = 0
        self.waited = False


class Prog:
    ENGS = ("pe", "act", "dve", "pool", "sp")
    SAME_ENG_SYNC = ("act", "dve", "pool")

    def __init__(self, nc):
        self.nc = nc
        self.streams = {e: [] for e in self.ENGS}
        self.eng_sem = {}
        self.free_sems = []
        self.phase_bufs = []
        self.barrier = []
        self.nsem = 0

    def new_sem(self, name):
        self.nsem += 1
        return self.nc.alloc_semaphore(f"{name}_{self.nsem}")

    def new_phase(self):
        bar = []
        seen = set()
        for b in self.phase_bufs:
            for o in ([b.last_w] if b.last_w is not None else []) + b.readers:
                if id(o) not in seen:
                    seen.add(id(o))
                    bar.append(o)
            for o in b.base:
                if id(o) not in seen:
                    seen.add(id(o))
                    bar.append(o)
            if b.sem is not None:
                self.free_sems.append((b.sem, b.cnt))
        self.barrier = bar
        self.phase_bufs = []

    def buf(self, name, dma=False):
        b = Buf(name)
        b.readers = list(self.barrier)
        b.base = tuple(self.barrier)
        if dma:
            if self.free_sems:
                b.sem, b.cnt = self.free_sems.pop()
            else:
                b.sem = self.new_sem("d_" + name)
        self.phase_bufs.append(b)
        return b

    def op(self, eng, fn, reads=(), writes=(), dma_buf=None):
        o = Op(eng, fn)
        deps = []
        seen = set()

        def add(d):
            if d is None or id(d) in seen:
                return
            if (not d.is_dma) and d.eng == eng and eng not in self.SAME_ENG_SYNC:
                return
            seen.add(id(d))
            deps.append(d)

        for b in reads:
            add(b.last_w)
            if b.last_w is None:
                for r in b.base:
                    add(r)
            if b.excl:
                for r in b.readers:
                    if r.eng != eng:
                        add(r)
        for b in writes:
            add(b.last_w)
            for r in b.readers:
                add(r)
        o.deps = deps
        for d in deps:
            d.waited = True
        if dma_buf is not None:
            o.is_dma = True
            dma_buf.cnt += 16
            o.sem = dma_buf.sem
            o.val = dma_buf.cnt
        for b in writes:
            b.last_w = o
            b.readers = []
        for b in reads:
            if b.last_w is o:
                continue
            if not o.is_dma:
                b.readers = [r for r in b.readers if r.is_dma or r.eng != eng]
            b.readers.append(o)
        self.streams[eng].append(o)
        return o

    def emit(self):
        nc = self.nc
        for e in ("pe", "act", "dve", "pool"):
            self.eng_sem[e] = self.new_sem("e_" + e)
            r = 0
            for o in self.streams[e]:
                if o.waited and not o.is_dma:
                    r += 1
                    o.sem = self.eng_sem[e]
                    o.val = r
        streams = self.streams

        def run(e, eng):
            known = {}
            for o in streams[e]:
                need = {}
                for d in o.deps:
                    k = d.sem.num
                    if known.get(k, 0) >= d.val:
                        continue
                    if k not in need or need[k][1] < d.val:
                        need[k] = (d.sem, d.val)
                for k, (sm_, v_) in need.items():
                    eng.wait_ge(sm_, v_)
                    known[k] = v_
                ins = o.fn(eng)
                if o.is_dma:
                    ins.then_inc(o.sem, 16)
                elif o.waited:
                    ins.then_inc(o.sem, 1)

        with nc.Block() as block:
            @block.tensor
            def _(eng):
                run("pe", eng)

            @block.scalar
            def _(eng):
                run("act", eng)

            @block.vector
            def _(eng):
                run("dve", eng)

            @block.gpsimd
            def _(eng):
                run("pool", eng)

            @block.sync
            def _(eng):
                run("sp", eng)


def build_program(T, debug=False, stop_after=5):
    assert T % 512 == 0 and T >= 1024
    NQT = T // 128
    NB = T // 512
    NKT = 2 * NQT
    NEXT = NQT + 4
    nc = bass.Bass("TRN2", target_bir_lowering=False)
    P = Prog(nc)

    def din(name, shape, dt=F32):
        return nc.dram_tensor(name, shape, dt, kind="ExternalInput").ap()

    def dscr(name, shape, dt):
        kind = "ExternalOutput" if (debug and not name.startswith("w")) else "Internal"
        return nc.dram_tensor(name, shape, dt, kind=kind).ap()

    x_own = din("x_own", [T, D])
    x_oth = din("x_oth", [T, D])
    w_in = din("w_in", [D, INC])
    w_ab = din("w_ab", [2048, 2048])
    w_out = din("w_out", [2048, 2048])
    w_up = din("w_up", [D, DFF])
    w_dn = din("w_dn", [DFF, D])
    gains = din("gains", [4, D])
    lamv = din("lamv", [1, 512])
    subw = din("subw", [1, 256])
    nbias = din("nbias", [8, 128, 27, 128])
    cs_own = din("cs_own", [128, 2, T])
    cs_oth = din("cs_oth", [128, 2, T])
    ident_d = din("ident", [128, 128], BF16)
    out = nc.dram_tensor("out", [T, D], F32, kind="ExternalOutput").ap()

    wi_bf = dscr("wi_bf", [20, 128, 16, 512], BF16)
    wab_bf = dscr("wab_bf", [4, 128, 16, 512], BF16)
    wout_bf = dscr("wout_bf", [4, 128, 16, 512], BF16)
    wup_bf = dscr("wup_bf", [16, 128, 16, 512], BF16)
    wdn_bf = dscr("wdn_bf", [16, 128, 16, 512], BF16)
    QaT = dscr("QaT", [8, 128, T], BF16)
    KaT = dscr("KaT", [8, 128, 2 * T], BF16)
    Va = dscr("Va", [4, 2 * T, 256], BF16)
    QnT = dscr("QnT", [8, 128, T], BF16)
    KnT = dscr("KnT", [8, 128, NEXT * 128], BF16)
    Vn = dscr("Vn", [8, NEXT * 128, 128], BF16)
    gT = dscr("gT", [32, 128, T], BF16)
    oaT = dscr("oaT", [8, 128, T], BF16)
    onT = dscr("onT", [8, 128, T], BF16)
    x1 = dscr("x1", [T, D], F32)

    dram = {}

    def dbuf(key):
        b = dram.get(key)
        if b is None:
            b = Buf(str(key))
            dram[key] = b
        return b

    SB_BASE = 16512
    SB_TOP = 229344
    sb_off = [SB_BASE]
    sb_persist = [SB_BASE]
    uid = [0]

    def sb(name, shape, dt):
        nbytes = int(np.prod(shape[1:])) * (4 if dt == F32 else 2)
        nbytes = (nbytes + 63) // 64 * 64
        uid[0] += 1
        t = nc.alloc_sbuf_tensor_at(f"{name}_{uid[0]}", list(shape), dt, offset=sb_off[0])
        sb_off[0] += nbytes
        assert sb_off[0] <= SB_TOP, (name, sb_off[0])
        return t

    def phase_start():
        P.new_phase()
        sb_off[0] = sb_persist[0]

    ps = [nc.alloc_psum_tensor(f"ps{i}", [128, 512], F32) for i in range(8)]
    psb = [p[:].bitcast(BF16) for p in ps]
    b_ps = [Buf(f"ps{i}") for i in range(8)]
    for b_ in b_ps:
        b_.excl = True

    def dma(eng, out_ap, in_ap, reads, writes, sem_buf):
        return P.op(eng, lambda e: e.dma_start(out=out_ap, in_=in_ap), reads, writes, dma_buf=sem_buf)

    idt = sb("idt", [128, 128], BF16)
    Rm = sb("Rm", [128, 128], BF16)
    mhalf = sb("mhalf", [128, 1], F32)
    sb_persist[0] = sb_off[0]
    b_idt = Buf("idt"); b_idt.sem = P.new_sem("d_idt")
    b_Rm = Buf("Rm")
    b_mhalf = Buf("mhalf")
    dma("sp", idt[:], ident_d, [], [b_idt], b_idt)
    P.op("pool", lambda e: e.memset(mhalf[:], -0.5), [], [b_mhalf])
    P.op("pool", lambda e: e.memset(Rm[:], 0.0), [], [b_Rm])
    P.op("dve", lambda e: e.tensor_copy(out=Rm[0:64, 64:128], in_=idt[0:64, 0:64]), [b_idt], [b_Rm])
    P.op("dve", lambda e: e.tensor_copy(out=Rm[64:128, 0:64], in_=idt[64:128, 64:128]), [b_idt], [b_Rm])

    cslot = [Buf(f"cslot{i}") for i in range(4)]
    for cb in cslot:
        cb.sem = P.new_sem("cslot")
    cast_list = []
    for j in range(20):
        cast_list.append((wi_bf[j], w_in[:, j * 512:(j + 1) * 512].rearrange("(k p) n -> p k n", p=128), ("wi", j)))
    for j in range(4):
        cast_list.append((wab_bf[j], w_ab[:, j * 512:(j + 1) * 512].rearrange("(k p) n -> p k n", p=128), ("wab", j)))
    for j in range(4):
        cast_list.append((wout_bf[j], w_out[:, j * 512:(j + 1) * 512].rearrange("(k p) n -> p k n", p=128), ("wout", j)))
    for j in range(16):
        cast_list.append((wup_bf[j], w_up[:, j * 512:(j + 1) * 512].rearrange("(k p) n -> p k n", p=128), ("wup", j)))
    for j in range(16):
        cast_list.append((wdn_bf[j], w_dn[(j % 4) * 2048:(j % 4 + 1) * 2048, (j // 4) * 512:(j // 4 + 1) * 512]
                          .rearrange("(k p) n -> p k n", p=128), ("wdn", j)))
    cast_pos = [0]

    def issue_casts(n):
        for _ in range(n):
            if cast_pos[0] >= len(cast_list):
                return
            dst, src, key = cast_list[cast_pos[0]]
            sl = cslot[cast_pos[0] % 4]
            cast_pos[0] += 1
            dma("pool", dst, src, [], [sl, dbuf(key)], sl)


    def rstd_from_ss(ss, v, rstd, b_ss, b_v, b_rstd, n):
        P.op("dve", lambda e: e.tensor_scalar(out=v[:], in0=ss[:], scalar1=1.0 / n, scalar2=EPS,
                                              op0=ALU.mult, op1=ALU.add), [b_ss], [b_v])
        P.op("pool", lambda e: e.tensor_tensor(out=rstd[:], in0=v[:], in1=mhalf[:], op=ALU.pow),
             [b_v, b_mhalf], [b_rstd])

    def norm_tile(xt, b_xt, gbc, b_g, junk, b_junk, ss, b_ss, v, b_v, rstd, b_rstd, hb, b_hb):
        P.op("act", lambda e: e.activation(out=junk[:], in_=xt[:], func=AF.Square, accum_out=ss[:]),
             [b_xt], [b_junk, b_ss])
        rstd_from_ss(ss, v, rstd, b_ss, b_v, b_rstd, D)
        P.op("dve", lambda e: e.scalar_tensor_tensor(out=hb[:], in0=xt[:], scalar=rstd[:], in1=gbc[:],
                                                     op0=ALU.mult, op1=ALU.mult),
             [b_xt, b_rstd, b_g], [b_hb])

    def transpose_tile(hb, b_hb, hTb, b_hTb, tt, cp_engs=("act", "dve")):
        for half in range(2):
            bank = 6 + half

            def tr(e, half=half, bank=bank):
                i = None
                for k in range(8):
                    kc = half * 8 + k
                    i = e.transpose(out=psb[bank][:, k * 128:(k + 1) * 128],
                                    in_=hb[:, kc * 128:(kc + 1) * 128], identity=idt[:])
                return i
            P.op("pe", tr, [b_hb, b_idt], [b_ps[bank]])
            dst = hTb[:, half * 8:(half + 1) * 8, tt * 128:(tt + 1) * 128]
            src = psb[bank].rearrange("p (k n) -> p k n", k=8)
            if cp_engs[half] == "act":
                P.op("act", lambda e, dst=dst, src=src: e.copy(out=dst, in_=src), [b_ps[bank]], [b_hTb])
            else:
                P.op("dve", lambda e, dst=dst, src=src: e.tensor_copy(out=dst, in_=src), [b_ps[bank]], [b_hTb])

    if stop_after == 0:
        issue_casts(1000)
        P.op("sp", lambda e: e.nop(), list(dram.values()), [])
        P.emit()
        return nc

    phase_start()
    xin = [sb("xin", [128, D], F32) for _ in range(2)]
    b_xin = [P.buf("xin", dma=True) for _ in range(2)]
    gpre = sb("gpre", [128, D], F32); b_gpre = P.buf("gpre", dma=True)
    junk = sb("junk", [128, D], BF16); b_junk = P.buf("junk")
    hbf = [sb("hbf", [128, D], BF16) for _ in range(4)]
    b_hbf = [P.buf("hbf") for _ in range(4)]
    hT = [sb("hT", [128, 16, 512], BF16) for _ in range(2)]
    b_hT = [P.buf("hT") for _ in range(2)]
    wr = [sb("wr", [128, 16, 512], BF16) for _ in range(3)]
    b_wr = [P.buf("wr", dma=True) for _ in range(3)]
    cst = [sb("cs", [128, 2, 512], F32) for _ in range(2)]
    b_cst = [P.buf("cs", dma=True) for _ in range(2)]
    qsb = [sb("qsb", [128, 512], BF16) for _ in range(2)]
    b_qsb = [P.buf("qsb") for _ in range(2)]
    t1 = [sb("t1", [128, 512], F32) for _ in range(2)]
    b_t1 = [P.buf("t1") for _ in range(2)]
    t2 = [sb("t2", [128, 512], F32) for _ in range(2)]
    b_t2 = [P.buf("t2") for _ in range(2)]
    stage = [sb("stage", [128, 4, 512], BF16) for _ in range(3)]
    b_stage = [P.buf("stage", dma=True) for _ in range(3)]
    sm = [[sb("sm", [128, 1], F32) for _ in range(3)] for _ in range(2)]
    b_sm = [[P.buf("sm") for _ in range(3)] for _ in range(2)]

    dma("sp", gpre[:], gains[0:1, :].partition_broadcast(128), [], [b_gpre], b_gpre)

    blocks = [("own", b) for b in range(NB)] + [("oth", b) for b in range(NB)]

    def tiles_of(kind, b):
        if kind == "own":
            return list(range(20))
        tl = [2, 3, 4, 5]
        if b == 0 or b == NB - 1:
            tl += [8, 9, 10, 11]
        return tl

    items = []
    for bi, (kind, b) in enumerate(blocks):
        tl = tiles_of(kind, b)
        for k_, j in enumerate(tl):
            items.append((bi, kind, b, j, k_, len(tl)))
    p1 = {"wl": 0, "bank": 0, "rope": 0, "ntile": 0, "stores": []}

    def ensure_wloads(upto):
        while p1["wl"] < min(upto, len(items)):
            n = p1["wl"]
            j = items[n][3]
            ws = n % 3
            dma("sp", wr[ws][:], wi_bf[j], [dbuf(("wi", j))], [b_wr[ws]], b_wr[ws])
            p1["wl"] += 1
            if cast_pos[0] < 20:
                issue_casts(1)

    def prologue_nonpe(bi):
        kind, b = blocks[bi]
        xsrc = x_own if kind == "own" else x_oth
        cs_src = cs_own if kind == "own" else cs_oth
        for tt in range(4):
            i = p1["ntile"] % 2
            p1["ntile"] += 1
            tok0 = b * 512 + tt * 128
            dma("sp", xin[i][:], xsrc[tok0:tok0 + 128, :], [], [b_xin[i]], b_xin[i])
            norm_tile(xin[i], b_xin[i], gpre, b_gpre, junk, b_junk, sm[i][0], b_sm[i][0], sm[i][1], b_sm[i][1],
                      sm[i][2], b_sm[i][2], hbf[tt], b_hbf[tt])
        csl = bi % 2
        dma("sp", cst[csl][:], cs_src[:, :, b * 512:(b + 1) * 512], [], [b_cst[csl]], b_cst[csl])

    def prologue_pe(bi):
        for tt in range(4):
            transpose_tile(hbf[tt], b_hbf[tt], hT[bi % 2], b_hT[bi % 2], tt)

    def flush_stores():
        for fn in p1["stores"]:
            fn()
        p1["stores"] = []

    issue_casts(4)
    prologue_nonpe(0)
    prologue_pe(0)
    for n, (bi, kind, b, j, kidx, ntl) in enumerate(items):
        ensure_wloads(n + 3)
        hTb, b_hTb = hT[bi % 2], b_hT[bi % 2]
        koff = 0 if kind == "own" else T
        csl = bi % 2
        ws = n % 3
        st = n % 3
        stg, b_stg = stage[st], b_stage[st]
        typ = ["qa", "qa", "ka", "ka", "va", "va", "qn", "qn", "kn", "kn", "vn", "vn"][j] if j < 12 else "gate"
        new_stores = []

        def store(dst, src, key, stg=stg, b_stg=b_stg):
            new_stores.append(lambda: dma("sp", dst, src, [b_stg], [dbuf(key)], b_stg))

        if typ in ("qa", "ka", "qn", "kn", "gate"):
            for ct in range(4):
                bank = p1["bank"] % 4
                p1["bank"] += 1

                def mm(e, ws=ws, ct=ct, bank=bank, hTb=hTb):
                    i_ = None
                    for kc in range(16):
                        i_ = e.matmul(ps[bank][:], lhsT=wr[ws][:, kc, ct * 128:(ct + 1) * 128], rhs=hTb[:, kc, :],
                                      start=(kc == 0), stop=(kc == 15))
                    return i_
                P.op("pe", mm, [b_wr[ws], b_hTb], [b_ps[bank]])
                if typ in ("qa", "ka"):
                    r = p1["rope"] % 2
                    p1["rope"] += 1
                    rb = 4 + r
                    P.op("act", lambda e, r=r, bank=bank: e.copy(out=qsb[r][:], in_=ps[bank][:]),
                         [b_ps[bank]], [b_qsb[r]])
                    P.op("pe", lambda e, r=r, rb=rb: e.matmul(ps[rb][:], lhsT=Rm[:], rhs=qsb[r][:], start=True, stop=True),
                         [b_qsb[r], b_Rm], [b_ps[rb]])
                    P.op("dve", lambda e, r=r, bank=bank, csl=csl: e.tensor_tensor(
                        out=t1[r][:], in0=ps[bank][:], in1=cst[csl][:, 0, :], op=ALU.mult),
                        [b_ps[bank], b_cst[csl]], [b_t1[r]])
                    P.op("dve", lambda e, r=r, rb=rb, csl=csl: e.tensor_tensor(
                        out=t2[r][:], in0=ps[rb][:], in1=cst[csl][:, 1, :], op=ALU.mult),
                        [b_ps[rb], b_cst[csl]], [b_t2[r]])
                    P.op("pool", lambda e, r=r, stg=stg, ct=ct: e.tensor_tensor(
                        out=stg[:, ct, :], in0=t1[r][:], in1=t2[r][:], op=ALU.add),
                        [b_t1[r], b_t2[r]], [b_stg])
                elif typ == "gate":
                    P.op("act", lambda e, bank=bank, stg=stg, ct=ct: e.activation(
                        out=stg[:, ct, :], in_=ps[bank][:], func=AF.Sigmoid), [b_ps[bank]], [b_stg])
                else:
                    P.op("act", lambda e, bank=bank, stg=stg, ct=ct: e.copy(out=stg[:, ct, :], in_=ps[bank][:]),
                         [b_ps[bank]], [b_stg])
            if typ == "qa":
                c0 = (j - 0) * 4
                store(QaT[c0:c0 + 4, :, b * 512:(b + 1) * 512].rearrange("c p n -> p c n"), stg[:], ("QaT", c0, b))
            elif typ == "ka":
                c0 = (j - 2) * 4
                store(KaT[c0:c0 + 4, :, koff + b * 512:koff + (b + 1) * 512].rearrange("c p n -> p c n"), stg[:],
                      ("KaT", c0, kind, b))
            elif typ == "qn":
                c0 = (j - 6) * 4
                store(QnT[c0:c0 + 4, :, b * 512:(b + 1) * 512].rearrange("c p n -> p c n"), stg[:], ("QnT", c0, b))
            elif typ == "kn":
                c0 = (j - 8) * 4
                if kind == "own":
                    e0 = 256 + b * 512
                    store(KnT[c0:c0 + 4, :, e0:e0 + 512].rearrange("c p n -> p c n"), stg[:], ("KnT", c0, kind, b))
                else:
                    if b == NB - 1:
                        store(KnT[c0:c0 + 4, :, 0:256].rearrange("c p n -> p c n"), stg[:, :, 256:512],
                              ("KnT", c0, kind, b, 0))
                    if b == 0:
                        e0 = (NQT + 2) * 128
                        store(KnT[c0:c0 + 4, :, e0:e0 + 256].rearrange("c p n -> p c n"), stg[:, :, 0:256],
                              ("KnT", c0, kind, b, 1))
            else:
                c0 = (j - 12) * 4
                store(gT[c0:c0 + 4, :, b * 512:(b + 1) * 512].rearrange("c p n -> p c n"), stg[:], ("gT", c0, b))
        else:
            for tt in range(4):
                bank = p1["bank"] % 4
                p1["bank"] += 1

                def mmv(e, ws=ws, tt=tt, bank=bank, hTb=hTb):
                    i_ = None
                    for kc in range(16):
                        i_ = e.matmul(ps[bank][:], lhsT=hTb[:, kc, tt * 128:(tt + 1) * 128], rhs=wr[ws][:, kc, :],
                                      start=(kc == 0), stop=(kc == 15))
                    return i_
                P.op("pe", mmv, [b_wr[ws], b_hTb], [b_ps[bank]])
                if tt % 2 == 0:
                    P.op("dve", lambda e, bank=bank, stg=stg, tt=tt: e.tensor_copy(out=stg[:, tt, :], in_=ps[bank][:]),
                         [b_ps[bank]], [b_stg])
                else:
                    P.op("act", lambda e, bank=bank, stg=stg, tt=tt: e.copy(out=stg[:, tt, :], in_=ps[bank][:]),
                         [b_ps[bank]], [b_stg])
            if typ == "va":
                for hh in range(2):
                    h = (j - 4) * 2 + hh
                    store(Va[h, koff + b * 512:koff + (b + 1) * 512, :].rearrange("(t p) e -> p t e", p=128),
                          stg[:, :, hh * 256:(hh + 1) * 256], ("Va", h, kind, b))
            else:
                for hh in range(4):
                    h = (j - 10) * 4 + hh
                    if kind == "own":
                        e0 = 256 + b * 512
                        store(Vn[h, e0:e0 + 512, :].rearrange("(t p) e -> p t e", p=128),
                              stg[:, :, hh * 128:(hh + 1) * 128], ("Vn", h, kind, b))
                    else:
                        if b == NB - 1:
                            store(Vn[h, 0:256, :].rearrange("(t p) e -> p t e", p=128),
                                  stg[:, 2:4, hh * 128:(hh + 1) * 128], ("Vn", h, kind, b, 0))
                        if b == 0:
                            e0 = (NQT + 2) * 128
                            store(Vn[h, e0:e0 + 256, :].rearrange("(t p) e -> p t e", p=128),
                                  stg[:, 0:2, hh * 128:(hh + 1) * 128], ("Vn", h, kind, b, 1))
        flush_stores()
        p1["stores"] = new_stores
        if kidx == min(1, ntl - 1) and bi + 1 < len(blocks):
            prologue_nonpe(bi + 1)
        if kidx == ntl - 1 and bi + 1 < len(blocks):
            prologue_pe(bi + 1)
    flush_stores()

    issue_casts(20 - cast_pos[0])

    def dkeys(prefix):
        return [v for k, v in dram.items() if isinstance(k, tuple) and k[0] == prefix]

    def finish():
        P.op("sp", lambda e: e.nop(), list(dram.values()), [])
        P.emit()
        return nc

    if stop_after == 1:
        return finish()

    phase_start()
    KTs = [[sb("KT", [128, 2 * T], BF16) for _ in range(2)] for _ in range(2)]
    b_KTs = [[P.buf("KT", dma=True) for _ in range(2)] for _ in range(2)]
    QTs = [[sb("QT", [128, T], BF16) for _ in range(2)] for _ in range(2)]
    b_QTs = [[P.buf("QT", dma=True) for _ in range(2)] for _ in range(2)]
    Vts = [sb("Vt", [128, NKT, 258], BF16) for _ in range(2)]
    NVP = 4
    b_Vts = [[P.buf("Vt", dma=True) for _ in range(NVP)] for _ in range(2)]
    b_Vones = [P.buf("Vones") for _ in range(2)]
    raw = [sb("raw", [128, 257], F32) for _ in range(8)]
    b_raw = [P.buf("raw") for _ in range(8)]
    ET = [sb("ET", [128, 512], BF16) for _ in range(3)]
    b_ET = [P.buf("ET") for _ in range(3)]
    o0 = [sb("o0", [128, 256], F32) for _ in range(4)]
    b_o0 = [P.buf("o0") for _ in range(4)]
    osb = [sb("osb", [128, 256], F32) for _ in range(2)]
    b_osb = [P.buf("osb") for _ in range(2)]
    ojunk = sb("ojunk", [128, 256], BF16); b_ojunk = P.buf("ojunk")
    oabf = [sb("oabf", [128, 256], BF16) for _ in range(4)]
    b_oabf = [P.buf("oabf") for _ in range(4)]
    sw8 = sb("sw8", [128, 256], F32); b_sw8 = P.buf("sw8", dma=True)
    oast = [sb("oast", [128, 2, 512], BF16) for _ in range(2)]
    b_oast = [P.buf("oast", dma=True) for _ in range(2)]
    lt = sb("lt", [128, 512], F32); b_lt = P.buf("lt", dma=True)
    lprod = sb("lprod", [128, 256], F32); b_lprod = P.buf("lprod")
    lsm = [sb("lsm", [128, 1], F32) for _ in range(6)]
    b_lsm = [P.buf("lsm") for _ in range(6)]
    dsm = [[sb("dsm", [128, 1], F32) for _ in range(6)] for _ in range(2)]
    b_dsm = [[P.buf("dsm") for _ in range(6)] for _ in range(2)]
    rz0 = [sb("rz0", [128, 1], F32) for _ in range(4)]
    b_rz0 = [P.buf("rz0") for _ in range(4)]

    dma("sp", lt[:], lamv.partition_broadcast(128), [], [b_lt], b_lt)
    dma("sp", sw8[:], subw.partition_broadcast(128), [], [b_sw8], b_sw8)
    P.op("dve", lambda e: e.tensor_scalar(out=sw8[:], in0=sw8[:], scalar1=float(1.0 - LAMBDA_INIT), scalar2=None,
                                          op0=ALU.mult), [b_sw8], [b_sw8])
    P.op("dve", lambda e: e.tensor_tensor(out=lprod[:, 0:128], in0=lt[:, 0:128], in1=lt[:, 128:256], op=ALU.mult),
         [b_lt], [b_lprod])
    P.op("dve", lambda e: e.tensor_tensor(out=lprod[:, 128:256], in0=lt[:, 256:384], in1=lt[:, 384:512], op=ALU.mult),
         [b_lt], [b_lprod])
    P.op("dve", lambda e: e.reduce_sum(out=lsm[0][:], in_=lprod[:, 0:128], axis=AX.X), [b_lprod], [b_lsm[0]])
    P.op("dve", lambda e: e.reduce_sum(out=lsm[1][:], in_=lprod[:, 128:256], axis=AX.X), [b_lprod], [b_lsm[1]])
    P.op("act", lambda e: e.activation(out=lsm[2][:], in_=lsm[0][:], func=AF.Exp), [b_lsm[0]], [b_lsm[2]])
    P.op("act", lambda e: e.activation(out=lsm[3][:], in_=lsm[1][:], func=AF.Exp), [b_lsm[1]], [b_lsm[3]])
    P.op("dve", lambda e: e.tensor_tensor(out=lsm[4][:], in0=lsm[2][:], in1=lsm[3][:], op=ALU.subtract),
         [b_lsm[2], b_lsm[3]], [b_lsm[4]])
    nlam, b_nlam = lsm[5], b_lsm[5]
    P.op("dve", lambda e: e.tensor_scalar(out=nlam[:], in0=lsm[4][:], scalar1=float(LAMBDA_INIT), scalar2=-1.0,
                                          op0=ALU.add, op1=ALU.mult), [b_lsm[4]], [b_nlam])
    for hs_ in range(2):
        P.op("pool", lambda e, hs_=hs_: e.memset(Vts[hs_][:, :, 256:258], 1.0), [], [b_Vones[hs_]])

    NG = T // 512
    OB = [2, 3, 4, 5, 6]
    state = {"step": 0, "ob": 0, "pend": None, "defer": []}

    def da_evac(h, g, c, banks):
        for i in range(4):
            B = banks[i]
            rw = c * 4 + i
            P.op("dve", lambda e, rw=rw, B=B: e.tensor_copy(out=raw[rw][:], in_=ps[B][:, 0:257]),
                 [b_ps[B]], [b_raw[rw]])
        for i in range(4):
            rw = c * 4 + i
            if c == 0:
                P.op("dve", lambda e, i=i, rw=rw: e.reciprocal(out=rz0[i][:], in_=raw[rw][:, 256:257]),
                     [b_raw[rw]], [b_rz0[i]])
                P.op("dve", lambda e, i=i, rw=rw: e.tensor_scalar(out=o0[i][:], in0=raw[rw][:, 0:256], scalar1=rz0[i][:],
                                                                  scalar2=None, op0=ALU.mult),
                     [b_raw[rw], b_rz0[i]], [b_o0[i]])
            else:
                k = i % 2
                d, bd = dsm[k], b_dsm[k]
                P.op("dve", lambda e, d=d, rw=rw: e.reciprocal(out=d[0][:], in_=raw[rw][:, 256:257]), [b_raw[rw]], [bd[0]])
                P.op("dve", lambda e, d=d: e.tensor_tensor(out=d[1][:], in0=d[0][:], in1=nlam[:], op=ALU.mult),
                     [bd[0], b_nlam], [bd[1]])
                P.op("dve", lambda e, d=d, rw=rw, i=i, k=k: e.scalar_tensor_tensor(
                    out=osb[k][:], in0=raw[rw][:, 0:256], scalar=d[1][:], in1=o0[i][:], op0=ALU.mult, op1=ALU.add),
                    [b_raw[rw], bd[1], b_o0[i]], [b_osb[k]])
                P.op("dve", lambda e, d=d, k=k: e.scalar_tensor_tensor(
                    out=ojunk[:], in0=osb[k][:], scalar=1.0, in1=osb[k][:], op0=ALU.mult, op1=ALU.mult,
                    accum_out=d[2][:]), [b_osb[k]], [b_ojunk, bd[2]])
                rstd_from_ss(d[2], d[3], d[4], bd[2], bd[3], bd[4], 256)
                P.op("dve", lambda e, d=d, k=k, i=i: e.scalar_tensor_tensor(
                    out=oabf[i][:], in0=osb[k][:], scalar=d[4][:], in1=sw8[:], op0=ALU.mult, op1=ALU.mult),
                    [b_osb[k], bd[4], b_sw8], [b_oabf[i]])

                def trf(i=i, k=k):
                    def tr(e):
                        i_ = None
                        for jj in range(2):
                            i_ = e.transpose(out=psb[7][:, jj * 512 + i * 128: jj * 512 + (i + 1) * 128],
                                             in_=oabf[i][:, jj * 128:(jj + 1) * 128], identity=idt[:])
                        return i_
                    P.op("pe", tr, [b_oabf[i], b_idt], [b_ps[7]])
                state["defer"].append((state["step"] + 6 + 2 * i, trf))
        if c == 1:
            def fin(h=h, g=g):
                s = (h * NG + g) % 2
                P.op("dve", lambda e, s=s: e.tensor_copy(out=oast[s][:].rearrange("p c n -> p (c n)"), in_=psb[7]),
                     [b_ps[7]], [b_oast[s]])
                dst = oaT[2 * h:2 * h + 2, :, g * 512:(g + 1) * 512].rearrange("c p n -> p c n")
                dma("sp", dst, oast[s][:], [b_oast[s]], [dbuf(("oaT", h, g))], b_oast[s])
            state["defer"].append((state["step"] + 14, fin))

    def run_deferred(force=False):
        keep = []
        for (at, fn) in state["defer"]:
            if force or at <= state["step"]:
                fn()
            else:
                keep.append((at, fn))
        state["defer"] = keep

    def emit_pv(pend):
        h, g, c, kt, banks, es = pend
        hs = h % 2

        def pv(e):
            i_ = None
            for i in range(4):
                i_ = e.matmul(ps[banks[i]][:, 0:257], lhsT=ET[es][:, i * 128:(i + 1) * 128], rhs=Vts[hs][:, kt, 0:257],
                              start=(kt == 0), stop=(kt == NKT - 1))
            return i_
        P.op("pe", pv, [b_ET[es], b_Vts[hs][kt * NVP // NKT], b_Vones[hs]], [b_ps[bb] for bb in banks])
        if kt == NKT - 1:
            da_evac(h, g, c, banks)

    def da_loads(h):
        hs = h % 2
        for c in range(2):
            dma("sp", KTs[hs][c][:], KaT[2 * h + c], dkeys("KaT"), [b_KTs[hs][c]], b_KTs[hs][c])
            dma("sp", QTs[hs][c][:], QaT[2 * h + c], dkeys("QaT"), [b_QTs[hs][c]], b_QTs[hs][c])
        for vp in range(NVP):
            k0 = vp * NKT // NVP
            k1 = (vp + 1) * NKT // NVP
            src = Va[h, k0 * 128:k1 * 128, :].rearrange("(t p) e -> p t e", p=128)
            dma("sp", Vts[hs][:, k0:k1, 0:256], src, dkeys("Va"), [b_Vts[hs][vp]], b_Vts[hs][vp])

    da_loads(0)
    for h in range(4):
        hs = h % 2
        KT, b_KT, QT, b_QT = KTs[hs], b_KTs[hs], QTs[hs], b_QTs[hs]
        for g in range(NG):
            if g == 1 and h + 1 < 4:
                da_loads(h + 1)
            for c in range(2):
                banks = [OB[(state["ob"] + i) % 5] for i in range(4)]
                state["ob"] += 4
                for kt in range(NKT):
                    sbk = state["step"] % 2
                    es = state["step"] % 3
                    P.op("pe", lambda e, sbk=sbk, c=c, kt=kt, g=g, KT=KT, QT=QT: e.matmul(
                        ps[sbk][:], lhsT=KT[c][:, kt * 128:(kt + 1) * 128], rhs=QT[c][:, g * 512:(g + 1) * 512],
                        start=True, stop=True), [b_KT[c], b_QT[c]], [b_ps[sbk]])
                    P.op("act", lambda e, sbk=sbk, es=es: e.activation(out=ET[es][:], in_=ps[sbk][:], func=AF.Exp,
                                                                      scale=float(SCALE)),
                         [b_ps[sbk]], [b_ET[es]])
                    if state["pend"] is not None:
                        emit_pv(state["pend"])
                    state["pend"] = (h, g, c, kt, banks, es)
                    state["step"] += 1
                    run_deferred()
                    if state["step"] % 48 == 0:
                        issue_casts(1)
    emit_pv(state["pend"])
    state["pend"] = None
    run_deferred(force=True)

    issue_casts(1000)
    if stop_after == 2:
        return finish()

    phase_start()
    Qn = [sb("Qn", [128, T], BF16) for _ in range(2)]
    b_Qn = [P.buf("Qn", dma=True) for _ in range(2)]
    Kn = [sb("Kn", [128, NEXT * 128], BF16) for _ in range(2)]
    b_Kn = [P.buf("Kn", dma=True) for _ in range(2)]
    Vnt = [sb("Vnt", [128, NEXT, 130], BF16) for _ in range(2)]
    b_Vnt = [P.buf("Vnt", dma=True) for _ in range(2)]
    b_Vn1 = [P.buf("Vn1") for _ in range(2)]
    nbt = [sb("nbt", [128, 27, 128], F32) for _ in range(2)]
    b_nbt = [P.buf("nbt", dma=True) for _ in range(2)]
    ssb = [sb("ssb", [128, 768], F32) for _ in range(2)]
    b_ssb = [P.buf("ssb") for _ in range(2)]
    ETn = [sb("ETn", [128, 768], BF16) for _ in range(2)]
    b_ETn = [P.buf("ETn") for _ in range(2)]
    rzn = [sb("rzn", [128, 1], F32) for _ in range(2)]
    b_rzn = [P.buf("rzn") for _ in range(2)]
    onbf = [sb("onbf", [128, 128], BF16) for _ in range(2)]
    b_onbf = [P.buf("onbf") for _ in range(2)]
    onst = [sb("onst", [128, T], BF16) for _ in range(2)]
    b_onst = [P.buf("onst", dma=True) for _ in range(2)]
    for s in range(2):
        P.op("pool", lambda e, s=s: e.memset(Vnt[s][:, :, 128:130], 1.0), [], [b_Vn1[s]])

    def na_tiles(r):
        if r == 0:
            return list(range(0, 6)), 5
        if r == 1:
            return list(range(1, 6)), 11
        if r == NQT - 2:
            return list(range(r, r + 5)), 16
        if r == NQT - 1:
            return list(range(r - 1, r + 5)), 21
        return list(range(r, r + 5)), 0

    na_state = {"pend": None, "cnt": 0}

    def na_pv(pend):
        h, s, r, tl, ws = pend
        ob = 4 + ws
        n = len(tl)

        def pv(e):
            i_ = None
            for i, et in enumerate(tl):
                i_ = e.matmul(ps[ob][:, 0:129], lhsT=ETn[ws][:, i * 128:(i + 1) * 128], rhs=Vnt[s][:, et, 0:129],
                              start=(i == 0), stop=(i == n - 1))
            return i_
        P.op("pe", pv, [b_ETn[ws], b_Vnt[s], b_Vn1[s]], [b_ps[ob]])
        P.op("dve", lambda e: e.reciprocal(out=rzn[ws][:], in_=ps[ob][:, 128:129]), [b_ps[ob]], [b_rzn[ws]])
        P.op("dve", lambda e: e.tensor_scalar(out=onbf[ws][:], in0=ps[ob][:, 0:128], scalar1=rzn[ws][:], scalar2=None,
                                              op0=ALU.mult), [b_ps[ob], b_rzn[ws]], [b_onbf[ws]])
        tb = 6 + (r // 8) % 2
        P.op("pe", lambda e: e.transpose(out=psb[tb][:, (r % 8) * 128:(r % 8 + 1) * 128], in_=onbf[ws][:],
                                         identity=idt[:]), [b_onbf[ws], b_idt], [b_ps[tb]])
        if r % 8 == 7:
            r0 = r - 7
            P.op("dve", lambda e: e.tensor_copy(out=onst[s][:, r0 * 128:(r0 + 8) * 128], in_=psb[tb]),
                 [b_ps[tb]], [b_onst[s]])
        if r == NQT - 1:
            dma("sp", onT[h], onst[s][:], [b_onst[s]], [dbuf(("onT", h))], b_onst[s])

    for h in range(8):
        s = h % 2
        dma("sp", Qn[s][:], QnT[h], dkeys("QnT"), [b_Qn[s]], b_Qn[s])
        dma("sp", Kn[s][:], KnT[h], dkeys("KnT"), [b_Kn[s]], b_Kn[s])
        dma("sp", Vnt[s][:, :, 0:128], Vn[h].rearrange("(t p) e -> p t e", p=128), dkeys("Vn"), [b_Vnt[s]], b_Vnt[s])
        dma("sp", nbt[s][:], nbias[h], [], [b_nbt[s]], b_nbt[s])
        for r in range(NQT):
            tl, bi0 = na_tiles(r)
            n = len(tl)
            ws = na_state["cnt"] % 2
            na_state["cnt"] += 1
            sb0, sb1 = 2 * ws, 2 * ws + 1

            def smm(e, tl=tl, s=s, r=r, sb0=sb0, sb1=sb1):
                i_ = None
                for i, et in enumerate(tl):
                    bk = sb0 if i < 4 else sb1
                    i_ = e.matmul(ps[bk][:, (i % 4) * 128:(i % 4 + 1) * 128], lhsT=Kn[s][:, et * 128:(et + 1) * 128],
                                  rhs=Qn[s][:, r * 128:(r + 1) * 128], start=True, stop=True)
                return i_
            P.op("pe", smm, [b_Kn[s], b_Qn[s]], [b_ps[sb0], b_ps[sb1]])
            n0 = min(n, 4)
            P.op("dve", lambda e, ws=ws, sb0=sb0, n0=n0, s=s, bi0=bi0: e.scalar_tensor_tensor(
                out=ssb[ws][:, 0:n0 * 128], in0=ps[sb0][:, 0:n0 * 128], scalar=float(SCALE),
                in1=nbt[s][:, bi0:bi0 + n0, :].rearrange("p a b -> p (a b)"), op0=ALU.mult, op1=ALU.add),
                [b_ps[sb0], b_nbt[s]], [b_ssb[ws]])
            if n > 4:
                n1 = n - 4
                P.op("dve", lambda e, ws=ws, sb1=sb1, n1=n1, s=s, bi0=bi0: e.scalar_tensor_tensor(
                    out=ssb[ws][:, 512:512 + n1 * 128], in0=ps[sb1][:, 0:n1 * 128], scalar=float(SCALE),
                    in1=nbt[s][:, bi0 + 4:bi0 + 4 + n1, :].rearrange("p a b -> p (a b)"), op0=ALU.mult, op1=ALU.add),
                    [b_ps[sb1], b_nbt[s]], [b_ssb[ws]])
            P.op("act", lambda e, ws=ws, n=n: e.activation(out=ETn[ws][:, 0:n * 128], in_=ssb[ws][:, 0:n * 128],
                                                           func=AF.Exp), [b_ssb[ws]], [b_ETn[ws]])
            if na_state["pend"] is not None:
                na_pv(na_state["pend"])
            na_state["pend"] = (h, s, r, tl, ws)
    na_pv(na_state["pend"])

    if stop_after == 3:
        return finish()

    phase_start()
    wr4 = [sb("wr4", [128, 16, 512], BF16) for _ in range(3)]
    b_wr4 = [P.buf("wr4", dma=True) for _ in range(3)]
    gta = [sb("gta", [128, 4, 512], BF16) for _ in range(3)]
    b_gta = [P.buf("gta", dma=True) for _ in range(3)]
    gtb = [sb("gtb", [128, 4, 512], BF16) for _ in range(3)]
    b_gtb = [P.buf("gtb", dma=True) for _ in range(3)]
    oab = [sb("oab", [128, 8, 512], BF16) for _ in range(2)]
    b_oab = [P.buf("oab", dma=True) for _ in range(2)]
    onb = [sb("onb", [128, 8, 512], BF16) for _ in range(2)]
    b_onb = [P.buf("onb", dma=True) for _ in range(2)]
    mixT = sb("mixT", [128, 16, 512], BF16); b_mixT = P.buf("mixT")
    ta = [sb("ta", [128, 512], F32) for _ in range(2)]
    b_ta = [P.buf("ta") for _ in range(2)]
    tb_ = [sb("tb", [128, 512], F32) for _ in range(2)]
    b_tb = [P.buf("tb") for _ in range(2)]
    ysb = [sb("ysb", [128, D], F32) for _ in range(4)]
    b_ysb = [P.buf("ysb") for _ in range(4)]
    xin4 = [sb("xin4", [128, D], F32) for _ in range(2)]
    b_xin4 = [P.buf("xin4", dma=True) for _ in range(2)]
    gpost = sb("gpost", [128, D], F32); b_gpost = P.buf("gpost", dma=True)
    junk4 = sb("junk4", [128, D], BF16); b_junk4 = P.buf("junk4")
    sm4 = [[sb("sm4", [128, 1], F32) for _ in range(3)] for _ in range(2)]
    b_sm4 = [[P.buf("sm4") for _ in range(3)] for _ in range(2)]
    dma("sp", gpost[:], gains[1:2, :].partition_broadcast(128), [], [b_gpost], b_gpost)

    items4 = []
    for b in range(NB):
        for ctg in range(4):
            items4.append((b, "ab", ctg))
        for cg in range(4):
            items4.append((b, "out", cg))
    p4 = {"wl": 0, "bk": 0, "tc": 0, "xc": 0, "tail": []}

    def ensure_loads4(upto):
        while p4["wl"] < min(upto, len(items4)):
            n = p4["wl"]
            b, typ, idx = items4[n]
            ws = n % 3
            if typ == "ab":
                gs = (n // 8 * 4 + idx) % 3
                dma("sp", wr4[ws][:], wab_bf[idx], [dbuf(("wab", idx))], [b_wr4[ws]], b_wr4[ws])
                dma("sp", gta[gs][:], gT[idx * 4:idx * 4 + 4, :, b * 512:(b + 1) * 512].rearrange("c p n -> p c n"),
                    dkeys("gT"), [b_gta[gs]], b_gta[gs])
                dma("sp", gtb[gs][:],
                    gT[16 + idx * 4:16 + idx * 4 + 4, :, b * 512:(b + 1) * 512].rearrange("c p n -> p c n"),
                    dkeys("gT"), [b_gtb[gs]], b_gtb[gs])
            else:
                dma("sp", wr4[ws][:], wout_bf[idx], [dbuf(("wout", idx))], [b_wr4[ws]], b_wr4[ws])
            p4["wl"] += 1

    def load_blk4(b):
        s_ = b % 2
        dma("sp", oab[s_][:], oaT[:, :, b * 512:(b + 1) * 512].rearrange("c p n -> p c n"), dkeys("oaT"),
            [b_oab[s_]], b_oab[s_])
        dma("sp", onb[s_][:], onT[:, :, b * 512:(b + 1) * 512].rearrange("c p n -> p c n"), dkeys("onT"),
            [b_onb[s_]], b_onb[s_])

    load_blk4(0)
    for n, (b, typ, idx) in enumerate(items4):
        ensure_loads4(n + 3)
        s = b % 2
        ws = n % 3
        if typ == "ab":
            ctg = idx
            gs = (n // 8 * 4 + idx) % 3
            if idx == 0 and b + 1 < NB:
                load_blk4(b + 1)
            for ct in range(4):
                bA = p4["bk"] % 4
                bB = (p4["bk"] + 1) % 4
                p4["bk"] += 2

                def mma(e, ws=ws, ct=ct, bA=bA, s=s):
                    i_ = None
                    for kc in range(8):
                        i_ = e.matmul(ps[bA][:], lhsT=wr4[ws][:, kc, ct * 128:(ct + 1) * 128], rhs=oab[s][:, kc, :],
                                      start=(kc == 0), stop=(kc == 7))
                    return i_

                def mmb(e, ws=ws, ct=ct, bB=bB, s=s):
                    i_ = None
                    for kc in range(8):
                        i_ = e.matmul(ps[bB][:], lhsT=wr4[ws][:, 8 + kc, ct * 128:(ct + 1) * 128], rhs=onb[s][:, kc, :],
                                      start=(kc == 0), stop=(kc == 7))
                    return i_
                P.op("pe", mma, [b_wr4[ws], b_oab[s]], [b_ps[bA]])
                P.op("pe", mmb, [b_wr4[ws], b_onb[s]], [b_ps[bB]])
                k = p4["tc"] % 2
                p4["tc"] += 1
                P.op("dve", lambda e, k=k, bA=bA, gs=gs, ct=ct: e.tensor_tensor(
                    out=ta[k][:], in0=ps[bA][:], in1=gta[gs][:, ct, :], op=ALU.mult), [b_ps[bA], b_gta[gs]], [b_ta[k]])
                P.op("dve", lambda e, k=k, bB=bB, gs=gs, ct=ct: e.tensor_tensor(
                    out=tb_[k][:], in0=ps[bB][:], in1=gtb[gs][:, ct, :], op=ALU.mult), [b_ps[bB], b_gtb[gs]], [b_tb[k]])
                P.op("pool", lambda e, k=k, ctg=ctg, ct=ct: e.tensor_tensor(
                    out=mixT[:, ctg * 4 + ct, :], in0=ta[k][:], in1=tb_[k][:], op=ALU.add),
                    [b_ta[k], b_tb[k]], [b_mixT])
        else:
            cg = idx
            for tt in range(4):
                bY = p4["bk"] % 4
                p4["bk"] += 1

                def mmy(e, ws=ws, tt=tt, bY=bY):
                    i_ = None
                    for kc in range(16):
                        i_ = e.matmul(ps[bY][:], lhsT=mixT[:, kc, tt * 128:(tt + 1) * 128], rhs=wr4[ws][:, kc, :],
                                      start=(kc == 0), stop=(kc == 15))
                    return i_
                P.op("pe", mmy, [b_wr4[ws], b_mixT], [b_ps[bY]])
                P.op("act", lambda e, tt=tt, cg=cg, bY=bY: e.copy(out=ysb[tt][:, cg * 512:(cg + 1) * 512], in_=ps[bY][:]),
                     [b_ps[bY]], [b_ysb[tt]])
            if cg == 3:
                for tt in range(4):
                    def tail(b=b, tt=tt):
                        i = p4["xc"] % 2
                        p4["xc"] += 1
                        tok0 = b * 512 + tt * 128
                        dma("sp", xin4[i][:], x_own[tok0:tok0 + 128, :], [], [b_xin4[i]], b_xin4[i])
                        P.op("act", lambda e, tt=tt, i=i: e.activation(out=junk4[:], in_=ysb[tt][:], func=AF.Square,
                                                                      accum_out=sm4[i][0][:]),
                             [b_ysb[tt]], [b_junk4, b_sm4[i][0]])
                        rstd_from_ss(sm4[i][0], sm4[i][1], sm4[i][2], b_sm4[i][0], b_sm4[i][1], b_sm4[i][2], D)
                        P.op("dve", lambda e, tt=tt, i=i: e.scalar_tensor_tensor(
                            out=ysb[tt][:], in0=ysb[tt][:], scalar=sm4[i][2][:], in1=gpost[:], op0=ALU.mult,
                            op1=ALU.mult), [b_sm4[i][2], b_gpost], [b_ysb[tt]])
                        P.op("dve", lambda e, tt=tt, i=i: e.tensor_tensor(out=xin4[i][:], in0=xin4[i][:], in1=ysb[tt][:],
                                                                          op=ALU.add), [b_ysb[tt]], [b_xin4[i]])
                        dma("sp", x1[tok0:tok0 + 128, :], xin4[i][:], [b_xin4[i]], [dbuf(("x1", b, tt))], b_xin4[i])
                    p4["tail"].append(tail)
        if typ == "ab" and p4["tail"] and (idx < 3 or True):
            p4["tail"].pop(0)()
    while p4["tail"]:
        p4["tail"].pop(0)()

    if stop_after == 4:
        return finish()

    phase_start()
    wr5 = [sb("wr5", [128, 16, 512], BF16) for _ in range(2)]
    b_wr5 = [P.buf("wr5", dma=True) for _ in range(2)]
    uT = sb("uT", [128, 64, 512], BF16); b_uT = P.buf("uT")
    h2T = sb("h2T", [128, 16, 512], BF16); b_h2T = P.buf("h2T")
    xin5 = [sb("xin5", [128, D], F32) for _ in range(2)]
    b_xin5 = [P.buf("xin5", dma=True) for _ in range(2)]
    zsb = [sb("zsb", [128, D], F32) for _ in range(4)]
    b_zsb = [P.buf("zsb") for _ in range(4)]
    gm1 = sb("gm1", [128, D], F32); b_gm1 = P.buf("gm1", dma=True)
    gm2 = sb("gm2", [128, D], F32); b_gm2 = P.buf("gm2", dma=True)
    hb5 = [sb("hb5", [128, D], BF16) for _ in range(4)]
    b_hb5 = [P.buf("hb5") for _ in range(4)]
    junk5 = sb("junk5", [128, D], BF16); b_junk5 = P.buf("junk5")
    rr = [sb("rr", [128, 512], F32) for _ in range(2)]
    b_rr = [P.buf("rr") for _ in range(2)]
    sm5 = [[sb("sm5", [128, 1], F32) for _ in range(3)] for _ in range(2)]
    b_sm5 = [[P.buf("sm5") for _ in range(3)] for _ in range(2)]
    dma("sp", gm1[:], gains[2:3, :].partition_broadcast(128), [], [b_gm1], b_gm1)
    dma("sp", gm2[:], gains[3:4, :].partition_broadcast(128), [], [b_gm2], b_gm2)
    out_bufs = []
    p5 = {"wc": 0, "xc": 0, "bk": 0, "rc": 0, "tail": []}

    def prologue5_nonpe(b):
        for tt in range(4):
            i = p5["xc"] % 2
            p5["xc"] += 1
            tok0 = b * 512 + tt * 128
            dma("sp", xin5[i][:], x1[tok0:tok0 + 128, :], [dbuf(("x1", b, tt))], [b_xin5[i]], b_xin5[i])
            norm_tile(xin5[i], b_xin5[i], gm1, b_gm1, junk5, b_junk5, sm5[i][0], b_sm5[i][0], sm5[i][1], b_sm5[i][1],
                      sm5[i][2], b_sm5[i][2], hb5[tt], b_hb5[tt])

    def prologue5_pe():
        for tt in range(4):
            transpose_tile(hb5[tt], b_hb5[tt], h2T, b_h2T, tt)

    def make_tail5(b, tt):
        def tail():
            i = p5["xc"] % 2
            p5["xc"] += 1
            tok0 = b * 512 + tt * 128
            dma("sp", xin5[i][:], x1[tok0:tok0 + 128, :], [dbuf(("x1", b, tt))], [b_xin5[i]], b_xin5[i])
            P.op("act", lambda e: e.activation(out=junk5[:], in_=zsb[tt][:], func=AF.Square,
                                               accum_out=sm5[i][0][:]), [b_zsb[tt]], [b_junk5, b_sm5[i][0]])
            rstd_from_ss(sm5[i][0], sm5[i][1], sm5[i][2], b_sm5[i][0], b_sm5[i][1], b_sm5[i][2], D)
            P.op("dve", lambda e: e.scalar_tensor_tensor(
                out=zsb[tt][:], in0=zsb[tt][:], scalar=sm5[i][2][:], in1=gm2[:], op0=ALU.mult, op1=ALU.mult),
                [b_sm5[i][2], b_gm2], [b_zsb[tt]])
            P.op("dve", lambda e: e.tensor_tensor(out=xin5[i][:], in0=xin5[i][:], in1=zsb[tt][:], op=ALU.add),
                 [b_zsb[tt]], [b_xin5[i]])
            ob_ = dbuf(("out", b, tt))
            dma("sp", out[tok0:tok0 + 128, :], xin5[i][:], [b_xin5[i]], [ob_], b_xin5[i])
            out_bufs.append(ob_)
        return tail

    prologue5_nonpe(0)
    prologue5_pe()
    for b in range(NB):
        for fg in range(16):
            ws = p5["wc"] % 2
            p5["wc"] += 1
            dma("sp", wr5[ws][:], wup_bf[fg], [dbuf(("wup", fg))], [b_wr5[ws]], b_wr5[ws])
            for fc in range(4):
                bU = p5["bk"] % 2
                p5["bk"] += 1

                def mmu(e, ws=ws, fc=fc, bU=bU):
                    i_ = None
                    for kc in range(16):
                        i_ = e.matmul(ps[bU][:], lhsT=wr5[ws][:, kc, fc * 128:(fc + 1) * 128], rhs=h2T[:, kc, :],
                                      start=(kc == 0), stop=(kc == 15))
                    return i_
                P.op("pe", mmu, [b_wr5[ws], b_h2T], [b_ps[bU]])
                k = p5["rc"] % 2
                p5["rc"] += 1
                P.op("act", lambda e, k=k, bU=bU: e.activation(out=rr[k][:], in_=ps[bU][:], func=AF.Relu),
                     [b_ps[bU]], [b_rr[k]])
                P.op("pool", lambda e, k=k, fg=fg, fc=fc: e.tensor_tensor(
                    out=uT[:, fg * 4 + fc, :], in0=rr[k][:], in1=rr[k][:], op=ALU.mult), [b_rr[k]], [b_uT])
            if p5["tail"] and fg % 2 == 1:
                p5["tail"].pop(0)()
            if fg == 10 and b + 1 < NB:
                prologue5_nonpe(b + 1)
        if b + 1 < NB:
            prologue5_pe()
        for cg in range(4):
            for fs in range(4):
                ws = p5["wc"] % 2
                p5["wc"] += 1
                dma("sp", wr5[ws][:], wdn_bf[cg * 4 + fs], [dbuf(("wdn", cg * 4 + fs))], [b_wr5[ws]], b_wr5[ws])
                for tt in range(4):
                    bZ = 2 + tt

                    def mmz(e, ws=ws, tt=tt, bZ=bZ, fs=fs):
                        i_ = None
                        for fc in range(16):
                            i_ = e.matmul(ps[bZ][:], lhsT=uT[:, fs * 16 + fc, tt * 128:(tt + 1) * 128],
                                          rhs=wr5[ws][:, fc, :], start=(fs == 0 and fc == 0),
                                          stop=(fs == 3 and fc == 15))
                        return i_
                    P.op("pe", mmz, [b_wr5[ws], b_uT], [b_ps[bZ]])
            for tt in range(4):
                bZ = 2 + tt
                P.op("act", lambda e, tt=tt, cg=cg, bZ=bZ: e.copy(out=zsb[tt][:, cg * 512:(cg + 1) * 512], in_=ps[bZ][:]),
                     [b_ps[bZ]], [b_zsb[tt]])
        for tt in range(4):
            p5["tail"].append(make_tail5(b, tt))
    while p5["tail"]:
        p5["tail"].pop(0)()

    P.op("sp", lambda e: e.nop(), out_bufs, [])
    P.emit()
    return nc


def _rope_tables(pos):
    inv = (1.0 / (np.float32(10000.0) ** (np.arange(0, 128, 2, dtype=np.float32) / np.float32(128)))).astype(np.float32)
    ang = (pos.astype(np.float32)[:, None] * inv[None, :]).astype(np.float32)
    c = np.cos(ang).astype(np.float32).T
    s = np.sin(ang).astype(np.float32).T
    out = np.empty((128, 2, len(pos)), np.float32)
    out[0:64, 0] = c
    out[64:128, 0] = c
    out[0:64, 1] = -s
    out[64:128, 1] = s
    return out


def _na_bias_layout(rpb, S, half):
    T = S // 2
    NQT = T // 128
    rows = S // GRID_W
    kh = min(8, rows)
    H = rpb.shape[0]
    out = np.full((H, 128, 27, 128), NEG, np.float32)

    def ext_to_global(e):
        if e < 2:
            return (1 - half) * NQT + (NQT - 2 + e)
        if e < NQT + 2:
            return half * NQT + (e - 2)
        return (1 - half) * NQT + (e - NQT - 2)

    pats = [(2, list(range(2, 7)), 0), (0, list(range(0, 6)), 5), (1, list(range(1, 6)), 11),
            (NQT - 2, list(range(NQT - 2, NQT + 3)), 16), (NQT - 1, list(range(NQT - 2, NQT + 4)), 21)]
    kk = np.arange(128)
    ka, kc = kk // 64, kk % 64
    for (r, tl, base) in pats:
        Rg = half * NQT + r
        qrow = 2 * Rg + ka
        qcol = kc
        rs = np.clip(qrow - kh // 2, 0, rows - kh)
        cstart = np.clip(qcol - 8, 0, GRID_W - 16)
        for i, e in enumerate(tl):
            Kg = ext_to_global(e)
            krow = 2 * Kg + ka
            kcol = kc
            valid = ((krow[:, None] >= rs[None, :]) & (krow[:, None] < rs[None, :] + kh) &
                     (kcol[:, None] >= cstart[None, :]) & (kcol[:, None] < cstart[None, :] + 16))
            dr = np.clip(krow[:, None] - qrow[None, :] + 7, 0, 14)
            dc = np.clip(kcol[:, None] - qcol[None, :] + 15, 0, 30)
            g = rpb[:, dr, dc]
            out[:, :, base + i, :] = np.where(valid[None], g, np.float32(NEG))
    return out


_PROG_CACHE = {}


def run_layer(inputs, debug=False):
    x = np.asarray(inputs["x"], np.float32)
    B, S, _ = x.shape
    T = S // 2
    ncores = 2 * B
    key = (T, debug)
    if key not in _PROG_CACHE:
        _PROG_CACHE[key] = build_program(T, debug)
    nc = _PROG_CACHE[key]
    f = lambda k: np.ascontiguousarray(np.asarray(inputs[k], np.float32)[0])
    w_in = f("w_in")
    w_ab = np.ascontiguousarray(np.concatenate([f("w_branch_a"), f("w_branch_b")], axis=0))
    w_out = f("w_out")
    w_up = f("w_up")
    w_dn = f("w_down")
    gains = np.ascontiguousarray(np.stack([f("norm_mix_pre"), f("norm_mix_post"), f("norm_mlp_pre"), f("norm_mlp_post")]))
    lamv = np.ascontiguousarray(np.concatenate([f("lam_q1"), f("lam_k1"), f("lam_q2"), f("lam_k2")])[None, :])
    subw = f("subln_w")[None, :]
    rpb = f("na_rpb")
    ident = np.eye(128, dtype=np.float32).astype(ml_dtypes.bfloat16)
    nb_half = [_na_bias_layout(rpb, S, hf) for hf in range(2)]
    cs_half = [_rope_tables(np.arange(hf * T, (hf + 1) * T)) for hf in range(2)]
    in_maps = []
    for c in range(ncores):
        b, hf = c // 2, c % 2
        in_maps.append({
            "x_own": np.ascontiguousarray(x[b, hf * T:(hf + 1) * T]),
            "x_oth": np.ascontiguousarray(x[b, (1 - hf) * T:(2 - hf) * T]),
            "w_in": w_in, "w_ab": w_ab, "w_out": w_out, "w_up": w_up, "w_dn": w_dn,
            "gains": gains, "lamv": lamv, "subw": subw, "nbias": nb_half[hf],
            "cs_own": cs_half[hf], "cs_oth": cs_half[1 - hf], "ident": ident,
        })
    res = run_bass_kernel_spmd(nc, in_maps, core_ids=list(range(ncores)))
    outp = np.empty((B, S, D), np.float32)
    for c in range(ncores):
        b, hf = c // 2, c % 2
        outp[b, hf * T:(hf + 1) * T] = res.results[c]["out"]
    if debug:
        return outp, res.results
    return outp


def kernel(**inputs):
    return run_layer(inputs)
```

```python
import numpy as np
import ml_dtypes
import concourse.bass as bass
import concourse.mybir as mybir
from concourse.bass_utils import run_bass_kernel_spmd

F32 = mybir.dt.float32
BF16 = mybir.dt.bfloat16
AF = mybir.ActivationFunctionType
ALU = mybir.AluOpType
AX = mybir.AxisListType

D = 2048
DFF = 8192
INC = 10240
EPS = 1e-6
GRID_W = 64
NEG = -30000.0
LAMBDA_INIT = 0.8 - 0.6 * 1.0
SCALE = 128 ** -0.5


class Buf:
    __slots__ = ("name", "last_w", "readers", "sem", "cnt", "base", "excl")

    def __init__(self, name):
        self.name = name
        self.excl = False
        self.last_w = None
        self.readers = []
        self.sem = None
        self.cnt = 0
        self.base = ()


class Op:
    __slots__ = ("eng", "fn", "deps", "is_dma", "sem", "val", "waited")

    def __init__(self, eng, fn):
        self.eng = eng
        self.fn = fn
        self.deps = []
        self.is_dma = False
        self.sem = None
        self.val = 0
        self.waited = False


class Prog:
    ENGS = ("pe", "act", "dve", "pool", "sp")
    SAME_ENG_SYNC = ("act", "dve", "pool")

    def __init__(self, nc):
        self.nc = nc
        self.streams = {e: [] for e in self.ENGS}
        self.eng_sem = {}
        self.free_sems = []
        self.phase_bufs = []
        self.barrier = []
        self.nsem = 0

    def new_sem(self, name):
        self.nsem += 1
        return self.nc.alloc_semaphore(f"{name}_{self.nsem}")

    def new_phase(self):
        bar = []
        seen = set()
        for b in self.phase_bufs:
            for o in ([b.last_w] if b.last_w is not None else []) + b.readers:
                if id(o) not in seen:
                    seen.add(id(o))
                    bar.append(o)
            for o in b.base:
                if id(o) not in seen:
                    seen.add(id(o))
                    bar.append(o)
            if b.sem is not None:
                self.free_sems.append((b.sem, b.cnt))
        self.barrier = bar
        self.phase_bufs = []

    def buf(self, name, dma=False):
        b = Buf(name)
        b.readers = list(self.barrier)
        b.base = tuple(self.barrier)
        if dma:
            if self.free_sems:
                b.sem, b.cnt = self.free_sems.pop()
            else:
                b.sem = self.new_sem("d_" + name)
        self.phase_bufs.append(b)
        return b

    def op(self, eng, fn, reads=(), writes=(), dma_buf=None):
        o = Op(eng, fn)
        deps = []
        seen = set()

        def add(d):
            if d is None or id(d) in seen:
                return
            if (not d.is_dma) and d.eng == eng and eng not in self.SAME_ENG_SYNC:
                return
            seen.add(id(d))
            deps.append(d)

        for b in reads:
            add(b.last_w)
            if b.last_w is None:
                for r in b.base:
                    add(r)
            if b.excl:
                for r in b.readers:
                    if r.eng != eng:
                        add(r)
        for b in writes:
            add(b.last_w)
            for r in b.readers:
                add(r)
        o.deps = deps
        for d in deps:
            d.waited = True
        if dma_buf is not None:
            o.is_dma = True
            dma_buf.cnt += 16
            o.sem = dma_buf.sem
            o.val = dma_buf.cnt
        for b in writes:
            b.last_w = o
            b.readers = []
        for b in reads:
            if b.last_w is o:
                continue
            if not o.is_dma:
                b.readers = [r for r in b.readers if r.is_dma or r.eng != eng]
            b.readers.append(o)
        self.streams[eng].append(o)
        return o

    def emit(self):
        nc = self.nc
        for e in ("pe", "act", "dve", "pool"):
            self.eng_sem[e] = self.new_sem("e_" + e)
            r = 0
            for o in self.streams[e]:
                if o.waited and not o.is_dma:
                    r += 1
                    o.sem = self.eng_sem[e]
                    o.val = r
        streams = self.streams

        def run(e, eng):
            known = {}
            for o in streams[e]:
                need = {}
                for d in o.deps:
                    k = d.sem.num
                    if known.get(k, 0) >= d.val:
                        continue
                    if k not in need or need[k][1] < d.val:
                        need[k] = (d.sem, d.val)
                for k, (sm_, v_) in need.items():
                    eng.wait_ge(sm_, v_)
                    known[k] = v_
                ins = o.fn(eng)
                if o.is_dma:
                    ins.then_inc(o.sem, 16)
                elif o.waited:
                    ins.then_inc(o.sem, 1)

        with nc.Block() as block:
            @block.tensor
            def _(eng):
                run("pe", eng)

            @block.scalar
            def _(eng):
                run("act", eng)

            @block.vector
            def _(eng):
                run("dve", eng)

            @block.gpsimd
            def _(eng):
                run("pool", eng)

            @block.sync
            def _(eng):
                run("sp", eng)


def build_program(T, debug=False, stop_after=5):
    assert T % 512 == 0 and T >= 1024
    NQT = T // 128
    NB = T // 512
    NKT = 2 * NQT
    NEXT = NQT + 4
    nc = bass.Bass("TRN2", target_bir_lowering=False)
    P = Prog(nc)

    def din(name, shape, dt=F32):
        return nc.dram_tensor(name, shape, dt, kind="ExternalInput").ap()

    def dscr(name, shape, dt):
        kind = "ExternalOutput" if (debug and not name.startswith("w")) else "Internal"
        return nc.dram_tensor(name, shape, dt, kind=kind).ap()

    x_own = din("x_own", [T, D])
    x_oth = din("x_oth", [T, D])
    w_in = din("w_in", [D, INC])
    w_ab = din("w_ab", [2048, 2048])
    w_out = din("w_out", [2048, 2048])
    w_up = din("w_up", [D, DFF])
    w_dn = din("w_dn", [DFF, D])
    gains = din("gains", [4, D])
    lamv = din("lamv", [1, 512])
    subw = din("subw", [1, 256])
    nbias = din("nbias", [8, 128, 27, 128])
    cs_own = din("cs_own", [128, 2, T])
    cs_oth = din("cs_oth", [128, 2, T])
    ident_d = din("ident", [128, 128], BF16)
    out = nc.dram_tensor("out", [T, D], F32, kind="ExternalOutput").ap()

    wi_bf = dscr("wi_bf", [20, 128, 16, 512], BF16)
    wab_bf = dscr("wab_bf", [4, 128, 16, 512], BF16)
    wout_bf = dscr("wout_bf", [4, 128, 16, 512], BF16)
    wup_bf = dscr("wup_bf", [16, 128, 16, 512], BF16)
    wdn_bf = dscr("wdn_bf", [16, 128, 16, 512], BF16)
    QaT = dscr("QaT", [8, 128, T], BF16)
    KaT = dscr("KaT", [8, 128, 2 * T], BF16)
    Va = dscr("Va", [4, 2 * T, 256], BF16)
    QnT = dscr("QnT", [8, 128, T], BF16)
    KnT = dscr("KnT", [8, 128, NEXT * 128], BF16)
    Vn = dscr("Vn", [8, NEXT * 128, 128], BF16)
    gT = dscr("gT", [32, 128, T], BF16)
    oaT = dscr("oaT", [8, 128, T], BF16)
    onT = dscr("onT", [8, 128, T], BF16)
    x1 = dscr("x1", [T, D], F32)

    dram = {}

    def dbuf(key):
        b = dram.get(key)
        if b is None:
            b = Buf(str(key))
            dram[key] = b
        return b

    SB_BASE = 16512
    SB_TOP = 229344
    sb_off = [SB_BASE]
    sb_persist = [SB_BASE]
    uid = [0]

    def sb(name, shape, dt):
        nbytes = int(np.prod(shape[1:])) * (4 if dt == F32 else 2)
        nbytes = (nbytes + 63) // 64 * 64
        uid[0] += 1
        t = nc.alloc_sbuf_tensor_at(f"{name}_{uid[0]}", list(shape), dt, offset=sb_off[0])
        sb_off[0] += nbytes
        assert sb_off[0] <= SB_TOP, (name, sb_off[0])
        return t

    def phase_start():
        P.new_phase()
        sb_off[0] = sb_persist[0]

    ps = [nc.alloc_psum_tensor(f"ps{i}", [128, 512], F32) for i in range(8)]
    psb = [p[:].bitcast(BF16) for p in ps]
    b_ps = [Buf(f"ps{i}") for i in range(8)]
    for b_ in b_ps:
        b_.excl = True

    def dma(eng, out_ap, in_ap, reads, writes, sem_buf):
        return P.op(eng, lambda e: e.dma_start(out=out_ap, in_=in_ap), reads, writes, dma_buf=sem_buf)

    idt = sb("idt", [128, 128], BF16)
    Rm = sb("Rm", [128, 128], BF16)
    mhalf = sb("mhalf", [128, 1], F32)
    sb_persist[0] = sb_off[0]
    b_idt = Buf("idt"); b_idt.sem = P.new_sem("d_idt")
    b_Rm = Buf("Rm")
    b_mhalf = Buf("mhalf")
    dma("sp", idt[:], ident_d, [], [b_idt], b_idt)
    P.op("pool", lambda e: e.memset(mhalf[:], -0.5), [], [b_mhalf])
    P.op("pool", lambda e: e.memset(Rm[:], 0.0), [], [b_Rm])
    P.op("dve", lambda e: e.tensor_copy(out=Rm[0:64, 64:128], in_=idt[0:64, 0:64]), [b_idt], [b_Rm])
    P.op("dve", lambda e: e.tensor_copy(out=Rm[64:128, 0:64], in_=idt[64:128, 64:128]), [b_idt], [b_Rm])

    cslot = [Buf(f"cslot{i}") for i in range(4)]
    for cb in cslot:
        cb.sem = P.new_sem("cslot")
    cast_list = []
    for j in range(20):
        cast_list.append((wi_bf[j], w_in[:, j * 512:(j + 1) * 512].rearrange("(k p) n -> p k n", p=128), ("wi", j)))
    for j in range(4):
        cast_list.append((wab_bf[j], w_ab[:, j * 512:(j + 1) * 512].rearrange("(k p) n -> p k n", p=128), ("wab", j)))
    for j in range(4):
        cast_list.append((wout_bf[j], w_out[:, j * 512:(j + 1) * 512].rearrange("(k p) n -> p k n", p=128), ("wout", j)))
    for j in range(16):
        cast_list.append((wup_bf[j], w_up[:, j * 512:(j + 1) * 512].rearrange("(k p) n -> p k n", p=128), ("wup", j)))
    for j in range(16):
        cast_list.append((wdn_bf[j], w_dn[(j % 4) * 2048:(j % 4 + 1) * 2048, (j // 4) * 512:(j // 4 + 1) * 512]
                          .rearrange("(k p) n -> p k n", p=128), ("wdn", j)))
    cast_pos = [0]

    def issue_casts(n):
        for _ in range(n):
            if cast_pos[0] >= len(cast_list):
                return
            dst, src, key = cast_list[cast_pos[0]]
            sl = cslot[cast_pos[0] % 4]
            cast_pos[0] += 1
            dma("pool", dst, src, [], [sl, dbuf(key)], sl)


    def rstd_from_ss(ss, v, rstd, b_ss, b_v, b_rstd, n):
        P.op("dve", lambda e: e.tensor_scalar(out=v[:], in0=ss[:], scalar1=1.0 / n, scalar2=EPS,
                                              op0=ALU.mult, op1=ALU.add), [b_ss], [b_v])
        P.op("pool", lambda e: e.tensor_tensor(out=rstd[:], in0=v[:], in1=mhalf[:], op=ALU.pow),
             [b_v, b_mhalf], [b_rstd])

    def norm_tile(xt, b_xt, gbc, b_g, junk, b_junk, ss, b_ss, v, b_v, rstd, b_rstd, hb, b_hb):
        P.op("act", lambda e: e.activation(out=junk[:], in_=xt[:], func=AF.Square, accum_out=ss[:]),
             [b_xt], [b_junk, b_ss])
        rstd_from_ss(ss, v, rstd, b_ss, b_v, b_rstd, D)
        P.op("dve", lambda e: e.scalar_tensor_tensor(out=hb[:], in0=xt[:], scalar=rstd[:], in1=gbc[:],
                                                     op0=ALU.mult, op1=ALU.mult),
             [b_xt, b_rstd, b_g], [b_hb])

    def transpose_tile(hb, b_hb, hTb, b_hTb, tt, cp_engs=("act", "dve")):
        for half in range(2):
            bank = 6 + half

            def tr(e, half=half, bank=bank):
                i = None
                for k in range(8):
                    kc = half * 8 + k
                    i = e.transpose(out=psb[bank][:, k * 128:(k + 1) * 128],
                                    in_=hb[:, kc * 128:(kc + 1) * 128], identity=idt[:])
                return i
            P.op("pe", tr, [b_hb, b_idt], [b_ps[bank]])
            dst = hTb[:, half * 8:(half + 1) * 8, tt * 128:(tt + 1) * 128]
            src = psb[bank].rearrange("p (k n) -> p k n", k=8)
            if cp_engs[half] == "act":
                P.op("act", lambda e, dst=dst, src=src: e.copy(out=dst, in_=src), [b_ps[bank]], [b_hTb])
            else:
                P.op("dve", lambda e, dst=dst, src=src: e.tensor_copy(out=dst, in_=src), [b_ps[bank]], [b_hTb])

    if stop_after == 0:
        issue_casts(1000)
        P.op("sp", lambda e: e.nop(), list(dram.values()), [])
        P.emit()
        return nc

    phase_start()
    xin = [sb("xin", [128, D], F32) for _ in range(2)]
    b_xin = [P.buf("xin", dma=True) for _ in range(2)]
    gpre = sb("gpre", [128, D], F32); b_gpre = P.buf("gpre", dma=True)
    junk = sb("junk", [128, D], BF16); b_junk = P.buf("junk")
    hbf = [sb("hbf", [128, D], BF16) for _ in range(4)]
    b_hbf = [P.buf("hbf") for _ in range(4)]
    hT = [sb("hT", [128, 16, 512], BF16) for _ in range(2)]
    b_hT = [P.buf("hT") for _ in range(2)]
    wr = [sb("wr", [128, 16, 512], BF16) for _ in range(3)]
    b_wr = [P.buf("wr", dma=True) for _ in range(3)]
    cst = [sb("cs", [128, 2, 512], F32) for _ in range(2)]
    b_cst = [P.buf("cs", dma=True) for _ in range(2)]
    qsb = [sb("qsb", [128, 512], BF16) for _ in range(2)]
    b_qsb = [P.buf("qsb") for _ in range(2)]
    t1 = [sb("t1", [128, 512], F32) for _ in range(2)]
    b_t1 = [P.buf("t1") for _ in range(2)]
    t2 = [sb("t2", [128, 512], F32) for _ in range(2)]
    b_t2 = [P.buf("t2") for _ in range(2)]
    stage = [sb("stage", [128, 4, 512], BF16) for _ in range(3)]
    b_stage = [P.buf("stage", dma=True) for _ in range(3)]
    sm = [[sb("sm", [128, 1], F32) for _ in range(3)] for _ in range(2)]
    b_sm = [[P.buf("sm") for _ in range(3)] for _ in range(2)]

    dma("sp", gpre[:], gains[0:1, :].partition_broadcast(128), [], [b_gpre], b_gpre)

    blocks = [("own", b) for b in range(NB)] + [("oth", b) for b in range(NB)]

    def tiles_of(kind, b):
        if kind == "own":
            return list(range(20))
        tl = [2, 3, 4, 5]
        if b == 0 or b == NB - 1:
            tl += [8, 9, 10, 11]
        return tl

    items = []
    for bi, (kind, b) in enumerate(blocks):
        tl = tiles_of(kind, b)
        for k_, j in enumerate(tl):
            items.append((bi, kind, b, j, k_, len(tl)))
    p1 = {"wl": 0, "bank": 0, "rope": 0, "ntile": 0, "stores": []}

    def ensure_wloads(upto):
        while p1["wl"] < min(upto, len(items)):
            n = p1["wl"]
            j = items[n][3]
            ws = n % 3
            dma("sp", wr[ws][:], wi_bf[j], [dbuf(("wi", j))], [b_wr[ws]], b_wr[ws])
            p1["wl"] += 1
            if cast_pos[0] < 20:
                issue_casts(1)

    def prologue_nonpe(bi):
        kind, b = blocks[bi]
        xsrc = x_own if kind == "own" else x_oth
        cs_src = cs_own if kind == "own" else cs_oth
        for tt in range(4):
            i = p1["ntile"] % 2
            p1["ntile"] += 1
            tok0 = b * 512 + tt * 128
            dma("sp", xin[i][:], xsrc[tok0:tok0 + 128, :], [], [b_xin[i]], b_xin[i])
            norm_tile(xin[i], b_xin[i], gpre, b_gpre, junk, b_junk, sm[i][0], b_sm[i][0], sm[i][1], b_sm[i][1],
                      sm[i][2], b_sm[i][2], hbf[tt], b_hbf[tt])
        csl = bi % 2
        dma("sp", cst[csl][:], cs_src[:, :, b * 512:(b + 1) * 512], [], [b_cst[csl]], b_cst[csl])

    def prologue_pe(bi):
        for tt in range(4):
            transpose_tile(hbf[tt], b_hbf[tt], hT[bi % 2], b_hT[bi % 2], tt)

    def flush_stores():
        for fn in p1["stores"]:
            fn()
        p1["stores"] = []

    issue_casts(4)
    prologue_nonpe(0)
    prologue_pe(0)
    for n, (bi, kind, b, j, kidx, ntl) in enumerate(items):
        ensure_wloads(n + 3)
        hTb, b_hTb = hT[bi % 2], b_hT[bi % 2]
        koff = 0 if kind == "own" else T
        csl = bi % 2
        ws = n % 3
        st = n % 3
        stg, b_stg = stage[st], b_stage[st]
        typ = ["qa", "qa", "ka", "ka", "va", "va", "qn", "qn", "kn", "kn", "vn", "vn"][j] if j < 12 else "gate"
        new_stores = []

        def store(dst, src, key, stg=stg, b_stg=b_stg):
            new_stores.append(lambda: dma("sp", dst, src, [b_stg], [dbuf(key)], b_stg))

        if typ in ("qa", "ka", "qn", "kn", "gate"):
            for ct in range(4):
                bank = p1["bank"] % 4
                p1["bank"] += 1

                def mm(e, ws=ws, ct=ct, bank=bank, hTb=hTb):
                    i_ = None
                    for kc in range(16):
                        i_ = e.matmul(ps[bank][:], lhsT=wr[ws][:, kc, ct * 128:(ct + 1) * 128], rhs=hTb[:, kc, :],
                                      start=(kc == 0), stop=(kc == 15))
                    return i_
                P.op("pe", mm, [b_wr[ws], b_hTb], [b_ps[bank]])
                if typ in ("qa", "ka"):
                    r = p1["rope"] % 2
                    p1["rope"] += 1
                    rb = 4 + r
                    P.op("act", lambda e, r=r, bank=bank: e.copy(out=qsb[r][:], in_=ps[bank][:]),
                         [b_ps[bank]], [b_qsb[r]])
                    P.op("pe", lambda e, r=r, rb=rb: e.matmul(ps[rb][:], lhsT=Rm[:], rhs=qsb[r][:], start=True, stop=True),
                         [b_qsb[r], b_Rm], [b_ps[rb]])
                    P.op("dve", lambda e, r=r, bank=bank, csl=csl: e.tensor_tensor(
                        out=t1[r][:], in0=ps[bank][:], in1=cst[csl][:, 0, :], op=ALU.mult),
                        [b_ps[bank], b_cst[csl]], [b_t1[r]])
                    P.op("dve", lambda e, r=r, rb=rb, csl=csl: e.tensor_tensor(
                        out=t2[r][:], in0=ps[rb][:], in1=cst[csl][:, 1, :], op=ALU.mult),
                        [b_ps[rb], b_cst[csl]], [b_t2[r]])
                    P.op("pool", lambda e, r=r, stg=stg, ct=ct: e.tensor_tensor(
                        out=stg[:, ct, :], in0=t1[r][:], in1=t2[r][:], op=ALU.add),
                        [b_t1[r], b_t2[r]], [b_stg])
                elif typ == "gate":
                    P.op("act", lambda e, bank=bank, stg=stg, ct=ct: e.activation(
                        out=stg[:, ct, :], in_=ps[bank][:], func=AF.Sigmoid), [b_ps[bank]], [b_stg])
                else:
                    P.op("act", lambda e, bank=bank, stg=stg, ct=ct: e.copy(out=stg[:, ct, :], in_=ps[bank][:]),
                         [b_ps[bank]], [b_stg])
            if typ == "qa":
                c0 = (j - 0) * 4
                store(QaT[c0:c0 + 4, :, b * 512:(b + 1) * 512].rearrange("c p n -> p c n"), stg[:], ("QaT", c0, b))
            elif typ == "ka":
                c0 = (j - 2) * 4
                store(KaT[c0:c0 + 4, :, koff + b * 512:koff + (b + 1) * 512].rearrange("c p n -> p c n"), stg[:],
                      ("KaT", c0, kind, b))
            elif typ == "qn":
                c0 = (j - 6) * 4
                store(QnT[c0:c0 + 4, :, b * 512:(b + 1) * 512].rearrange("c p n -> p c n"), stg[:], ("QnT", c0, b))
            elif typ == "kn":
                c0 = (j - 8) * 4
                if kind == "own":
                    e0 = 256 + b * 512
                    store(KnT[c0:c0 + 4, :, e0:e0 + 512].rearrange("c p n -> p c n"), stg[:], ("KnT", c0, kind, b))
                else:
                    if b == NB - 1:
                        store(KnT[c0:c0 + 4, :, 0:256].rearrange("c p n -> p c n"), stg[:, :, 256:512],
                              ("KnT", c0, kind, b, 0))
                    if b == 0:
                        e0 = (NQT + 2) * 128
                        store(KnT[c0:c0 + 4, :, e0:e0 + 256].rearrange("c p n -> p c n"), stg[:, :, 0:256],
                              ("KnT", c0, kind, b, 1))
            else:
                c0 = (j - 12) * 4
                store(gT[c0:c0 + 4, :, b * 512:(b + 1) * 512].rearrange("c p n -> p c n"), stg[:], ("gT", c0, b))
        else:
            for tt in range(4):
                bank = p1["bank"] % 4
                p1["bank"] += 1

                def mmv(e, ws=ws, tt=tt, bank=bank, hTb=hTb):
                    i_ = None
                    for kc in range(16):
                        i_ = e.matmul(ps[bank][:], lhsT=hTb[:, kc, tt * 128:(tt + 1) * 128], rhs=wr[ws][:, kc, :],
                                      start=(kc == 0), stop=(kc == 15))
                    return i_
                P.op("pe", mmv, [b_wr[ws], b_hTb], [b_ps[bank]])
                if tt % 2 == 0:
                    P.op("dve", lambda e, bank=bank, stg=stg, tt=tt: e.tensor_copy(out=stg[:, tt, :], in_=ps[bank][:]),
                         [b_ps[bank]], [b_stg])
                else:
                    P.op("act", lambda e, bank=bank, stg=stg, tt=tt: e.copy(out=stg[:, tt, :], in_=ps[bank][:]),
                         [b_ps[bank]], [b_stg])
            if typ == "va":
                for hh in range(2):
                    h = (j - 4) * 2 + hh
                    store(Va[h, koff + b * 512:koff + (b + 1) * 512, :].rearrange("(t p) e -> p t e", p=128),
                          stg[:, :, hh * 256:(hh + 1) * 256], ("Va", h, kind, b))
            else:
                for hh in range(4):
                    h = (j - 10) * 4 + hh
                    if kind == "own":
                        e0 = 256 + b * 512
                        store(Vn[h, e0:e0 + 512, :].rearrange("(t p) e -> p t e", p=128),
                              stg[:, :, hh * 128:(hh + 1) * 128], ("Vn", h, kind, b))
                    else:
                        if b == NB - 1:
                            store(Vn[h, 0:256, :].rearrange("(t p) e -> p t e", p=128),
                                  stg[:, 2:4, hh * 128:(hh + 1) * 128], ("Vn", h, kind, b, 0))
                        if b == 0:
                            e0 = (NQT + 2) * 128
                            store(Vn[h, e0:e0 + 256, :].rearrange("(t p) e -> p t e", p=128),
                                  stg[:, 0:2, hh * 128:(hh + 1) * 128], ("Vn", h, kind, b, 1))
        flush_stores()
        p1["stores"] = new_stores
        if kidx == min(1, ntl - 1) and bi + 1 < len(blocks):
            prologue_nonpe(bi + 1)
        if kidx == ntl - 1 and bi + 1 < len(blocks):
            prologue_pe(bi + 1)
    flush_stores()

    issue_casts(20 - cast_pos[0])

    def dkeys(prefix):
        return [v for k, v in dram.items() if isinstance(k, tuple) and k[0] == prefix]

    def finish():
        P.op("sp", lambda e: e.nop(), list(dram.values()), [])
        P.emit()
        return nc

    if stop_after == 1:
        return finish()

    phase_start()
    KTs = [[sb("KT", [128, 2 * T], BF16) for _ in range(2)] for _ in range(2)]
    b_KTs = [[P.buf("KT", dma=True) for _ in range(2)] for _ in range(2)]
    QTs = [[sb("QT", [128, T], BF16) for _ in range(2)] for _ in range(2)]
    b_QTs = [[P.buf("QT", dma=True) for _ in range(2)] for _ in range(2)]
    Vts = [sb("Vt", [128, NKT, 258], BF16) for _ in range(2)]
    NVP = 4
    b_Vts = [[P.buf("Vt", dma=True) for _ in range(NVP)] for _ in range(2)]
    b_Vones = [P.buf("Vones") for _ in range(2)]
    raw = [sb("raw", [128, 257], F32) for _ in range(8)]
    b_raw = [P.buf("raw") for _ in range(8)]
    ET = [sb("ET", [128, 512], BF16) for _ in range(3)]
    b_ET = [[P.buf("ET") for _ in range(2)] for _ in range(3)]
    o0 = [sb("o0", [128, 256], F32) for _ in range(4)]
    b_o0 = [P.buf("o0") for _ in range(4)]
    osb = [sb("osb", [128, 256], F32) for _ in range(2)]
    b_osb = [P.buf("osb") for _ in range(2)]
    ojunk = sb("ojunk", [128, 256], BF16); b_ojunk = P.buf("ojunk")
    oabf = [sb("oabf", [128, 256], BF16) for _ in range(4)]
    b_oabf = [P.buf("oabf") for _ in range(4)]
    sw8 = sb("sw8", [128, 256], F32); b_sw8 = P.buf("sw8", dma=True)
    oast = [sb("oast", [128, 2, 512], BF16) for _ in range(2)]
    b_oast = [P.buf("oast", dma=True) for _ in range(2)]
    lt = sb("lt", [128, 512], F32); b_lt = P.buf("lt", dma=True)
    lprod = sb("lprod", [128, 256], F32); b_lprod = P.buf("lprod")
    lsm = [sb("lsm", [128, 1], F32) for _ in range(6)]
    b_lsm = [P.buf("lsm") for _ in range(6)]
    dsm = [[sb("dsm", [128, 1], F32) for _ in range(6)] for _ in range(2)]
    b_dsm = [[P.buf("dsm") for _ in range(6)] for _ in range(2)]
    rz0 = [sb("rz0", [128, 1], F32) for _ in range(4)]
    b_rz0 = [P.buf("rz0") for _ in range(4)]

    dma("sp", lt[:], lamv.partition_broadcast(128), [], [b_lt], b_lt)
    dma("sp", sw8[:], subw.partition_broadcast(128), [], [b_sw8], b_sw8)
    P.op("dve", lambda e: e.tensor_scalar(out=sw8[:], in0=sw8[:], scalar1=float(1.0 - LAMBDA_INIT), scalar2=None,
                                          op0=ALU.mult), [b_sw8], [b_sw8])
    P.op("dve", lambda e: e.tensor_tensor(out=lprod[:, 0:128], in0=lt[:, 0:128], in1=lt[:, 128:256], op=ALU.mult),
         [b_lt], [b_lprod])
    P.op("dve", lambda e: e.tensor_tensor(out=lprod[:, 128:256], in0=lt[:, 256:384], in1=lt[:, 384:512], op=ALU.mult),
         [b_lt], [b_lprod])
    P.op("dve", lambda e: e.reduce_sum(out=lsm[0][:], in_=lprod[:, 0:128], axis=AX.X), [b_lprod], [b_lsm[0]])
    P.op("dve", lambda e: e.reduce_sum(out=lsm[1][:], in_=lprod[:, 128:256], axis=AX.X), [b_lprod], [b_lsm[1]])
    P.op("act", lambda e: e.activation(out=lsm[2][:], in_=lsm[0][:], func=AF.Exp), [b_lsm[0]], [b_lsm[2]])
    P.op("act", lambda e: e.activation(out=lsm[3][:], in_=lsm[1][:], func=AF.Exp), [b_lsm[1]], [b_lsm[3]])
    P.op("dve", lambda e: e.tensor_tensor(out=lsm[4][:], in0=lsm[2][:], in1=lsm[3][:], op=ALU.subtract),
         [b_lsm[2], b_lsm[3]], [b_lsm[4]])
    nlam, b_nlam = lsm[5], b_lsm[5]
    P.op("dve", lambda e: e.tensor_scalar(out=nlam[:], in0=lsm[4][:], scalar1=float(LAMBDA_INIT), scalar2=-1.0,
                                          op0=ALU.add, op1=ALU.mult), [b_lsm[4]], [b_nlam])
    for hs_ in range(2):
        P.op("pool", lambda e, hs_=hs_: e.memset(Vts[hs_][:, :, 256:258], 1.0), [], [b_Vones[hs_]])

    NG = T // 512
    OB = [2, 3, 4, 5, 6]
    state = {"step": 0, "ob": 0, "pend": None, "defer": []}

    def da_evac(h, g, c, banks):
        for i in range(4):
            B = banks[i]
            rw = c * 4 + i
            P.op("dve", lambda e, rw=rw, B=B: e.tensor_copy(out=raw[rw][:], in_=ps[B][:, 0:257]),
                 [b_ps[B]], [b_raw[rw]])
        for i in range(4):
            rw = c * 4 + i
            if c == 0:
                P.op("dve", lambda e, i=i, rw=rw: e.reciprocal(out=rz0[i][:], in_=raw[rw][:, 256:257]),
                     [b_raw[rw]], [b_rz0[i]])
                P.op("dve", lambda e, i=i, rw=rw: e.tensor_scalar(out=o0[i][:], in0=raw[rw][:, 0:256], scalar1=rz0[i][:],
                                                                  scalar2=None, op0=ALU.mult),
                     [b_raw[rw], b_rz0[i]], [b_o0[i]])
            else:
                k = i % 2
                d, bd = dsm[k], b_dsm[k]
                P.op("dve", lambda e, d=d, rw=rw: e.reciprocal(out=d[0][:], in_=raw[rw][:, 256:257]), [b_raw[rw]], [bd[0]])
                P.op("dve", lambda e, d=d: e.tensor_tensor(out=d[1][:], in0=d[0][:], in1=nlam[:], op=ALU.mult),
                     [bd[0], b_nlam], [bd[1]])
                P.op("dve", lambda e, d=d, rw=rw, i=i, k=k: e.scalar_tensor_tensor(
                    out=osb[k][:], in0=raw[rw][:, 0:256], scalar=d[1][:], in1=o0[i][:], op0=ALU.mult, op1=ALU.add),
                    [b_raw[rw], bd[1], b_o0[i]], [b_osb[k]])
                P.op("dve", lambda e, d=d, k=k: e.scalar_tensor_tensor(
                    out=ojunk[:], in0=osb[k][:], scalar=1.0, in1=osb[k][:], op0=ALU.mult, op1=ALU.mult,
                    accum_out=d[2][:]), [b_osb[k]], [b_ojunk, bd[2]])
                rstd_from_ss(d[2], d[3], d[4], bd[2], bd[3], bd[4], 256)
                P.op("dve", lambda e, d=d, k=k, i=i: e.scalar_tensor_tensor(
                    out=oabf[i][:], in0=osb[k][:], scalar=d[4][:], in1=sw8[:], op0=ALU.mult, op1=ALU.mult),
                    [b_osb[k], bd[4], b_sw8], [b_oabf[i]])

                def trf(i=i, k=k):
                    def tr(e):
                        i_ = None
                        for jj in range(2):
                            i_ = e.transpose(out=psb[7][:, jj * 512 + i * 128: jj * 512 + (i + 1) * 128],
                                             in_=oabf[i][:, jj * 128:(jj + 1) * 128], identity=idt[:])
                        return i_
                    P.op("pe", tr, [b_oabf[i], b_idt], [b_ps[7]])
                state["defer"].append((state["step"] + 6 + 2 * i, trf))
        if c == 1:
            def fin(h=h, g=g):
                s = (h * NG + g) % 2
                P.op("dve", lambda e, s=s: e.tensor_copy(out=oast[s][:].rearrange("p c n -> p (c n)"), in_=psb[7]),
                     [b_ps[7]], [b_oast[s]])
                dst = oaT[2 * h:2 * h + 2, :, g * 512:(g + 1) * 512].rearrange("c p n -> p c n")
                dma("sp", dst, oast[s][:], [b_oast[s]], [dbuf(("oaT", h, g))], b_oast[s])
            state["defer"].append((state["step"] + 14, fin))

    def run_deferred(force=False):
        keep = []
        for (at, fn) in state["defer"]:
            if force or at <= state["step"]:
                fn()
            else:
                keep.append((at, fn))
        state["defer"] = keep

    def emit_pv(pend):
        h, g, c, kt, banks, es = pend
        hs = h % 2

        for hf_ in range(2):
            def pv(e, hf_=hf_):
                i_ = None
                for i in (2 * hf_, 2 * hf_ + 1):
                    i_ = e.matmul(ps[banks[i]][:, 0:257], lhsT=ET[es][:, i * 128:(i + 1) * 128],
                                  rhs=Vts[hs][:, kt, 0:257], start=(kt == 0), stop=(kt == NKT - 1))
                return i_
            P.op("pe", pv, [b_ET[es][hf_], b_Vts[hs][kt * NVP // NKT], b_Vones[hs]],
                 [b_ps[banks[2 * hf_]], b_ps[banks[2 * hf_ + 1]]])
        if kt == NKT - 1:
            da_evac(h, g, c, banks)

    def da_loads(h):
        hs = h % 2
        for c in range(2):
            dma("sp", KTs[hs][c][:], KaT[2 * h + c], dkeys("KaT"), [b_KTs[hs][c]], b_KTs[hs][c])
            dma("sp", QTs[hs][c][:], QaT[2 * h + c], dkeys("QaT"), [b_QTs[hs][c]], b_QTs[hs][c])
        for vp in range(NVP):
            k0 = vp * NKT // NVP
            k1 = (vp + 1) * NKT // NVP
            src = Va[h, k0 * 128:k1 * 128, :].rearrange("(t p) e -> p t e", p=128)
            dma("sp", Vts[hs][:, k0:k1, 0:256], src, dkeys("Va"), [b_Vts[hs][vp]], b_Vts[hs][vp])

    da_loads(0)
    for h in range(4):
        hs = h % 2
        KT, b_KT, QT, b_QT = KTs[hs], b_KTs[hs], QTs[hs], b_QTs[hs]
        for g in range(NG):
            if g == 1 and h + 1 < 4:
                da_loads(h + 1)
            for c in range(2):
                banks = [OB[(state["ob"] + i) % 5] for i in range(4)]
                state["ob"] += 4
                for kt in range(NKT):
                    sbk = state["step"] % 2
                    es = state["step"] % 3
                    P.op("pe", lambda e, sbk=sbk, c=c, kt=kt, g=g, KT=KT, QT=QT: e.matmul(
                        ps[sbk][:], lhsT=KT[c][:, kt * 128:(kt + 1) * 128], rhs=QT[c][:, g * 512:(g + 1) * 512],
                        start=True, stop=True), [b_KT[c], b_QT[c]], [b_ps[sbk]])
                    for hf_ in range(2):
                        P.op("act", lambda e, sbk=sbk, es=es, hf_=hf_: e.activation(
                            out=ET[es][:, hf_ * 256:(hf_ + 1) * 256], in_=ps[sbk][:, hf_ * 256:(hf_ + 1) * 256],
                            func=AF.Exp, scale=float(SCALE)), [b_ps[sbk]], [b_ET[es][hf_]])
                    if state["pend"] is not None:
                        emit_pv(state["pend"])
                    state["pend"] = (h, g, c, kt, banks, es)
                    state["step"] += 1
                    run_deferred()
                    if state["step"] % 48 == 0:
                        issue_casts(1)
    emit_pv(state["pend"])
    state["pend"] = None
    run_deferred(force=True)

    issue_casts(1000)
    if stop_after == 2:
        return finish()

    phase_start()
    Qn = [sb("Qn", [128, T], BF16) for _ in range(2)]
    b_Qn = [P.buf("Qn", dma=True) for _ in range(2)]
    Kn = [sb("Kn", [128, NEXT * 128], BF16) for _ in range(2)]
    b_Kn = [P.buf("Kn", dma=True) for _ in range(2)]
    Vnt = [sb("Vnt", [128, NEXT, 130], BF16) for _ in range(2)]
    b_Vnt = [P.buf("Vnt", dma=True) for _ in range(2)]
    b_Vn1 = [P.buf("Vn1") for _ in range(2)]
    nbt = [sb("nbt", [128, 27, 128], F32) for _ in range(2)]
    b_nbt = [P.buf("nbt", dma=True) for _ in range(2)]
    ssb = [sb("ssb", [128, 768], F32) for _ in range(2)]
    b_ssb = [P.buf("ssb") for _ in range(2)]
    ETn = [sb("ETn", [128, 768], BF16) for _ in range(2)]
    b_ETn = [P.buf("ETn") for _ in range(2)]
    rzn = [sb("rzn", [128, 1], F32) for _ in range(2)]
    b_rzn = [P.buf("rzn") for _ in range(2)]
    onbf = [sb("onbf", [128, 128], BF16) for _ in range(2)]
    b_onbf = [P.buf("onbf") for _ in range(2)]
    onst = [sb("onst", [128, T], BF16) for _ in range(2)]
    b_onst = [P.buf("onst", dma=True) for _ in range(2)]
    for s in range(2):
        P.op("pool", lambda e, s=s: e.memset(Vnt[s][:, :, 128:130], 1.0), [], [b_Vn1[s]])

    def na_tiles(r):
        if r == 0:
            return list(range(0, 6)), 5
        if r == 1:
            return list(range(1, 6)), 11
        if r == NQT - 2:
            return list(range(r, r + 5)), 16
        if r == NQT - 1:
            return list(range(r - 1, r + 5)), 21
        return list(range(r, r + 5)), 0

    na_state = {"pend": None, "cnt": 0}

    def na_pv(pend):
        h, s, r, tl, ws = pend
        ob = 4 + ws
        n = len(tl)

        def pv(e):
            i_ = None
            for i, et in enumerate(tl):
                i_ = e.matmul(ps[ob][:, 0:129], lhsT=ETn[ws][:, i * 128:(i + 1) * 128], rhs=Vnt[s][:, et, 0:129],
                              start=(i == 0), stop=(i == n - 1))
            return i_
        P.op("pe", pv, [b_ETn[ws], b_Vnt[s], b_Vn1[s]], [b_ps[ob]])
        P.op("dve", lambda e: e.reciprocal(out=rzn[ws][:], in_=ps[ob][:, 128:129]), [b_ps[ob]], [b_rzn[ws]])
        P.op("dve", lambda e: e.tensor_scalar(out=onbf[ws][:], in0=ps[ob][:, 0:128], scalar1=rzn[ws][:], scalar2=None,
                                              op0=ALU.mult), [b_ps[ob], b_rzn[ws]], [b_onbf[ws]])
        tb = 6 + (r // 8) % 2
        P.op("pe", lambda e: e.transpose(out=psb[tb][:, (r % 8) * 128:(r % 8 + 1) * 128], in_=onbf[ws][:],
                                         identity=idt[:]), [b_onbf[ws], b_idt], [b_ps[tb]])
        if r % 8 == 7:
            r0 = r - 7
            P.op("dve", lambda e: e.tensor_copy(out=onst[s][:, r0 * 128:(r0 + 8) * 128], in_=psb[tb]),
                 [b_ps[tb]], [b_onst[s]])
        if r == NQT - 1:
            dma("sp", onT[h], onst[s][:], [b_onst[s]], [dbuf(("onT", h))], b_onst[s])

    for h in range(8):
        s = h % 2
        dma("sp", Qn[s][:], QnT[h], dkeys("QnT"), [b_Qn[s]], b_Qn[s])
        dma("sp", Kn[s][:], KnT[h], dkeys("KnT"), [b_Kn[s]], b_Kn[s])
        dma("sp", Vnt[s][:, :, 0:128], Vn[h].rearrange("(t p) e -> p t e", p=128), dkeys("Vn"), [b_Vnt[s]], b_Vnt[s])
        dma("sp", nbt[s][:], nbias[h], [], [b_nbt[s]], b_nbt[s])
        for r in range(NQT):
            tl, bi0 = na_tiles(r)
            n = len(tl)
            ws = na_state["cnt"] % 2
            na_state["cnt"] += 1
            sb0, sb1 = 2 * ws, 2 * ws + 1

            def smm(e, tl=tl, s=s, r=r, sb0=sb0, sb1=sb1):
                i_ = None
                for i, et in enumerate(tl):
                    bk = sb0 if i < 4 else sb1
                    i_ = e.matmul(ps[bk][:, (i % 4) * 128:(i % 4 + 1) * 128], lhsT=Kn[s][:, et * 128:(et + 1) * 128],
                                  rhs=Qn[s][:, r * 128:(r + 1) * 128], start=True, stop=True)
                return i_
            P.op("pe", smm, [b_Kn[s], b_Qn[s]], [b_ps[sb0], b_ps[sb1]])
            n0 = min(n, 4)
            P.op("dve", lambda e, ws=ws, sb0=sb0, n0=n0, s=s, bi0=bi0: e.scalar_tensor_tensor(
                out=ssb[ws][:, 0:n0 * 128], in0=ps[sb0][:, 0:n0 * 128], scalar=float(SCALE),
                in1=nbt[s][:, bi0:bi0 + n0, :].rearrange("p a b -> p (a b)"), op0=ALU.mult, op1=ALU.add),
                [b_ps[sb0], b_nbt[s]], [b_ssb[ws]])
            if n > 4:
                n1 = n - 4
                P.op("dve", lambda e, ws=ws, sb1=sb1, n1=n1, s=s, bi0=bi0: e.scalar_tensor_tensor(
                    out=ssb[ws][:, 512:512 + n1 * 128], in0=ps[sb1][:, 0:n1 * 128], scalar=float(SCALE),
                    in1=nbt[s][:, bi0 + 4:bi0 + 4 + n1, :].rearrange("p a b -> p (a b)"), op0=ALU.mult, op1=ALU.add),
                    [b_ps[sb1], b_nbt[s]], [b_ssb[ws]])
            P.op("act", lambda e, ws=ws, n=n: e.activation(out=ETn[ws][:, 0:n * 128], in_=ssb[ws][:, 0:n * 128],
                                                           func=AF.Exp), [b_ssb[ws]], [b_ETn[ws]])
            if na_state["pend"] is not None:
                na_pv(na_state["pend"])
            na_state["pend"] = (h, s, r, tl, ws)
    na_pv(na_state["pend"])

    if stop_after == 3:
        return finish()

    phase_start()
    wr4 = [sb("wr4", [128, 16, 512], BF16) for _ in range(3)]
    b_wr4 = [P.buf("wr4", dma=True) for _ in range(3)]
    gta = [sb("gta", [128, 4, 512], BF16) for _ in range(3)]
    b_gta = [P.buf("gta", dma=True) for _ in range(3)]
    gtb = [sb("gtb", [128, 4, 512], BF16) for _ in range(3)]
    b_gtb = [P.buf("gtb", dma=True) for _ in range(3)]
    oab = [sb("oab", [128, 8, 512], BF16) for _ in range(2)]
    b_oab = [P.buf("oab", dma=True) for _ in range(2)]
    onb = [sb("onb", [128, 8, 512], BF16) for _ in range(2)]
    b_onb = [P.buf("onb", dma=True) for _ in range(2)]
    mixT = sb("mixT", [128, 16, 512], BF16); b_mixT = P.buf("mixT")
    ta = [sb("ta", [128, 512], F32) for _ in range(2)]
    b_ta = [P.buf("ta") for _ in range(2)]
    tb_ = [sb("tb", [128, 512], F32) for _ in range(2)]
    b_tb = [P.buf("tb") for _ in range(2)]
    ysb = [sb("ysb", [128, D], F32) for _ in range(4)]
    b_ysb = [P.buf("ysb") for _ in range(4)]
    xin4 = [sb("xin4", [128, D], F32) for _ in range(2)]
    b_xin4 = [P.buf("xin4", dma=True) for _ in range(2)]
    gpost = sb("gpost", [128, D], F32); b_gpost = P.buf("gpost", dma=True)
    junk4 = sb("junk4", [128, D], BF16); b_junk4 = P.buf("junk4")
    sm4 = [[sb("sm4", [128, 1], F32) for _ in range(3)] for _ in range(2)]
    b_sm4 = [[P.buf("sm4") for _ in range(3)] for _ in range(2)]
    dma("sp", gpost[:], gains[1:2, :].partition_broadcast(128), [], [b_gpost], b_gpost)

    items4 = []
    for b in range(NB):
        for ctg in range(4):
            items4.append((b, "ab", ctg))
        for cg in range(4):
            items4.append((b, "out", cg))
    p4 = {"wl": 0, "bk": 0, "tc": 0, "xc": 0, "tail": [], "st": None}

    def ensure_loads4(upto):
        while p4["wl"] < min(upto, len(items4)):
            n = p4["wl"]
            b, typ, idx = items4[n]
            ws = n % 3
            if typ == "ab":
                gs = (n // 8 * 4 + idx) % 3
                dma("sp", wr4[ws][:], wab_bf[idx], [dbuf(("wab", idx))], [b_wr4[ws]], b_wr4[ws])
                dma("sp", gta[gs][:], gT[idx * 4:idx * 4 + 4, :, b * 512:(b + 1) * 512].rearrange("c p n -> p c n"),
                    dkeys("gT"), [b_gta[gs]], b_gta[gs])
                dma("sp", gtb[gs][:],
                    gT[16 + idx * 4:16 + idx * 4 + 4, :, b * 512:(b + 1) * 512].rearrange("c p n -> p c n"),
                    dkeys("gT"), [b_gtb[gs]], b_gtb[gs])
            else:
                dma("sp", wr4[ws][:], wout_bf[idx], [dbuf(("wout", idx))], [b_wr4[ws]], b_wr4[ws])
            p4["wl"] += 1

    def load_blk4(b):
        s_ = b % 2
        dma("sp", oab[s_][:], oaT[:, :, b * 512:(b + 1) * 512].rearrange("c p n -> p c n"), dkeys("oaT"),
            [b_oab[s_]], b_oab[s_])
        dma("sp", onb[s_][:], onT[:, :, b * 512:(b + 1) * 512].rearrange("c p n -> p c n"), dkeys("onT"),
            [b_onb[s_]], b_onb[s_])

    load_blk4(0)
    for n, (b, typ, idx) in enumerate(items4):
        ensure_loads4(n + 3)
        s = b % 2
        ws = n % 3
        if typ == "ab":
            ctg = idx
            gs = (n // 8 * 4 + idx) % 3
            if idx == 0 and b + 1 < NB:
                load_blk4(b + 1)
            for ct in range(4):
                bA = p4["bk"] % 4
                bB = (p4["bk"] + 1) % 4
                p4["bk"] += 2

                def mma(e, ws=ws, ct=ct, bA=bA, s=s):
                    i_ = None
                    for kc in range(8):
                        i_ = e.matmul(ps[bA][:], lhsT=wr4[ws][:, kc, ct * 128:(ct + 1) * 128], rhs=oab[s][:, kc, :],
                                      start=(kc == 0), stop=(kc == 7))
                    return i_

                def mmb(e, ws=ws, ct=ct, bB=bB, s=s):
                    i_ = None
                    for kc in range(8):
                        i_ = e.matmul(ps[bB][:], lhsT=wr4[ws][:, 8 + kc, ct * 128:(ct + 1) * 128], rhs=onb[s][:, kc, :],
                                      start=(kc == 0), stop=(kc == 7))
                    return i_
                P.op("pe", mma, [b_wr4[ws], b_oab[s]], [b_ps[bA]])
                P.op("pe", mmb, [b_wr4[ws], b_onb[s]], [b_ps[bB]])
                k = p4["tc"] % 2
                p4["tc"] += 1
                P.op("dve", lambda e, k=k, bA=bA, gs=gs, ct=ct: e.tensor_tensor(
                    out=ta[k][:], in0=ps[bA][:], in1=gta[gs][:, ct, :], op=ALU.mult), [b_ps[bA], b_gta[gs]], [b_ta[k]])
                P.op("dve", lambda e, k=k, bB=bB, gs=gs, ct=ct: e.tensor_tensor(
                    out=tb_[k][:], in0=ps[bB][:], in1=gtb[gs][:, ct, :], op=ALU.mult), [b_ps[bB], b_gtb[gs]], [b_tb[k]])
                P.op("pool", lambda e, k=k, ctg=ctg, ct=ct: e.tensor_tensor(
                    out=mixT[:, ctg * 4 + ct, :], in0=ta[k][:], in1=tb_[k][:], op=ALU.add),
                    [b_ta[k], b_tb[k]], [b_mixT])
        else:
            cg = idx
            for tt in range(4):
                bY = p4["bk"] % 4
                p4["bk"] += 1

                def mmy(e, ws=ws, tt=tt, bY=bY):
                    i_ = None
                    for kc in range(16):
                        i_ = e.matmul(ps[bY][:], lhsT=mixT[:, kc, tt * 128:(tt + 1) * 128], rhs=wr4[ws][:, kc, :],
                                      start=(kc == 0), stop=(kc == 15))
                    return i_
                P.op("pe", mmy, [b_wr4[ws], b_mixT], [b_ps[bY]])
                P.op("act", lambda e, tt=tt, cg=cg, bY=bY: e.copy(out=ysb[tt][:, cg * 512:(cg + 1) * 512], in_=ps[bY][:]),
                     [b_ps[bY]], [b_ysb[tt]])
            if cg == 3:
                for tt in range(4):
                    def tail(b=b, tt=tt):
                        if p4["st"] is not None:
                            p4["st"]()
                            p4["st"] = None
                        i = p4["xc"] % 2
                        p4["xc"] += 1
                        tok0 = b * 512 + tt * 128
                        dma("sp", xin4[i][:], x_own[tok0:tok0 + 128, :], [], [b_xin4[i]], b_xin4[i])
                        P.op("act", lambda e, tt=tt, i=i: e.activation(out=junk4[:], in_=ysb[tt][:], func=AF.Square,
                                                                      accum_out=sm4[i][0][:]),
                             [b_ysb[tt]], [b_junk4, b_sm4[i][0]])
                        rstd_from_ss(sm4[i][0], sm4[i][1], sm4[i][2], b_sm4[i][0], b_sm4[i][1], b_sm4[i][2], D)
                        P.op("dve", lambda e, tt=tt, i=i: e.scalar_tensor_tensor(
                            out=ysb[tt][:], in0=ysb[tt][:], scalar=sm4[i][2][:], in1=gpost[:], op0=ALU.mult,
                            op1=ALU.mult), [b_sm4[i][2], b_gpost], [b_ysb[tt]])
                        P.op("dve", lambda e, tt=tt, i=i: e.tensor_tensor(out=xin4[i][:], in0=xin4[i][:], in1=ysb[tt][:],
                                                                          op=ALU.add), [b_ysb[tt]], [b_xin4[i]])
                        p4["st"] = lambda: dma("sp", x1[tok0:tok0 + 128, :], xin4[i][:], [b_xin4[i]],
                                               [dbuf(("x1", b, tt))], b_xin4[i])
                    p4["tail"].append(tail)
        if typ == "ab" and p4["tail"] and (idx < 3 or True):
            p4["tail"].pop(0)()
    while p4["tail"]:
        p4["tail"].pop(0)()
    if p4["st"] is not None:
        p4["st"]()
        p4["st"] = None

    if stop_after == 4:
        return finish()

    phase_start()
    wr5 = [sb("wr5", [128, 16, 512], BF16) for _ in range(2)]
    b_wr5 = [P.buf("wr5", dma=True) for _ in range(2)]
    uT = sb("uT", [128, 64, 512], BF16); b_uT = P.buf("uT")
    h2T = sb("h2T", [128, 16, 512], BF16); b_h2T = P.buf("h2T")
    xin5 = [sb("xin5", [128, D], F32) for _ in range(2)]
    b_xin5 = [P.buf("xin5", dma=True) for _ in range(2)]
    zsb = [sb("zsb", [128, D], F32) for _ in range(4)]
    b_zsb = [P.buf("zsb") for _ in range(4)]
    gm1 = sb("gm1", [128, D], F32); b_gm1 = P.buf("gm1", dma=True)
    gm2 = sb("gm2", [128, D], F32); b_gm2 = P.buf("gm2", dma=True)
    hb5 = [sb("hb5", [128, D], BF16) for _ in range(4)]
    b_hb5 = [P.buf("hb5") for _ in range(4)]
    junk5 = sb("junk5", [128, D], BF16); b_junk5 = P.buf("junk5")
    rr = [sb("rr", [128, 512], F32) for _ in range(2)]
    b_rr = [P.buf("rr") for _ in range(2)]
    sm5 = [[sb("sm5", [128, 1], F32) for _ in range(3)] for _ in range(2)]
    b_sm5 = [[P.buf("sm5") for _ in range(3)] for _ in range(2)]
    dma("sp", gm1[:], gains[2:3, :].partition_broadcast(128), [], [b_gm1], b_gm1)
    dma("sp", gm2[:], gains[3:4, :].partition_broadcast(128), [], [b_gm2], b_gm2)
    out_bufs = []
    p5 = {"wc": 0, "xc": 0, "bk": 0, "rc": 0, "tail": [], "st": None}

    def flush_st5():
        if p5["st"] is not None:
            p5["st"]()
            p5["st"] = None

    def prologue5_nonpe(b):
        flush_st5()
        for tt in range(4):
            i = p5["xc"] % 2
            p5["xc"] += 1
            tok0 = b * 512 + tt * 128
            dma("sp", xin5[i][:], x1[tok0:tok0 + 128, :], [dbuf(("x1", b, tt))], [b_xin5[i]], b_xin5[i])
            norm_tile(xin5[i], b_xin5[i], gm1, b_gm1, junk5, b_junk5, sm5[i][0], b_sm5[i][0], sm5[i][1], b_sm5[i][1],
                      sm5[i][2], b_sm5[i][2], hb5[tt], b_hb5[tt])

    def prologue5_pe():
        for tt in range(4):
            transpose_tile(hb5[tt], b_hb5[tt], h2T, b_h2T, tt)

    def make_tail5(b, tt):
        def tail():
            flush_st5()
            i = p5["xc"] % 2
            p5["xc"] += 1
            tok0 = b * 512 + tt * 128
            dma("sp", xin5[i][:], x1[tok0:tok0 + 128, :], [dbuf(("x1", b, tt))], [b_xin5[i]], b_xin5[i])
            P.op("act", lambda e: e.activation(out=junk5[:], in_=zsb[tt][:], func=AF.Square,
                                               accum_out=sm5[i][0][:]), [b_zsb[tt]], [b_junk5, b_sm5[i][0]])
            rstd_from_ss(sm5[i][0], sm5[i][1], sm5[i][2], b_sm5[i][0], b_sm5[i][1], b_sm5[i][2], D)
            P.op("dve", lambda e: e.scalar_tensor_tensor(
                out=zsb[tt][:], in0=zsb[tt][:], scalar=sm5[i][2][:], in1=gm2[:], op0=ALU.mult, op1=ALU.mult),
                [b_sm5[i][2], b_gm2], [b_zsb[tt]])
            P.op("dve", lambda e: e.tensor_tensor(out=xin5[i][:], in0=xin5[i][:], in1=zsb[tt][:], op=ALU.add),
                 [b_zsb[tt]], [b_xin5[i]])
            ob_ = dbuf(("out", b, tt))
            out_bufs.append(ob_)
            p5["st"] = lambda: dma("sp", out[tok0:tok0 + 128, :], xin5[i][:], [b_xin5[i]], [ob_], b_xin5[i])
        return tail

    prologue5_nonpe(0)
    prologue5_pe()
    for b in range(NB):
        for fg in range(16):
            ws = p5["wc"] % 2
            p5["wc"] += 1
            dma("sp", wr5[ws][:], wup_bf[fg], [dbuf(("wup", fg))], [b_wr5[ws]], b_wr5[ws])
            for fc in range(4):
                bU = p5["bk"] % 2
                p5["bk"] += 1

                def mmu(e, ws=ws, fc=fc, bU=bU):
                    i_ = None
                    for kc in range(16):
                        i_ = e.matmul(ps[bU][:], lhsT=wr5[ws][:, kc, fc * 128:(fc + 1) * 128], rhs=h2T[:, kc, :],
                                      start=(kc == 0), stop=(kc == 15))
                    return i_
                P.op("pe", mmu, [b_wr5[ws], b_h2T], [b_ps[bU]])
                k = p5["rc"] % 2
                p5["rc"] += 1
                P.op("act", lambda e, k=k, bU=bU: e.activation(out=rr[k][:], in_=ps[bU][:], func=AF.Relu),
                     [b_ps[bU]], [b_rr[k]])
                P.op("pool", lambda e, k=k, fg=fg, fc=fc: e.tensor_tensor(
                    out=uT[:, fg * 4 + fc, :], in0=rr[k][:], in1=rr[k][:], op=ALU.mult), [b_rr[k]], [b_uT])
            if p5["tail"] and fg % 2 == 1:
                p5["tail"].pop(0)()
            if fg == 10 and b + 1 < NB:
                prologue5_nonpe(b + 1)
        if b + 1 < NB:
            prologue5_pe()
        for cg in range(4):
            for fs in range(4):
                ws = p5["wc"] % 2
                p5["wc"] += 1
                dma("sp", wr5[ws][:], wdn_bf[cg * 4 + fs], [dbuf(("wdn", cg * 4 + fs))], [b_wr5[ws]], b_wr5[ws])
                for tt in range(4):
                    bZ = 2 + tt

                    def mmz(e, ws=ws, tt=tt, bZ=bZ, fs=fs):
                        i_ = None
                        for fc in range(16):
                            i_ = e.matmul(ps[bZ][:], lhsT=uT[:, fs * 16 + fc, tt * 128:(tt + 1) * 128],
                                          rhs=wr5[ws][:, fc, :], start=(fs == 0 and fc == 0),
                                          stop=(fs == 3 and fc == 15))
                        return i_
                    P.op("pe", mmz, [b_wr5[ws], b_uT], [b_ps[bZ]])
            for tt in range(4):
                bZ = 2 + tt
                P.op("act", lambda e, tt=tt, cg=cg, bZ=bZ: e.copy(out=zsb[tt][:, cg * 512:(cg + 1) * 512], in_=ps[bZ][:]),
                     [b_ps[bZ]], [b_zsb[tt]])
        for tt in range(4):
            p5["tail"].append(make_tail5(b, tt))
    while p5["tail"]:
        p5["tail"].pop(0)()
    flush_st5()

    P.op("sp", lambda e: e.nop(), out_bufs, [])
    P.emit()
    return nc


def _rope_tables(pos):
    inv = (1.0 / (np.float32(10000.0) ** (np.arange(0, 128, 2, dtype=np.float32) / np.float32(128)))).astype(np.float32)
    ang = (pos.astype(np.float32)[:, None] * inv[None, :]).astype(np.float32)
    c = np.cos(ang).astype(np.float32).T
    s = np.sin(ang).astype(np.float32).T
    out = np.empty((128, 2, len(pos)), np.float32)
    out[0:64, 0] = c
    out[64:128, 0] = c
    out[0:64, 1] = -s
    out[64:128, 1] = s
    return out


def _na_bias_layout(rpb, S, half):
    T = S // 2
    NQT = T // 128
    rows = S // GRID_W
    kh = min(8, rows)
    H = rpb.shape[0]
    out = np.full((H, 128, 27, 128), NEG, np.float32)

    def ext_to_global(e):
        if e < 2:
            return (1 - half) * NQT + (NQT - 2 + e)
        if e < NQT + 2:
            return half * NQT + (e - 2)
        return (1 - half) * NQT + (e - NQT - 2)

    pats = [(2, list(range(2, 7)), 0), (0, list(range(0, 6)), 5), (1, list(range(1, 6)), 11),
            (NQT - 2, list(range(NQT - 2, NQT + 3)), 16), (NQT - 1, list(range(NQT - 2, NQT + 4)), 21)]
    kk = np.arange(128)
    ka, kc = kk // 64, kk % 64
    for (r, tl, base) in pats:
        Rg = half * NQT + r
        qrow = 2 * Rg + ka
        qcol = kc
        rs = np.clip(qrow - kh // 2, 0, rows - kh)
        cstart = np.clip(qcol - 8, 0, GRID_W - 16)
        for i, e in enumerate(tl):
            Kg = ext_to_global(e)
            krow = 2 * Kg + ka
            kcol = kc
            valid = ((krow[:, None] >= rs[None, :]) & (krow[:, None] < rs[None, :] + kh) &
                     (kcol[:, None] >= cstart[None, :]) & (kcol[:, None] < cstart[None, :] + 16))
            dr = np.clip(krow[:, None] - qrow[None, :] + 7, 0, 14)
            dc = np.clip(kcol[:, None] - qcol[None, :] + 15, 0, 30)
            g = rpb[:, dr, dc]
            out[:, :, base + i, :] = np.where(valid[None], g, np.float32(NEG))
    return out


_PROG_CACHE = {}


def run_layer(inputs, debug=False):
    x = np.asarray(inputs["x"], np.float32)
    B, S, _ = x.shape
    T = S // 2
    ncores = 2 * B
    key = (T, debug)
    if key not in _PROG_CACHE:
        _PROG_CACHE[key] = build_program(T, debug)
    nc = _PROG_CACHE[key]
    f = lambda k: np.ascontiguousarray(np.asarray(inputs[k], np.float32)[0])
    w_in = f("w_in")
    w_ab = np.ascontiguousarray(np.concatenate([f("w_branch_a"), f("w_branch_b")], axis=0))
    w_out = f("w_out")
    w_up = f("w_up")
    w_dn = f("w_down")
    gains = np.ascontiguousarray(np.stack([f("norm_mix_pre"), f("norm_mix_post"), f("norm_mlp_pre"), f("norm_mlp_post")]))
    lamv = np.ascontiguousarray(np.concatenate([f("lam_q1"), f("lam_k1"), f("lam_q2"), f("lam_k2")])[None, :])
    subw = f("subln_w")[None, :]
    rpb = f("na_rpb")
    ident = np.eye(128, dtype=np.float32).astype(ml_dtypes.bfloat16)
    nb_half = [_na_bias_layout(rpb, S, hf) for hf in range(2)]
    cs_half = [_rope_tables(np.arange(hf * T, (hf + 1) * T)) for hf in range(2)]
    in_maps = []
    for c in range(ncores):
        b, hf = c // 2, c % 2
        in_maps.append({
            "x_own": np.ascontiguousarray(x[b, hf * T:(hf + 1) * T]),
            "x_oth": np.ascontiguousarray(x[b, (1 - hf) * T:(2 - hf) * T]),
            "w_in": w_in, "w_ab": w_ab, "w_out": w_out, "w_up": w_up, "w_dn": w_dn,
            "gains": gains, "lamv": lamv, "subw": subw, "nbias": nb_half[hf],
            "cs_own": cs_half[hf], "cs_oth": cs_half[1 - hf], "ident": ident,
        })
    res = run_bass_kernel_spmd(nc, in_maps, core_ids=list(range(ncores)))
    outp = np.empty((B, S, D), np.float32)
    for c in range(ncores):
        b, hf = c // 2, c % 2
        outp[b, hf * T:(hf + 1) * T] = res.results[c]["out"]
    if debug:
        return outp, res.results
    return outp


def kernel(**inputs):
    return run_layer(inputs)
```

```python
import numpy as np
import ml_dtypes
import concourse.bass as bass
import concourse.mybir as mybir
from concourse.bass_utils import run_bass_kernel_spmd

F32 = mybir.dt.float32
BF16 = mybir.dt.bfloat16
AF = mybir.ActivationFunctionType
ALU = mybir.AluOpType
AX = mybir.AxisListType

D = 2048
DFF = 8192
INC = 10240
EPS = 1e-6
GRID_W = 64
NEG = -30000.0
LAMBDA_INIT = 0.8 - 0.6 * 1.0
SCALE = 128 ** -0.5


class Buf:
    __slots__ = ("name", "last_w", "readers", "sem", "cnt", "base", "excl")

    def __init__(self, name):
        self.name = name
        self.excl = False
        self.last_w = None
        self.readers = []
        self.sem = None
        self.cnt = 0
        self.base = ()


class Op:
    __slots__ = ("eng", "fn", "deps", "is_dma", "sem", "val", "waited")

    def __init__(self, eng, fn):
        self.eng = eng
        self.fn = fn
        self.deps = []
        self.is_dma = False
        self.sem = None
        self.val = 0
        self.waited = False


class Prog:
    ENGS = ("pe", "act", "dve", "pool", "sp")
    SAME_ENG_SYNC = ("act", "dve", "pool")

    def __init__(self, nc):
        self.nc = nc
        self.streams = {e: [] for e in self.ENGS}
        self.eng_sem = {}
        self.free_sems = []
        self.phase_bufs = []
        self.barrier = []
        self.nsem = 0

    def new_sem(self, name):
        self.nsem += 1
        return self.nc.alloc_semaphore(f"{name}_{self.nsem}")

    def new_phase(self):
        bar = []
        seen = set()
        for b in self.phase_bufs:
            for o in ([b.last_w] if b.last_w is not None else []) + b.readers:
                if id(o) not in seen:
                    seen.add(id(o))
                    bar.append(o)
            for o in b.base:
                if id(o) not in seen:
                    seen.add(id(o))
                    bar.append(o)
            if b.sem is not None:
                self.free_sems.append((b.sem, b.cnt))
        self.barrier = bar
        self.phase_bufs = []

    def buf(self, name, dma=False):
        b = Buf(name)
        b.readers = list(self.barrier)
        b.base = tuple(self.barrier)
        if dma:
            if self.free_sems:
                b.sem, b.cnt = self.free_sems.pop()
            else:
                b.sem = self.new_sem("d_" + name)
        self.phase_bufs.append(b)
        return b

    def op(self, eng, fn, reads=(), writes=(), dma_buf=None):
        o = Op(eng, fn)
        deps = []
        seen = set()

        def add(d):
            if d is None or id(d) in seen:
                return
            if (not d.is_dma) and d.eng == eng and eng not in self.SAME_ENG_SYNC:
                return
            seen.add(id(d))
            deps.append(d)

        for b in reads:
            add(b.last_w)
            if b.last_w is None:
                for r in b.base:
                    add(r)
            if b.excl:
                for r in b.readers:
                    if r.eng != eng:
                        add(r)
        for b in writes:
            add(b.last_w)
            for r in b.readers:
                add(r)
        o.deps = deps
        for d in deps:
            d.waited = True
        if dma_buf is not None:
            o.is_dma = True
            dma_buf.cnt += 16
            o.sem = dma_buf.sem
            o.val = dma_buf.cnt
        for b in writes:
            b.last_w = o
            b.readers = []
        for b in reads:
            if b.last_w is o:
                continue
            if not o.is_dma:
                b.readers = [r for r in b.readers if r.is_dma or r.eng != eng]
            b.readers.append(o)
        self.streams[eng].append(o)
        return o

    def emit(self):
        nc = self.nc
        for e in ("pe", "act", "dve", "pool"):
            self.eng_sem[e] = self.new_sem("e_" + e)
            r = 0
            for o in self.streams[e]:
                if o.waited and not o.is_dma:
                    r += 1
                    o.sem = self.eng_sem[e]
                    o.val = r
        streams = self.streams

        def run(e, eng):
            known = {}
            for o in streams[e]:
                need = {}
                for d in o.deps:
                    k = d.sem.num
                    if known.get(k, 0) >= d.val:
                        continue
                    if k not in need or need[k][1] < d.val:
                        need[k] = (d.sem, d.val)
                for k, (sm_, v_) in need.items():
                    eng.wait_ge(sm_, v_)
                    known[k] = v_
                ins = o.fn(eng)
                if o.is_dma:
                    ins.then_inc(o.sem, 16)
                elif o.waited:
                    ins.then_inc(o.sem, 1)

        with nc.Block() as block:
            @block.tensor
            def _(eng):
                run("pe", eng)

            @block.scalar
            def _(eng):
                run("act", eng)

            @block.vector
            def _(eng):
                run("dve", eng)

            @block.gpsimd
            def _(eng):
                run("pool", eng)

            @block.sync
            def _(eng):
                run("sp", eng)


def build_program(T, debug=False, stop_after=5):
    assert T % 512 == 0 and T >= 1024
    NQT = T // 128
    NB = T // 512
    NKT = 2 * NQT
    NEXT = NQT + 4
    nc = bass.Bass("TRN2", target_bir_lowering=False)
    P = Prog(nc)

    def din(name, shape, dt=F32):
        return nc.dram_tensor(name, shape, dt, kind="ExternalInput").ap()

    def dscr(name, shape, dt):
        kind = "ExternalOutput" if (debug and not name.startswith("w")) else "Internal"
        return nc.dram_tensor(name, shape, dt, kind=kind).ap()

    x_own = din("x_own", [T, D])
    x_oth = din("x_oth", [T, D])
    w_in = din("w_in", [D, INC])
    w_ab = din("w_ab", [2048, 2048])
    w_out = din("w_out", [2048, 2048])
    w_up = din("w_up", [D, DFF])
    w_dn = din("w_dn", [DFF, D])
    gains = din("gains", [4, D])
    lamv = din("lamv", [1, 512])
    subw = din("subw", [1, 256])
    nbias = din("nbias", [8, 128, 27, 128])
    cs_own = din("cs_own", [128, 2, T])
    cs_oth = din("cs_oth", [128, 2, T])
    ident_d = din("ident", [128, 128], BF16)
    out = nc.dram_tensor("out", [T, D], F32, kind="ExternalOutput").ap()

    wi_bf = dscr("wi_bf", [20, 128, 16, 512], BF16)
    wab_bf = dscr("wab_bf", [4, 128, 16, 512], BF16)
    wout_bf = dscr("wout_bf", [4, 128, 16, 512], BF16)
    wup_bf = dscr("wup_bf", [16, 128, 16, 512], BF16)
    wdn_bf = dscr("wdn_bf", [16, 128, 16, 512], BF16)
    QaT = dscr("QaT", [8, 128, T], BF16)
    KaT = dscr("KaT", [8, 128, 2 * T], BF16)
    Va = dscr("Va", [4, 2 * T, 256], BF16)
    QnT = dscr("QnT", [8, 128, T], BF16)
    KnT = dscr("KnT", [8, 128, NEXT * 128], BF16)
    Vn = dscr("Vn", [8, NEXT * 128, 128], BF16)
    gT = dscr("gT", [32, 128, T], BF16)
    oaT = dscr("oaT", [8, 128, T], BF16)
    onT = dscr("onT", [8, 128, T], BF16)
    x1 = dscr("x1", [T, D], F32)

    dram = {}

    def dbuf(key):
        b = dram.get(key)
        if b is None:
            b = Buf(str(key))
            dram[key] = b
        return b

    SB_BASE = 16512
    SB_TOP = 229344
    sb_off = [SB_BASE]
    sb_persist = [SB_BASE]
    uid = [0]

    def sb(name, shape, dt):
        nbytes = int(np.prod(shape[1:])) * (4 if dt == F32 else 2)
        nbytes = (nbytes + 63) // 64 * 64
        uid[0] += 1
        t = nc.alloc_sbuf_tensor_at(f"{name}_{uid[0]}", list(shape), dt, offset=sb_off[0])
        sb_off[0] += nbytes
        assert sb_off[0] <= SB_TOP, (name, sb_off[0])
        return t

    def phase_start():
        P.new_phase()
        sb_off[0] = sb_persist[0]

    ps = [nc.alloc_psum_tensor(f"ps{i}", [128, 512], F32) for i in range(8)]
    psb = [p[:].bitcast(BF16) for p in ps]
    b_ps = [Buf(f"ps{i}") for i in range(8)]
    for b_ in b_ps:
        b_.excl = True

    def dma(eng, out_ap, in_ap, reads, writes, sem_buf):
        return P.op(eng, lambda e: e.dma_start(out=out_ap, in_=in_ap), reads, writes, dma_buf=sem_buf)

    idt = sb("idt", [128, 128], BF16)
    Rm = sb("Rm", [128, 128], BF16)
    mhalf = sb("mhalf", [128, 1], F32)
    sb_persist[0] = sb_off[0]
    b_idt = Buf("idt"); b_idt.sem = P.new_sem("d_idt")
    b_Rm = Buf("Rm")
    b_mhalf = Buf("mhalf")
    dma("sp", idt[:], ident_d, [], [b_idt], b_idt)
    P.op("pool", lambda e: e.memset(mhalf[:], -0.5), [], [b_mhalf])
    P.op("pool", lambda e: e.memset(Rm[:], 0.0), [], [b_Rm])
    P.op("dve", lambda e: e.tensor_copy(out=Rm[0:64, 64:128], in_=idt[0:64, 0:64]), [b_idt], [b_Rm])
    P.op("dve", lambda e: e.tensor_copy(out=Rm[64:128, 0:64], in_=idt[64:128, 64:128]), [b_idt], [b_Rm])

    cslot = [Buf(f"cslot{i}") for i in range(4)]
    for cb in cslot:
        cb.sem = P.new_sem("cslot")
    cast_list = []
    for j in range(20):
        cast_list.append((wi_bf[j], w_in[:, j * 512:(j + 1) * 512].rearrange("(k p) n -> p k n", p=128), ("wi", j)))
    for j in range(4):
        cast_list.append((wab_bf[j], w_ab[:, j * 512:(j + 1) * 512].rearrange("(k p) n -> p k n", p=128), ("wab", j)))
    for j in range(4):
        cast_list.append((wout_bf[j], w_out[:, j * 512:(j + 1) * 512].rearrange("(k p) n -> p k n", p=128), ("wout", j)))
    for j in range(16):
        cast_list.append((wup_bf[j], w_up[:, j * 512:(j + 1) * 512].rearrange("(k p) n -> p k n", p=128), ("wup", j)))
    for j in range(16):
        cast_list.append((wdn_bf[j], w_dn[(j % 4) * 2048:(j % 4 + 1) * 2048, (j // 4) * 512:(j // 4 + 1) * 512]
                          .rearrange("(k p) n -> p k n", p=128), ("wdn", j)))
    cast_pos = [0]

    def issue_casts(n):
        for _ in range(n):
            if cast_pos[0] >= len(cast_list):
                return
            dst, src, key = cast_list[cast_pos[0]]
            sl = cslot[cast_pos[0] % 4]
            cast_pos[0] += 1
            dma("pool", dst, src, [], [sl, dbuf(key)], sl)


    def rstd_from_ss(ss, v, rstd, b_ss, b_v, b_rstd, n):
        P.op("dve", lambda e: e.tensor_scalar(out=v[:], in0=ss[:], scalar1=1.0 / n, scalar2=EPS,
                                              op0=ALU.mult, op1=ALU.add), [b_ss], [b_v])
        P.op("pool", lambda e: e.tensor_tensor(out=rstd[:], in0=v[:], in1=mhalf[:], op=ALU.pow),
             [b_v, b_mhalf], [b_rstd])

    def norm_tile(xt, b_xt, gbc, b_g, junk, b_junk, ss, b_ss, v, b_v, rstd, b_rstd, hb, b_hb):
        P.op("act", lambda e: e.activation(out=junk[:], in_=xt[:], func=AF.Square, accum_out=ss[:]),
             [b_xt], [b_junk, b_ss])
        rstd_from_ss(ss, v, rstd, b_ss, b_v, b_rstd, D)
        P.op("dve", lambda e: e.scalar_tensor_tensor(out=hb[:], in0=xt[:], scalar=rstd[:], in1=gbc[:],
                                                     op0=ALU.mult, op1=ALU.mult),
             [b_xt, b_rstd, b_g], [b_hb])

    def transpose_tile(hb, b_hb, hTb, b_hTb, tt, cp_engs=("act", "dve")):
        for half in range(2):
            bank = 6 + half

            def tr(e, half=half, bank=bank):
                i = None
                for k in range(8):
                    kc = half * 8 + k
                    i = e.transpose(out=psb[bank][:, k * 128:(k + 1) * 128],
                                    in_=hb[:, kc * 128:(kc + 1) * 128], identity=idt[:])
                return i
            P.op("pe", tr, [b_hb, b_idt], [b_ps[bank]])
            dst = hTb[:, half * 8:(half + 1) * 8, tt * 128:(tt + 1) * 128]
            src = psb[bank].rearrange("p (k n) -> p k n", k=8)
            if cp_engs[half] == "act":
                P.op("act", lambda e, dst=dst, src=src: e.copy(out=dst, in_=src), [b_ps[bank]], [b_hTb])
            else:
                P.op("dve", lambda e, dst=dst, src=src: e.tensor_copy(out=dst, in_=src), [b_ps[bank]], [b_hTb])

    if stop_after == 0:
        issue_casts(1000)
        P.op("sp", lambda e: e.nop(), list(dram.values()), [])
        P.emit()
        return nc

    phase_start()
    xin = [sb("xin", [128, D], F32) for _ in range(2)]
    b_xin = [P.buf("xin", dma=True) for _ in range(2)]
    gpre = sb("gpre", [128, D], F32); b_gpre = P.buf("gpre", dma=True)
    junk = sb("junk", [128, D], BF16); b_junk = P.buf("junk")
    hbf = [sb("hbf", [128, D], BF16) for _ in range(4)]
    b_hbf = [P.buf("hbf") for _ in range(4)]
    hT = [sb("hT", [128, 16, 512], BF16) for _ in range(2)]
    b_hT = [P.buf("hT") for _ in range(2)]
    wr = [sb("wr", [128, 16, 512], BF16) for _ in range(3)]
    b_wr = [P.buf("wr", dma=True) for _ in range(3)]
    cst = [sb("cs", [128, 2, 512], F32) for _ in range(2)]
    b_cst = [P.buf("cs", dma=True) for _ in range(2)]
    qsb = [sb("qsb", [128, 512], BF16) for _ in range(2)]
    b_qsb = [P.buf("qsb") for _ in range(2)]
    t1 = [sb("t1", [128, 512], F32) for _ in range(2)]
    b_t1 = [P.buf("t1") for _ in range(2)]
    t2 = [sb("t2", [128, 512], F32) for _ in range(2)]
    b_t2 = [P.buf("t2") for _ in range(2)]
    stage = [sb("stage", [128, 4, 512], BF16) for _ in range(3)]
    b_stage = [P.buf("stage", dma=True) for _ in range(3)]
    sm = [[sb("sm", [128, 1], F32) for _ in range(3)] for _ in range(2)]
    b_sm = [[P.buf("sm") for _ in range(3)] for _ in range(2)]

    dma("sp", gpre[:], gains[0:1, :].partition_broadcast(128), [], [b_gpre], b_gpre)

    blocks = [("own", b) for b in range(NB)] + [("oth", b) for b in range(NB)]

    def tiles_of(kind, b):
        if kind == "own":
            return list(range(20))
        tl = [2, 3, 4, 5]
        if b == 0 or b == NB - 1:
            tl += [8, 9, 10, 11]
        return tl

    items = []
    for bi, (kind, b) in enumerate(blocks):
        tl = tiles_of(kind, b)
        for k_, j in enumerate(tl):
            items.append((bi, kind, b, j, k_, len(tl)))
    p1 = {"wl": 0, "bank": 0, "rope": 0, "ntile": 0, "stores": []}

    def ensure_wloads(upto):
        while p1["wl"] < min(upto, len(items)):
            n = p1["wl"]
            j = items[n][3]
            ws = n % 3
            dma("sp", wr[ws][:], wi_bf[j], [dbuf(("wi", j))], [b_wr[ws]], b_wr[ws])
            p1["wl"] += 1
            if cast_pos[0] < 20:
                issue_casts(1)

    def prologue_nonpe(bi):
        kind, b = blocks[bi]
        xsrc = x_own if kind == "own" else x_oth
        cs_src = cs_own if kind == "own" else cs_oth
        for tt in range(4):
            i = p1["ntile"] % 2
            p1["ntile"] += 1
            tok0 = b * 512 + tt * 128
            dma("sp", xin[i][:], xsrc[tok0:tok0 + 128, :], [], [b_xin[i]], b_xin[i])
            norm_tile(xin[i], b_xin[i], gpre, b_gpre, junk, b_junk, sm[i][0], b_sm[i][0], sm[i][1], b_sm[i][1],
                      sm[i][2], b_sm[i][2], hbf[tt], b_hbf[tt])
        csl = bi % 2
        dma("sp", cst[csl][:], cs_src[:, :, b * 512:(b + 1) * 512], [], [b_cst[csl]], b_cst[csl])

    def prologue_pe(bi):
        for tt in range(4):
            transpose_tile(hbf[tt], b_hbf[tt], hT[bi % 2], b_hT[bi % 2], tt)

    def flush_stores():
        for fn in p1["stores"]:
            fn()
        p1["stores"] = []

    issue_casts(4)
    prologue_nonpe(0)
    prologue_pe(0)
    for n, (bi, kind, b, j, kidx, ntl) in enumerate(items):
        ensure_wloads(n + 3)
        hTb, b_hTb = hT[bi % 2], b_hT[bi % 2]
        koff = 0 if kind == "own" else T
        csl = bi % 2
        ws = n % 3
        st = n % 3
        stg, b_stg = stage[st], b_stage[st]
        typ = ["qa", "qa", "ka", "ka", "va", "va", "qn", "qn", "kn", "kn", "vn", "vn"][j] if j < 12 else "gate"
        new_stores = []

        def store(dst, src, key, stg=stg, b_stg=b_stg):
            new_stores.append(lambda: dma("sp", dst, src, [b_stg], [dbuf(key)], b_stg))

        if typ in ("qa", "ka", "qn", "kn", "gate"):
            for ct in range(4):
                bank = p1["bank"] % 4
                p1["bank"] += 1

                def mm(e, ws=ws, ct=ct, bank=bank, hTb=hTb):
                    i_ = None
                    for kc in range(16):
                        i_ = e.matmul(ps[bank][:], lhsT=wr[ws][:, kc, ct * 128:(ct + 1) * 128], rhs=hTb[:, kc, :],
                                      start=(kc == 0), stop=(kc == 15))
                    return i_
                P.op("pe", mm, [b_wr[ws], b_hTb], [b_ps[bank]])
                if typ in ("qa", "ka"):
                    r = p1["rope"] % 2
                    p1["rope"] += 1
                    rb = 4 + r
                    P.op("act", lambda e, r=r, bank=bank: e.copy(out=qsb[r][:], in_=ps[bank][:]),
                         [b_ps[bank]], [b_qsb[r]])
                    P.op("pe", lambda e, r=r, rb=rb: e.matmul(ps[rb][:], lhsT=Rm[:], rhs=qsb[r][:], start=True, stop=True),
                         [b_qsb[r], b_Rm], [b_ps[rb]])
                    P.op("dve", lambda e, r=r, bank=bank, csl=csl: e.tensor_tensor(
                        out=t1[r][:], in0=ps[bank][:], in1=cst[csl][:, 0, :], op=ALU.mult),
                        [b_ps[bank], b_cst[csl]], [b_t1[r]])
                    P.op("dve", lambda e, r=r, rb=rb, csl=csl: e.tensor_tensor(
                        out=t2[r][:], in0=ps[rb][:], in1=cst[csl][:, 1, :], op=ALU.mult),
                        [b_ps[rb], b_cst[csl]], [b_t2[r]])
                    P.op("pool", lambda e, r=r, stg=stg, ct=ct: e.tensor_tensor(
                        out=stg[:, ct, :], in0=t1[r][:], in1=t2[r][:], op=ALU.add),
                        [b_t1[r], b_t2[r]], [b_stg])
                elif typ == "gate":
                    P.op("act", lambda e, bank=bank, stg=stg, ct=ct: e.activation(
                        out=stg[:, ct, :], in_=ps[bank][:], func=AF.Sigmoid), [b_ps[bank]], [b_stg])
                else:
                    P.op("act", lambda e, bank=bank, stg=stg, ct=ct: e.copy(out=stg[:, ct, :], in_=ps[bank][:]),
                         [b_ps[bank]], [b_stg])
            if typ == "qa":
                c0 = (j - 0) * 4
                store(QaT[c0:c0 + 4, :, b * 512:(b + 1) * 512].rearrange("c p n -> p c n"), stg[:], ("QaT", c0, b))
            elif typ == "ka":
                c0 = (j - 2) * 4
                store(KaT[c0:c0 + 4, :, koff + b * 512:koff + (b + 1) * 512].rearrange("c p n -> p c n"), stg[:],
                      ("KaT", c0, kind, b))
            elif typ == "qn":
                c0 = (j - 6) * 4
                store(QnT[c0:c0 + 4, :, b * 512:(b + 1) * 512].rearrange("c p n -> p c n"), stg[:], ("QnT", c0, b))
            elif typ == "kn":
                c0 = (j - 8) * 4
                if kind == "own":
                    e0 = 256 + b * 512
                    store(KnT[c0:c0 + 4, :, e0:e0 + 512].rearrange("c p n -> p c n"), stg[:], ("KnT", c0, kind, b))
                else:
                    if b == NB - 1:
                        store(KnT[c0:c0 + 4, :, 0:256].rearrange("c p n -> p c n"), stg[:, :, 256:512],
                              ("KnT", c0, kind, b, 0))
                    if b == 0:
                        e0 = (NQT + 2) * 128
                        store(KnT[c0:c0 + 4, :, e0:e0 + 256].rearrange("c p n -> p c n"), stg[:, :, 0:256],
                              ("KnT", c0, kind, b, 1))
            else:
                c0 = (j - 12) * 4
                store(gT[c0:c0 + 4, :, b * 512:(b + 1) * 512].rearrange("c p n -> p c n"), stg[:], ("gT", c0, b))
        else:
            for tt in range(4):
                bank = p1["bank"] % 4
                p1["bank"] += 1

                def mmv(e, ws=ws, tt=tt, bank=bank, hTb=hTb):
                    i_ = None
                    for kc in range(16):
                        i_ = e.matmul(ps[bank][:], lhsT=hTb[:, kc, tt * 128:(tt + 1) * 128], rhs=wr[ws][:, kc, :],
                                      start=(kc == 0), stop=(kc == 15))
                    return i_
                P.op("pe", mmv, [b_wr[ws], b_hTb], [b_ps[bank]])
                if tt % 2 == 0:
                    P.op("dve", lambda e, bank=bank, stg=stg, tt=tt: e.tensor_copy(out=stg[:, tt, :], in_=ps[bank][:]),
                         [b_ps[bank]], [b_stg])
                else:
                    P.op("act", lambda e, bank=bank, stg=stg, tt=tt: e.copy(out=stg[:, tt, :], in_=ps[bank][:]),
                         [b_ps[bank]], [b_stg])
            if typ == "va":
                for hh in range(2):
                    h = (j - 4) * 2 + hh
                    store(Va[h, koff + b * 512:koff + (b + 1) * 512, :].rearrange("(t p) e -> p t e", p=128),
                          stg[:, :, hh * 256:(hh + 1) * 256], ("Va", h, kind, b))
            else:
                for hh in range(4):
                    h = (j - 10) * 4 + hh
                    if kind == "own":
                        e0 = 256 + b * 512
                        store(Vn[h, e0:e0 + 512, :].rearrange("(t p) e -> p t e", p=128),
                              stg[:, :, hh * 128:(hh + 1) * 128], ("Vn", h, kind, b))
                    else:
                        if b == NB - 1:
                            store(Vn[h, 0:256, :].rearrange("(t p) e -> p t e", p=128),
                                  stg[:, 2:4, hh * 128:(hh + 1) * 128], ("Vn", h, kind, b, 0))
                        if b == 0:
                            e0 = (NQT + 2) * 128
                            store(Vn[h, e0:e0 + 256, :].rearrange("(t p) e -> p t e", p=128),
                                  stg[:, 0:2, hh * 128:(hh + 1) * 128], ("Vn", h, kind, b, 1))
        flush_stores()
        p1["stores"] = new_stores
        if kidx == min(1, ntl - 1) and bi + 1 < len(blocks):
            prologue_nonpe(bi + 1)
        if kidx == ntl - 1 and bi + 1 < len(blocks):
            prologue_pe(bi + 1)
    flush_stores()

    issue_casts(20 - cast_pos[0])

    def dkeys(prefix):
        return [v for k, v in dram.items() if isinstance(k, tuple) and k[0] == prefix]

    def finish():
        P.op("sp", lambda e: e.nop(), list(dram.values()), [])
        P.emit()
        return nc

    if stop_after == 1:
        return finish()

    phase_start()
    KTs = [[sb("KT", [128, 2 * T], BF16) for _ in range(2)] for _ in range(2)]
    b_KTs = [[P.buf("KT", dma=True) for _ in range(2)] for _ in range(2)]
    QTs = [[sb("QT", [128, T], BF16) for _ in range(2)] for _ in range(2)]
    b_QTs = [[P.buf("QT", dma=True) for _ in range(2)] for _ in range(2)]
    Vts = [sb("Vt", [128, NKT, 258], BF16) for _ in range(2)]
    NVP = 4
    b_Vts = [[P.buf("Vt", dma=True) for _ in range(NVP)] for _ in range(2)]
    b_Vones = [P.buf("Vones") for _ in range(2)]
    raw = [sb("raw", [128, 257], F32) for _ in range(8)]
    b_raw = [P.buf("raw") for _ in range(8)]
    ET = [sb("ET", [128, 512], BF16) for _ in range(3)]
    b_ET = [[P.buf("ET") for _ in range(2)] for _ in range(3)]
    o0 = [sb("o0", [128, 256], F32) for _ in range(4)]
    b_o0 = [P.buf("o0") for _ in range(4)]
    osb = [sb("osb", [128, 256], F32) for _ in range(2)]
    b_osb = [P.buf("osb") for _ in range(2)]
    ojunk = sb("ojunk", [128, 256], BF16); b_ojunk = P.buf("ojunk")
    oabf = [sb("oabf", [128, 256], BF16) for _ in range(4)]
    b_oabf = [P.buf("oabf") for _ in range(4)]
    sw8 = sb("sw8", [128, 256], F32); b_sw8 = P.buf("sw8", dma=True)
    oast = [sb("oast", [128, 2, 512], BF16) for _ in range(2)]
    b_oast = [P.buf("oast", dma=True) for _ in range(2)]
    lt = sb("lt", [128, 512], F32); b_lt = P.buf("lt", dma=True)
    lprod = sb("lprod", [128, 256], F32); b_lprod = P.buf("lprod")
    lsm = [sb("lsm", [128, 1], F32) for _ in range(6)]
    b_lsm = [P.buf("lsm") for _ in range(6)]
    dsm = [[sb("dsm", [128, 1], F32) for _ in range(6)] for _ in range(2)]
    b_dsm = [[P.buf("dsm") for _ in range(6)] for _ in range(2)]
    rz0 = [sb("rz0", [128, 1], F32) for _ in range(4)]
    b_rz0 = [P.buf("rz0") for _ in range(4)]

    dma("sp", lt[:], lamv.partition_broadcast(128), [], [b_lt], b_lt)
    dma("sp", sw8[:], subw.partition_broadcast(128), [], [b_sw8], b_sw8)
    P.op("dve", lambda e: e.tensor_scalar(out=sw8[:], in0=sw8[:], scalar1=float(1.0 - LAMBDA_INIT), scalar2=None,
                                          op0=ALU.mult), [b_sw8], [b_sw8])
    P.op("dve", lambda e: e.tensor_tensor(out=lprod[:, 0:128], in0=lt[:, 0:128], in1=lt[:, 128:256], op=ALU.mult),
         [b_lt], [b_lprod])
    P.op("dve", lambda e: e.tensor_tensor(out=lprod[:, 128:256], in0=lt[:, 256:384], in1=lt[:, 384:512], op=ALU.mult),
         [b_lt], [b_lprod])
    P.op("dve", lambda e: e.reduce_sum(out=lsm[0][:], in_=lprod[:, 0:128], axis=AX.X), [b_lprod], [b_lsm[0]])
    P.op("dve", lambda e: e.reduce_sum(out=lsm[1][:], in_=lprod[:, 128:256], axis=AX.X), [b_lprod], [b_lsm[1]])
    P.op("act", lambda e: e.activation(out=lsm[2][:], in_=lsm[0][:], func=AF.Exp), [b_lsm[0]], [b_lsm[2]])
    P.op("act", lambda e: e.activation(out=lsm[3][:], in_=lsm[1][:], func=AF.Exp), [b_lsm[1]], [b_lsm[3]])
    P.op("dve", lambda e: e.tensor_tensor(out=lsm[4][:], in0=lsm[2][:], in1=lsm[3][:], op=ALU.subtract),
         [b_lsm[2], b_lsm[3]], [b_lsm[4]])
    nlam, b_nlam = lsm[5], b_lsm[5]
    P.op("dve", lambda e: e.tensor_scalar(out=nlam[:], in0=lsm[4][:], scalar1=float(LAMBDA_INIT), scalar2=-1.0,
                                          op0=ALU.add, op1=ALU.mult), [b_lsm[4]], [b_nlam])
    for hs_ in range(2):
        P.op("pool", lambda e, hs_=hs_: e.memset(Vts[hs_][:, :, 256:258], 1.0), [], [b_Vones[hs_]])

    NG = T // 512
    OB = [2, 3, 4, 5, 6]
    state = {"step": 0, "ob": 0, "pend": None, "defer": []}

    def da_evac(h, g, c, banks):
        for i in range(4):
            B = banks[i]
            rw = c * 4 + i
            P.op("dve", lambda e, rw=rw, B=B: e.tensor_copy(out=raw[rw][:], in_=ps[B][:, 0:257]),
                 [b_ps[B]], [b_raw[rw]])
        for i in range(4):
            rw = c * 4 + i
            if c == 0:
                P.op("dve", lambda e, i=i, rw=rw: e.reciprocal(out=rz0[i][:], in_=raw[rw][:, 256:257]),
                     [b_raw[rw]], [b_rz0[i]])
                P.op("dve", lambda e, i=i, rw=rw: e.tensor_scalar(out=o0[i][:], in0=raw[rw][:, 0:256], scalar1=rz0[i][:],
                                                                  scalar2=None, op0=ALU.mult),
                     [b_raw[rw], b_rz0[i]], [b_o0[i]])
            else:
                k = i % 2
                d, bd = dsm[k], b_dsm[k]
                P.op("dve", lambda e, d=d, rw=rw: e.reciprocal(out=d[0][:], in_=raw[rw][:, 256:257]), [b_raw[rw]], [bd[0]])
                P.op("dve", lambda e, d=d: e.tensor_tensor(out=d[1][:], in0=d[0][:], in1=nlam[:], op=ALU.mult),
                     [bd[0], b_nlam], [bd[1]])
                P.op("dve", lambda e, d=d, rw=rw, i=i, k=k: e.scalar_tensor_tensor(
                    out=osb[k][:], in0=raw[rw][:, 0:256], scalar=d[1][:], in1=o0[i][:], op0=ALU.mult, op1=ALU.add),
                    [b_raw[rw], bd[1], b_o0[i]], [b_osb[k]])
                P.op("dve", lambda e, d=d, k=k: e.scalar_tensor_tensor(
                    out=ojunk[:], in0=osb[k][:], scalar=1.0, in1=osb[k][:], op0=ALU.mult, op1=ALU.mult,
                    accum_out=d[2][:]), [b_osb[k]], [b_ojunk, bd[2]])
                rstd_from_ss(d[2], d[3], d[4], bd[2], bd[3], bd[4], 256)
                P.op("dve", lambda e, d=d, k=k, i=i: e.scalar_tensor_tensor(
                    out=oabf[i][:], in0=osb[k][:], scalar=d[4][:], in1=sw8[:], op0=ALU.mult, op1=ALU.mult),
                    [b_osb[k], bd[4], b_sw8], [b_oabf[i]])

                def trf(i=i, k=k):
                    def tr(e):
                        i_ = None
                        for jj in range(2):
                            i_ = e.transpose(out=psb[7][:, jj * 512 + i * 128: jj * 512 + (i + 1) * 128],
                                             in_=oabf[i][:, jj * 128:(jj + 1) * 128], identity=idt[:])
                        return i_
                    P.op("pe", tr, [b_oabf[i], b_idt], [b_ps[7]])
                state["defer"].append((state["step"] + 6 + 2 * i, trf))
        if c == 1:
            def fin(h=h, g=g):
                s = (h * NG + g) % 2
                P.op("dve", lambda e, s=s: e.tensor_copy(out=oast[s][:].rearrange("p c n -> p (c n)"), in_=psb[7]),
                     [b_ps[7]], [b_oast[s]])
                dst = oaT[2 * h:2 * h + 2, :, g * 512:(g + 1) * 512].rearrange("c p n -> p c n")
                dma("sp", dst, oast[s][:], [b_oast[s]], [dbuf(("oaT", h, g))], b_oast[s])
            state["defer"].append((state["step"] + 14, fin))

    def run_deferred(force=False):
        keep = []
        for (at, fn) in state["defer"]:
            if force or at <= state["step"]:
                fn()
            else:
                keep.append((at, fn))
        state["defer"] = keep

    def emit_pv(pend):
        h, g, c, kt, banks, es = pend
        hs = h % 2

        for hf_ in range(2):
            def pv(e, hf_=hf_):
                i_ = None
                for i in (2 * hf_, 2 * hf_ + 1):
                    i_ = e.matmul(ps[banks[i]][:, 0:257], lhsT=ET[es][:, i * 128:(i + 1) * 128],
                                  rhs=Vts[hs][:, kt, 0:257], start=(kt == 0), stop=(kt == NKT - 1))
                return i_
            P.op("pe", pv, [b_ET[es][hf_], b_Vts[hs][kt * NVP // NKT], b_Vones[hs]],
                 [b_ps[banks[2 * hf_]], b_ps[banks[2 * hf_ + 1]]])
        if kt == NKT - 1:
            da_evac(h, g, c, banks)

    def da_loads(h):
        hs = h % 2
        for c in range(2):
            dma("sp", KTs[hs][c][:], KaT[2 * h + c], dkeys("KaT"), [b_KTs[hs][c]], b_KTs[hs][c])
            dma("sp", QTs[hs][c][:], QaT[2 * h + c], dkeys("QaT"), [b_QTs[hs][c]], b_QTs[hs][c])
        for vp in range(NVP):
            k0 = vp * NKT // NVP
            k1 = (vp + 1) * NKT // NVP
            src = Va[h, k0 * 128:k1 * 128, :].rearrange("(t p) e -> p t e", p=128)
            dma("sp", Vts[hs][:, k0:k1, 0:256], src, dkeys("Va"), [b_Vts[hs][vp]], b_Vts[hs][vp])

    da_loads(0)
    for h in range(4):
        hs = h % 2
        KT, b_KT, QT, b_QT = KTs[hs], b_KTs[hs], QTs[hs], b_QTs[hs]
        for g in range(NG):
            if g == 1 and h + 1 < 4:
                da_loads(h + 1)
            for c in range(2):
                banks = [OB[(state["ob"] + i) % 5] for i in range(4)]
                state["ob"] += 4
                for kt in range(NKT):
                    sbk = state["step"] % 2
                    es = state["step"] % 3
                    P.op("pe", lambda e, sbk=sbk, c=c, kt=kt, g=g, KT=KT, QT=QT: e.matmul(
                        ps[sbk][:], lhsT=KT[c][:, kt * 128:(kt + 1) * 128], rhs=QT[c][:, g * 512:(g + 1) * 512],
                        start=True, stop=True), [b_KT[c], b_QT[c]], [b_ps[sbk]])
                    for hf_ in range(2):
                        P.op("act", lambda e, sbk=sbk, es=es, hf_=hf_: e.activation(
                            out=ET[es][:, hf_ * 256:(hf_ + 1) * 256], in_=ps[sbk][:, hf_ * 256:(hf_ + 1) * 256],
                            func=AF.Exp, scale=float(SCALE)), [b_ps[sbk]], [b_ET[es][hf_]])
                    if state["pend"] is not None:
                        emit_pv(state["pend"])
                    state["pend"] = (h, g, c, kt, banks, es)
                    state["step"] += 1
                    run_deferred()
                    if state["step"] % 48 == 0:
                        issue_casts(1)
    emit_pv(state["pend"])
    state["pend"] = None
    run_deferred(force=True)

    issue_casts(1000)
    if stop_after == 2:
        return finish()

    phase_start()
    Qn = [sb("Qn", [128, T], BF16) for _ in range(2)]
    b_Qn = [P.buf("Qn", dma=True) for _ in range(2)]
    Kn = [sb("Kn", [128, NEXT * 128], BF16) for _ in range(2)]
    b_Kn = [P.buf("Kn", dma=True) for _ in range(2)]
    Vnt = [sb("Vnt", [128, NEXT, 130], BF16) for _ in range(2)]
    b_Vnt = [P.buf("Vnt", dma=True) for _ in range(2)]
    b_Vn1 = [P.buf("Vn1") for _ in range(2)]
    nbt = [sb("nbt", [128, 27, 128], F32) for _ in range(2)]
    b_nbt = [P.buf("nbt", dma=True) for _ in range(2)]
    ssb = [sb("ssb", [128, 768], F32) for _ in range(3)]
    b_ssb = [P.buf("ssb") for _ in range(3)]
    ETn = [sb("ETn", [128, 768], BF16) for _ in range(3)]
    b_ETn = [P.buf("ETn") for _ in range(3)]
    rzn = [sb("rzn", [128, 1], F32) for _ in range(3)]
    b_rzn = [P.buf("rzn") for _ in range(3)]
    onbf = [sb("onbf", [128, 128], BF16) for _ in range(3)]
    b_onbf = [P.buf("onbf") for _ in range(3)]
    onst = [sb("onst", [128, T], BF16) for _ in range(2)]
    b_onst = [P.buf("onst", dma=True) for _ in range(2)]
    for s_ in range(2):
        P.op("pool", lambda e, s_=s_: e.memset(Vnt[s_][:, :, 128:130], 1.0), [], [b_Vn1[s_]])

    def na_tiles(r):
        if r == 0:
            return list(range(0, 6)), 5
        if r == 1:
            return list(range(1, 6)), 11
        if r == NQT - 2:
            return list(range(r, r + 5)), 16
        if r == NQT - 1:
            return list(range(r - 1, r + 5)), 21
        return list(range(r, r + 5)), 0

    na_state = {"pend": [], "cnt": 0}
    NA_LAG = 2

    def na_pv(pend):
        h, s, r, tl, ws, ob = pend
        n = len(tl)

        def pv(e):
            i_ = None
            for i, et in enumerate(tl):
                i_ = e.matmul(ps[ob][:, 0:129], lhsT=ETn[ws][:, i * 128:(i + 1) * 128], rhs=Vnt[s][:, et, 0:129],
                              start=(i == 0), stop=(i == n - 1))
            return i_
        P.op("pe", pv, [b_ETn[ws], b_Vnt[s], b_Vn1[s]], [b_ps[ob]])
        P.op("dve", lambda e: e.reciprocal(out=rzn[ws][:], in_=ps[ob][:, 128:129]), [b_ps[ob]], [b_rzn[ws]])
        P.op("dve", lambda e: e.tensor_scalar(out=onbf[ws][:], in0=ps[ob][:, 0:128], scalar1=rzn[ws][:], scalar2=None,
                                              op0=ALU.mult), [b_ps[ob], b_rzn[ws]], [b_onbf[ws]])
        tb = 6 + (r // 8) % 2
        P.op("pe", lambda e: e.transpose(out=psb[tb][:, (r % 8) * 128:(r % 8 + 1) * 128], in_=onbf[ws][:],
                                         identity=idt[:]), [b_onbf[ws], b_idt], [b_ps[tb]])
        if r % 8 == 7:
            r0 = r - 7
            P.op("dve", lambda e: e.tensor_copy(out=onst[s][:, r0 * 128:(r0 + 8) * 128], in_=psb[tb]),
                 [b_ps[tb]], [b_onst[s]])
        if r == NQT - 1:
            dma("sp", onT[h], onst[s][:], [b_onst[s]], [dbuf(("onT", h))], b_onst[s])

    def na_loads(h):
        s = h % 2
        dma("sp", Qn[s][:], QnT[h], dkeys("QnT"), [b_Qn[s]], b_Qn[s])
        dma("sp", Kn[s][:], KnT[h], dkeys("KnT"), [b_Kn[s]], b_Kn[s])
        dma("sp", Vnt[s][:, :, 0:128], Vn[h].rearrange("(t p) e -> p t e", p=128), dkeys("Vn"), [b_Vnt[s]], b_Vnt[s])
        dma("sp", nbt[s][:], nbias[h], [], [b_nbt[s]], b_nbt[s])

    na_loads(0)
    for h in range(8):
        s = h % 2
        if h + 1 < 8:
            while na_state["pend"] and na_state["pend"][0][0] < h:
                na_pv(na_state["pend"].pop(0))
            na_loads(h + 1)
        for r in range(NQT):
            tl, bi0 = na_tiles(r)
            n = len(tl)
            cnt = na_state["cnt"]
            na_state["cnt"] += 1
            ws = cnt % 3
            sset = cnt % 2
            sb0, sb1 = 2 * sset, 2 * sset + 1
            ob = 4 + sset

            def smm(e, tl=tl, s=s, r=r, sb0=sb0, sb1=sb1):
                i_ = None
                for i, et in enumerate(tl):
                    bk = sb0 if i < 4 else sb1
                    i_ = e.matmul(ps[bk][:, (i % 4) * 128:(i % 4 + 1) * 128], lhsT=Kn[s][:, et * 128:(et + 1) * 128],
                                  rhs=Qn[s][:, r * 128:(r + 1) * 128], start=True, stop=True)
                return i_
            P.op("pe", smm, [b_Kn[s], b_Qn[s]], [b_ps[sb0], b_ps[sb1]])
            n0 = min(n, 4)
            P.op("dve", lambda e, ws=ws, sb0=sb0, n0=n0, s=s, bi0=bi0: e.scalar_tensor_tensor(
                out=ssb[ws][:, 0:n0 * 128], in0=ps[sb0][:, 0:n0 * 128], scalar=float(SCALE),
                in1=nbt[s][:, bi0:bi0 + n0, :].rearrange("p a b -> p (a b)"), op0=ALU.mult, op1=ALU.add),
                [b_ps[sb0], b_nbt[s]], [b_ssb[ws]])
            if n > 4:
                n1 = n - 4
                P.op("dve", lambda e, ws=ws, sb1=sb1, n1=n1, s=s, bi0=bi0: e.scalar_tensor_tensor(
                    out=ssb[ws][:, 512:512 + n1 * 128], in0=ps[sb1][:, 0:n1 * 128], scalar=float(SCALE),
                    in1=nbt[s][:, bi0 + 4:bi0 + 4 + n1, :].rearrange("p a b -> p (a b)"), op0=ALU.mult, op1=ALU.add),
                    [b_ps[sb1], b_nbt[s]], [b_ssb[ws]])
            P.op("act", lambda e, ws=ws, n=n: e.activation(out=ETn[ws][:, 0:n * 128], in_=ssb[ws][:, 0:n * 128],
                                                           func=AF.Exp), [b_ssb[ws]], [b_ETn[ws]])
            na_state["pend"].append((h, s, r, tl, ws, ob))
            if len(na_state["pend"]) > NA_LAG:
                na_pv(na_state["pend"].pop(0))
    while na_state["pend"]:
        na_pv(na_state["pend"].pop(0))

    if stop_after == 3:
        return finish()

    phase_start()
    wr4 = [sb("wr4", [128, 16, 512], BF16) for _ in range(3)]
    b_wr4 = [P.buf("wr4", dma=True) for _ in range(3)]
    gta = [sb("gta", [128, 4, 512], BF16) for _ in range(3)]
    b_gta = [P.buf("gta", dma=True) for _ in range(3)]
    gtb = [sb("gtb", [128, 4, 512], BF16) for _ in range(3)]
    b_gtb = [P.buf("gtb", dma=True) for _ in range(3)]
    oab = [sb("oab", [128, 8, 512], BF16) for _ in range(2)]
    b_oab = [P.buf("oab", dma=True) for _ in range(2)]
    onb = [sb("onb", [128, 8, 512], BF16) for _ in range(2)]
    b_onb = [P.buf("onb", dma=True) for _ in range(2)]
    mixT = sb("mixT", [128, 16, 512], BF16); b_mixT = P.buf("mixT")
    ta = [sb("ta", [128, 512], F32) for _ in range(2)]
    b_ta = [P.buf("ta") for _ in range(2)]
    tb_ = [sb("tb", [128, 512], F32) for _ in range(2)]
    b_tb = [P.buf("tb") for _ in range(2)]
    ysb = [sb("ysb", [128, D], F32) for _ in range(4)]
    b_ysb = [P.buf("ysb") for _ in range(4)]
    xin4 = [sb("xin4", [128, D], F32) for _ in range(2)]
    b_xin4 = [P.buf("xin4", dma=True) for _ in range(2)]
    gpost = sb("gpost", [128, D], F32); b_gpost = P.buf("gpost", dma=True)
    junk4 = sb("junk4", [128, D], BF16); b_junk4 = P.buf("junk4")
    sm4 = [[sb("sm4", [128, 1], F32) for _ in range(3)] for _ in range(2)]
    b_sm4 = [[P.buf("sm4") for _ in range(3)] for _ in range(2)]
    dma("sp", gpost[:], gains[1:2, :].partition_broadcast(128), [], [b_gpost], b_gpost)

    items4 = []
    for b in range(NB):
        for ctg in range(4):
            items4.append((b, "ab", ctg))
        for cg in range(4):
            items4.append((b, "out", cg))
    p4 = {"wl": 0, "bk": 0, "tc": 0, "xc": 0, "tail": [], "st": None}

    def ensure_loads4(upto):
        while p4["wl"] < min(upto, len(items4)):
            n = p4["wl"]
            b, typ, idx = items4[n]
            ws = n % 3
            if typ == "ab":
                gs = (n // 8 * 4 + idx) % 3
                dma("sp", wr4[ws][:], wab_bf[idx], [dbuf(("wab", idx))], [b_wr4[ws]], b_wr4[ws])
                dma("sp", gta[gs][:], gT[idx * 4:idx * 4 + 4, :, b * 512:(b + 1) * 512].rearrange("c p n -> p c n"),
                    dkeys("gT"), [b_gta[gs]], b_gta[gs])
                dma("sp", gtb[gs][:],
                    gT[16 + idx * 4:16 + idx * 4 + 4, :, b * 512:(b + 1) * 512].rearrange("c p n -> p c n"),
                    dkeys("gT"), [b_gtb[gs]], b_gtb[gs])
            else:
                dma("sp", wr4[ws][:], wout_bf[idx], [dbuf(("wout", idx))], [b_wr4[ws]], b_wr4[ws])
            p4["wl"] += 1

    def load_blk4(b):
        s_ = b % 2
        dma("sp", oab[s_][:], oaT[:, :, b * 512:(b + 1) * 512].rearrange("c p n -> p c n"), dkeys("oaT"),
            [b_oab[s_]], b_oab[s_])
        dma("sp", onb[s_][:], onT[:, :, b * 512:(b + 1) * 512].rearrange("c p n -> p c n"), dkeys("onT"),
            [b_onb[s_]], b_onb[s_])

    load_blk4(0)
    for n, (b, typ, idx) in enumerate(items4):
        ensure_loads4(n + 3)
        s = b % 2
        ws = n % 3
        if typ == "ab":
            ctg = idx
            gs = (n // 8 * 4 + idx) % 3
            if idx == 0 and b + 1 < NB:
                load_blk4(b + 1)
            for ct in range(4):
                bA = p4["bk"] % 4
                bB = (p4["bk"] + 1) % 4
                p4["bk"] += 2

                def mma(e, ws=ws, ct=ct, bA=bA, s=s):
                    i_ = None
                    for kc in range(8):
                        i_ = e.matmul(ps[bA][:], lhsT=wr4[ws][:, kc, ct * 128:(ct + 1) * 128], rhs=oab[s][:, kc, :],
                                      start=(kc == 0), stop=(kc == 7))
                    return i_

                def mmb(e, ws=ws, ct=ct, bB=bB, s=s):
                    i_ = None
                    for kc in range(8):
                        i_ = e.matmul(ps[bB][:], lhsT=wr4[ws][:, 8 + kc, ct * 128:(ct + 1) * 128], rhs=onb[s][:, kc, :],
                                      start=(kc == 0), stop=(kc == 7))
                    return i_
                P.op("pe", mma, [b_wr4[ws], b_oab[s]], [b_ps[bA]])
                P.op("pe", mmb, [b_wr4[ws], b_onb[s]], [b_ps[bB]])
                k = p4["tc"] % 2
                p4["tc"] += 1
                P.op("dve", lambda e, k=k, bA=bA, gs=gs, ct=ct: e.tensor_tensor(
                    out=ta[k][:], in0=ps[bA][:], in1=gta[gs][:, ct, :], op=ALU.mult), [b_ps[bA], b_gta[gs]], [b_ta[k]])
                P.op("dve", lambda e, k=k, bB=bB, gs=gs, ct=ct: e.tensor_tensor(
                    out=tb_[k][:], in0=ps[bB][:], in1=gtb[gs][:, ct, :], op=ALU.mult), [b_ps[bB], b_gtb[gs]], [b_tb[k]])
                P.op("pool", lambda e, k=k, ctg=ctg, ct=ct: e.tensor_tensor(
                    out=mixT[:, ctg * 4 + ct, :], in0=ta[k][:], in1=tb_[k][:], op=ALU.add),
                    [b_ta[k], b_tb[k]], [b_mixT])
        else:
            cg = idx
            for tt in range(4):
                bY = p4["bk"] % 4
                p4["bk"] += 1

                def mmy(e, ws=ws, tt=tt, bY=bY):
                    i_ = None
                    for kc in range(16):
                        i_ = e.matmul(ps[bY][:], lhsT=mixT[:, kc, tt * 128:(tt + 1) * 128], rhs=wr4[ws][:, kc, :],
                                      start=(kc == 0), stop=(kc == 15))
                    return i_
                P.op("pe", mmy, [b_wr4[ws], b_mixT], [b_ps[bY]])
                P.op("act", lambda e, tt=tt, cg=cg, bY=bY: e.copy(out=ysb[tt][:, cg * 512:(cg + 1) * 512], in_=ps[bY][:]),
                     [b_ps[bY]], [b_ysb[tt]])
            if cg == 3:
                for tt in range(4):
                    def tail(b=b, tt=tt):
                        if p4["st"] is not None:
                            p4["st"]()
                            p4["st"] = None
                        i = p4["xc"] % 2
                        p4["xc"] += 1
                        tok0 = b * 512 + tt * 128
                        dma("sp", xin4[i][:], x_own[tok0:tok0 + 128, :], [], [b_xin4[i]], b_xin4[i])
                        P.op("act", lambda e, tt=tt, i=i: e.activation(out=junk4[:], in_=ysb[tt][:], func=AF.Square,
                                                                      accum_out=sm4[i][0][:]),
                             [b_ysb[tt]], [b_junk4, b_sm4[i][0]])
                        rstd_from_ss(sm4[i][0], sm4[i][1], sm4[i][2], b_sm4[i][0], b_sm4[i][1], b_sm4[i][2], D)
                        P.op("dve", lambda e, tt=tt, i=i: e.scalar_tensor_tensor(
                            out=ysb[tt][:], in0=ysb[tt][:], scalar=sm4[i][2][:], in1=gpost[:], op0=ALU.mult,
                            op1=ALU.mult), [b_sm4[i][2], b_gpost], [b_ysb[tt]])
                        P.op("dve", lambda e, tt=tt, i=i: e.tensor_tensor(out=xin4[i][:], in0=xin4[i][:], in1=ysb[tt][:],
                                                                          op=ALU.add), [b_ysb[tt]], [b_xin4[i]])
                        p4["st"] = lambda: dma("sp", x1[tok0:tok0 + 128, :], xin4[i][:], [b_xin4[i]],
                                               [dbuf(("x1", b, tt))], b_xin4[i])
                    p4["tail"].append(tail)
        if typ == "ab" and p4["tail"] and (idx < 3 or True):
            p4["tail"].pop(0)()
    while p4["tail"]:
        p4["tail"].pop(0)()
    if p4["st"] is not None:
        p4["st"]()
        p4["st"] = None

    if stop_after == 4:
        return finish()

    phase_start()
    wr5 = [sb("wr5", [128, 16, 512], BF16) for _ in range(2)]
    b_wr5 = [P.buf("wr5", dma=True) for _ in range(2)]
    uT = sb("uT", [128, 64, 512], BF16); b_uT = P.buf("uT")
    h2T = sb("h2T", [128, 16, 512], BF16); b_h2T = P.buf("h2T")
    xin5 = [sb("xin5", [128, D], F32) for _ in range(2)]
    b_xin5 = [P.buf("xin5", dma=True) for _ in range(2)]
    zsb = [sb("zsb", [128, D], F32) for _ in range(4)]
    b_zsb = [P.buf("zsb") for _ in range(4)]
    gm1 = sb("gm1", [128, D], F32); b_gm1 = P.buf("gm1", dma=True)
    gm2 = sb("gm2", [128, D], F32); b_gm2 = P.buf("gm2", dma=True)
    hb5 = [sb("hb5", [128, D], BF16) for _ in range(4)]
    b_hb5 = [P.buf("hb5") for _ in range(4)]
    junk5 = sb("junk5", [128, D], BF16); b_junk5 = P.buf("junk5")
    rr = [sb("rr", [128, 512], F32) for _ in range(2)]
    b_rr = [P.buf("rr") for _ in range(2)]
    sm5 = [[sb("sm5", [128, 1], F32) for _ in range(3)] for _ in range(2)]
    b_sm5 = [[P.buf("sm5") for _ in range(3)] for _ in range(2)]
    dma("sp", gm1[:], gains[2:3, :].partition_broadcast(128), [], [b_gm1], b_gm1)
    dma("sp", gm2[:], gains[3:4, :].partition_broadcast(128), [], [b_gm2], b_gm2)
    out_bufs = []
    p5 = {"wc": 0, "xc": 0, "bk": 0, "rc": 0, "tail": [], "st": None}

    def flush_st5():
        if p5["st"] is not None:
            p5["st"]()
            p5["st"] = None

    def prologue5_nonpe(b):
        flush_st5()
        for tt in range(4):
            i = p5["xc"] % 2
            p5["xc"] += 1
            tok0 = b * 512 + tt * 128
            dma("sp", xin5[i][:], x1[tok0:tok0 + 128, :], [dbuf(("x1", b, tt))], [b_xin5[i]], b_xin5[i])
            norm_tile(xin5[i], b_xin5[i], gm1, b_gm1, junk5, b_junk5, sm5[i][0], b_sm5[i][0], sm5[i][1], b_sm5[i][1],
                      sm5[i][2], b_sm5[i][2], hb5[tt], b_hb5[tt])

    def prologue5_pe():
        for tt in range(4):
            transpose_tile(hb5[tt], b_hb5[tt], h2T, b_h2T, tt)

    def make_tail5(b, tt):
        def tail():
            flush_st5()
            i = p5["xc"] % 2
            p5["xc"] += 1
            tok0 = b * 512 + tt * 128
            dma("sp", xin5[i][:], x1[tok0:tok0 + 128, :], [dbuf(("x1", b, tt))], [b_xin5[i]], b_xin5[i])
            P.op("act", lambda e: e.activation(out=junk5[:], in_=zsb[tt][:], func=AF.Square,
                                               accum_out=sm5[i][0][:]), [b_zsb[tt]], [b_junk5, b_sm5[i][0]])
            rstd_from_ss(sm5[i][0], sm5[i][1], sm5[i][2], b_sm5[i][0], b_sm5[i][1], b_sm5[i][2], D)
            P.op("dve", lambda e: e.scalar_tensor_tensor(
                out=zsb[tt][:], in0=zsb[tt][:], scalar=sm5[i][2][:], in1=gm2[:], op0=ALU.mult, op1=ALU.mult),
                [b_sm5[i][2], b_gm2], [b_zsb[tt]])
            P.op("dve", lambda e: e.tensor_tensor(out=xin5[i][:], in0=xin5[i][:], in1=zsb[tt][:], op=ALU.add),
                 [b_zsb[tt]], [b_xin5[i]])
            ob_ = dbuf(("out", b, tt))
            out_bufs.append(ob_)
            p5["st"] = lambda: dma("sp", out[tok0:tok0 + 128, :], xin5[i][:], [b_xin5[i]], [ob_], b_xin5[i])
        return tail

    prologue5_nonpe(0)
    prologue5_pe()
    for b in range(NB):
        for fg in range(16):
            ws = p5["wc"] % 2
            p5["wc"] += 1
            dma("sp", wr5[ws][:], wup_bf[fg], [dbuf(("wup", fg))], [b_wr5[ws]], b_wr5[ws])
            for fc in range(4):
                bU = p5["bk"] % 2
                p5["bk"] += 1

                def mmu(e, ws=ws, fc=fc, bU=bU):
                    i_ = None
                    for kc in range(16):
                        i_ = e.matmul(ps[bU][:], lhsT=wr5[ws][:, kc, fc * 128:(fc + 1) * 128], rhs=h2T[:, kc, :],
                                      start=(kc == 0), stop=(kc == 15))
                    return i_
                P.op("pe", mmu, [b_wr5[ws], b_h2T], [b_ps[bU]])
                k = p5["rc"] % 2
                p5["rc"] += 1
                P.op("act", lambda e, k=k, bU=bU: e.activation(out=rr[k][:], in_=ps[bU][:], func=AF.Relu),
                     [b_ps[bU]], [b_rr[k]])
                P.op("pool", lambda e, k=k, fg=fg, fc=fc: e.tensor_tensor(
                    out=uT[:, fg * 4 + fc, :], in0=rr[k][:], in1=rr[k][:], op=ALU.mult), [b_rr[k]], [b_uT])
            if p5["tail"] and fg % 2 == 1:
                p5["tail"].pop(0)()
            if fg == 10 and b + 1 < NB:
                prologue5_nonpe(b + 1)
        if b + 1 < NB:
            prologue5_pe()
        for cg in range(4):
            for fs in range(4):
                ws = p5["wc"] % 2
                p5["wc"] += 1
                dma("sp", wr5[ws][:], wdn_bf[cg * 4 + fs], [dbuf(("wdn", cg * 4 + fs))], [b_wr5[ws]], b_wr5[ws])
                for tt in range(4):
                    bZ = 2 + tt

                    def mmz(e, ws=ws, tt=tt, bZ=bZ, fs=fs):
                        i_ = None
                        for fc in range(16):
                            i_ = e.matmul(ps[bZ][:], lhsT=uT[:, fs * 16 + fc, tt * 128:(tt + 1) * 128],
                                          rhs=wr5[ws][:, fc, :], start=(fs == 0 and fc == 0),
                                          stop=(fs == 3 and fc == 15))
                        return i_
                    P.op("pe", mmz, [b_wr5[ws], b_uT], [b_ps[bZ]])
            for tt in range(4):
                bZ = 2 + tt
                P.op("act", lambda e, tt=tt, cg=cg, bZ=bZ: e.copy(out=zsb[tt][:, cg * 512:(cg + 1) * 512], in_=ps[bZ][:]),
                     [b_ps[bZ]], [b_zsb[tt]])
        for tt in range(4):
            p5["tail"].append(make_tail5(b, tt))
    while p5["tail"]:
        p5["tail"].pop(0)()
    flush_st5()

    P.op("sp", lambda e: e.nop(), out_bufs, [])
    P.emit()
    return nc


def _rope_tables(pos):
    inv = (1.0 / (np.float32(10000.0) ** (np.arange(0, 128, 2, dtype=np.float32) / np.float32(128)))).astype(np.float32)
    ang = (pos.astype(np.float32)[:, None] * inv[None, :]).astype(np.float32)
    c = np.cos(ang).astype(np.float32).T
    s = np.sin(ang).astype(np.float32).T
    out = np.empty((128, 2, len(pos)), np.float32)
    out[0:64, 0] = c
    out[64:128, 0] = c
    out[0:64, 1] = -s
    out[64:128, 1] = s
    return out


def _na_bias_layout(rpb, S, half):
    T = S // 2
    NQT = T // 128
    rows = S // GRID_W
    kh = min(8, rows)
    H = rpb.shape[0]
    out = np.full((H, 128, 27, 128), NEG, np.float32)

    def ext_to_global(e):
        if e < 2:
            return (1 - half) * NQT + (NQT - 2 + e)
        if e < NQT + 2:
            return half * NQT + (e - 2)
        return (1 - half) * NQT + (e - NQT - 2)

    pats = [(2, list(range(2, 7)), 0), (0, list(range(0, 6)), 5), (1, list(range(1, 6)), 11),
            (NQT - 2, list(range(NQT - 2, NQT + 3)), 16), (NQT - 1, list(range(NQT - 2, NQT + 4)), 21)]
    kk = np.arange(128)
    ka, kc = kk // 64, kk % 64
    for (r, tl, base) in pats:
        Rg = half * NQT + r
        qrow = 2 * Rg + ka
        qcol = kc
        rs = np.clip(qrow - kh // 2, 0, rows - kh)
        cstart = np.clip(qcol - 8, 0, GRID_W - 16)
        for i, e in enumerate(tl):
            Kg = ext_to_global(e)
            krow = 2 * Kg + ka
            kcol = kc
            valid = ((krow[:, None] >= rs[None, :]) & (krow[:, None] < rs[None, :] + kh) &
                     (kcol[:, None] >= cstart[None, :]) & (kcol[:, None] < cstart[None, :] + 16))
            dr = np.clip(krow[:, None] - qrow[None, :] + 7, 0, 14)
            dc = np.clip(kcol[:, None] - qcol[None, :] + 15, 0, 30)
            g = rpb[:, dr, dc]
            out[:, :, base + i, :] = np.where(valid[None], g, np.float32(NEG))
    return out


_PROG_CACHE = {}


def run_layer(inputs, debug=False):
    x = np.asarray(inputs["x"], np.float32)
    B, S, _ = x.shape
    T = S // 2
    ncores = 2 * B
    key = (T, debug)
    if key not in _PROG_CACHE:
        _PROG_CACHE[key] = build_program(T, debug)
    nc = _PROG_CACHE[key]
    f = lambda k: np.ascontiguousarray(np.asarray(inputs[k], np.float32)[0])
    w_in = f("w_in")
    w_ab = np.ascontiguousarray(np.concatenate([f("w_branch_a"), f("w_branch_b")], axis=0))
    w_out = f("w_out")
    w_up = f("w_up")
    w_dn = f("w_down")
    gains = np.ascontiguousarray(np.stack([f("norm_mix_pre"), f("norm_mix_post"), f("norm_mlp_pre"), f("norm_mlp_post")]))
    lamv = np.ascontiguousarray(np.concatenate([f("lam_q1"), f("lam_k1"), f("lam_q2"), f("lam_k2")])[None, :])
    subw = f("subln_w")[None, :]
    rpb = f("na_rpb")
    ident = np.eye(128, dtype=np.float32).astype(ml_dtypes.bfloat16)
    nb_half = [_na_bias_layout(rpb, S, hf) for hf in range(2)]
    cs_half = [_rope_tables(np.arange(hf * T, (hf + 1) * T)) for hf in range(2)]
    in_maps = []
    for c in range(ncores):
        b, hf = c // 2, c % 2
        in_maps.append({
            "x_own": np.ascontiguousarray(x[b, hf * T:(hf + 1) * T]),
            "x_oth": np.ascontiguousarray(x[b, (1 - hf) * T:(2 - hf) * T]),
            "w_in": w_in, "w_ab": w_ab, "w_out": w_out, "w_up": w_up, "w_dn": w_dn,
            "gains": gains, "lamv": lamv, "subw": subw, "nbias": nb_half[hf],
            "cs_own": cs_half[hf], "cs_oth": cs_half[1 - hf], "ident": ident,
        })
    res = run_bass_kernel_spmd(nc, in_maps, core_ids=list(range(ncores)))
    outp = np.empty((B, S, D), np.float32)
    for c in range(ncores):
        b, hf = c // 2, c % 2
        outp[b, hf * T:(hf + 1) * T] = res.results[c]["out"]
    if debug:
        return outp, res.results
    return outp


def kernel(**inputs):
    return run_layer(inputs)
```

```python
import numpy as np
import ml_dtypes
import concourse.bass as bass
import concourse.mybir as mybir
from concourse.bass_utils import run_bass_kernel_spmd

F32 = mybir.dt.float32
BF16 = mybir.dt.bfloat16
AF = mybir.ActivationFunctionType
ALU = mybir.AluOpType
AX = mybir.AxisListType

D = 2048
DFF = 8192
INC = 10240
EPS = 1e-6
GRID_W = 64
NEG = -30000.0
LAMBDA_INIT = 0.8 - 0.6 * 1.0
SCALE = 128 ** -0.5


class Buf:
    __slots__ = ("name", "last_w", "readers", "sem", "cnt", "base", "excl")

    def __init__(self, name):
        self.name = name
        self.excl = False
        self.last_w = None
        self.readers = []
        self.sem = None
        self.cnt = 0
        self.base = ()


class Op:
    __slots__ = ("eng", "fn", "deps", "is_dma", "sem", "val", "waited")

    def __init__(self, eng, fn):
        self.eng = eng
        self.fn = fn
        self.deps = []
        self.is_dma = False
        self.sem = None
        self.val = 0
        self.waited = False


class Prog:
    ENGS = ("pe", "act", "dve", "pool", "sp")
    SAME_ENG_SYNC = ("act", "dve", "pool")

    def __init__(self, nc):
        self.nc = nc
        self.streams = {e: [] for e in self.ENGS}
        self.eng_sem = {}
        self.free_sems = []
        self.phase_bufs = []
        self.barrier = []
        self.nsem = 0

    def new_sem(self, name):
        self.nsem += 1
        return self.nc.alloc_semaphore(f"{name}_{self.nsem}")

    def new_phase(self):
        bar = []
        seen = set()
        for b in self.phase_bufs:
            for o in ([b.last_w] if b.last_w is not None else []) + b.readers:
                if id(o) not in seen:
                    seen.add(id(o))
                    bar.append(o)
            for o in b.base:
                if id(o) not in seen:
                    seen.add(id(o))
                    bar.append(o)
            if b.sem is not None:
                self.free_sems.append((b.sem, b.cnt))
        self.barrier = bar
        self.phase_bufs = []

    def buf(self, name, dma=False):
        b = Buf(name)
        b.readers = list(self.barrier)
        b.base = tuple(self.barrier)
        if dma:
            if self.free_sems:
                b.sem, b.cnt = self.free_sems.pop()
            else:
                b.sem = self.new_sem("d_" + name)
        self.phase_bufs.append(b)
        return b

    def op(self, eng, fn, reads=(), writes=(), dma_buf=None):
        o = Op(eng, fn)
        deps = []
        seen = set()

        def add(d):
            if d is None or id(d) in seen:
                return
            if (not d.is_dma) and d.eng == eng and eng not in self.SAME_ENG_SYNC:
                return
            seen.add(id(d))
            deps.append(d)

        for b in reads:
            add(b.last_w)
            if b.last_w is None:
                for r in b.base:
                    add(r)
            if b.excl:
                for r in b.readers:
                    if r.eng != eng:
                        add(r)
        for b in writes:
            add(b.last_w)
            for r in b.readers:
                add(r)
        o.deps = deps
        for d in deps:
            d.waited = True
        if dma_buf is not None:
            o.is_dma = True
            dma_buf.cnt += 16
            o.sem = dma_buf.sem
            o.val = dma_buf.cnt
        for b in writes:
            b.last_w = o
            b.readers = []
        for b in reads:
            if b.last_w is o:
                continue
            if not o.is_dma:
                b.readers = [r for r in b.readers if r.is_dma or r.eng != eng]
            b.readers.append(o)
        self.streams[eng].append(o)
        return o

    def emit(self):
        nc = self.nc
        for e in ("pe", "act", "dve", "pool"):
            self.eng_sem[e] = self.new_sem("e_" + e)
            r = 0
            for o in self.streams[e]:
                if o.waited and not o.is_dma:
                    r += 1
                    o.sem = self.eng_sem[e]
                    o.val = r
        streams = self.streams

        def run(e, eng):
            known = {}
            for o in streams[e]:
                need = {}
                for d in o.deps:
                    k = d.sem.num
                    if known.get(k, 0) >= d.val:
                        continue
                    if k not in need or need[k][1] < d.val:
                        need[k] = (d.sem, d.val)
                for k, (sm_, v_) in need.items():
                    eng.wait_ge(sm_, v_)
                    known[k] = v_
                ins = o.fn(eng)
                if o.is_dma:
                    ins.then_inc(o.sem, 16)
                elif o.waited:
                    ins.then_inc(o.sem, 1)

        with nc.Block() as block:
            @block.tensor
            def _(eng):
                run("pe", eng)

            @block.scalar
            def _(eng):
                run("act", eng)

            @block.vector
            def _(eng):
                run("dve", eng)

            @block.gpsimd
            def _(eng):
                run("pool", eng)

            @block.sync
            def _(eng):
                run("sp", eng)


def build_program(T, debug=False, stop_after=5):
    assert T % 512 == 0 and T >= 1024
    NQT = T // 128
    NB = T // 512
    NKT = 2 * NQT
    NEXT = NQT + 4
    nc = bass.Bass("TRN2", target_bir_lowering=False)
    P = Prog(nc)

    def din(name, shape, dt=F32):
        return nc.dram_tensor(name, shape, dt, kind="ExternalInput").ap()

    def dscr(name, shape, dt):
        kind = "ExternalOutput" if (debug and not name.startswith("w")) else "Internal"
        return nc.dram_tensor(name, shape, dt, kind=kind).ap()

    x_own = din("x_own", [T, D])
    x_oth = din("x_oth", [T, D])
    w_in = din("w_in", [D, INC])
    w_ab = din("w_ab", [2048, 2048])
    w_out = din("w_out", [2048, 2048])
    w_up = din("w_up", [D, DFF])
    w_dn = din("w_dn", [DFF, D])
    gains = din("gains", [4, D])
    lamv = din("lamv", [1, 512])
    subw = din("subw", [1, 256])
    nbias = din("nbias", [8, 128, 27, 128])
    cs_own = din("cs_own", [128, 2, T])
    cs_oth = din("cs_oth", [128, 2, T])
    ident_d = din("ident", [128, 128], BF16)
    out = nc.dram_tensor("out", [T, D], F32, kind="ExternalOutput").ap()

    wi_bf = dscr("wi_bf", [20, 128, 16, 512], BF16)
    wab_bf = dscr("wab_bf", [4, 128, 16, 512], BF16)
    wout_bf = dscr("wout_bf", [4, 128, 16, 512], BF16)
    wup_bf = dscr("wup_bf", [16, 128, 16, 512], BF16)
    wdn_bf = dscr("wdn_bf", [16, 128, 16, 512], BF16)
    QaT = dscr("QaT", [8, 128, T], BF16)
    KaT = dscr("KaT", [8, 128, 2 * T], BF16)
    Va = dscr("Va", [4, 2 * T, 256], BF16)
    QnT = dscr("QnT", [8, 128, T], BF16)
    KnT = dscr("KnT", [8, 128, NEXT * 128], BF16)
    Vn = dscr("Vn", [8, NEXT * 128, 128], BF16)
    gT = dscr("gT", [32, 128, T], BF16)
    oaT = dscr("oaT", [8, 128, T], BF16)
    onT = dscr("onT", [8, 128, T], BF16)
    x1 = dscr("x1", [T, D], F32)

    dram = {}

    def dbuf(key):
        b = dram.get(key)
        if b is None:
            b = Buf(str(key))
            dram[key] = b
        return b

    SB_BASE = 16512
    SB_TOP = 229344
    sb_off = [SB_BASE]
    sb_persist = [SB_BASE]
    uid = [0]

    def sb(name, shape, dt):
        nbytes = int(np.prod(shape[1:])) * (4 if dt == F32 else 2)
        nbytes = (nbytes + 63) // 64 * 64
        uid[0] += 1
        t = nc.alloc_sbuf_tensor_at(f"{name}_{uid[0]}", list(shape), dt, offset=sb_off[0])
        sb_off[0] += nbytes
        assert sb_off[0] <= SB_TOP, (name, sb_off[0])
        return t

    def phase_start():
        P.new_phase()
        sb_off[0] = sb_persist[0]

    ps = [nc.alloc_psum_tensor(f"ps{i}", [128, 512], F32) for i in range(8)]
    psb = [p[:].bitcast(BF16) for p in ps]
    b_ps = [Buf(f"ps{i}") for i in range(8)]
    for b_ in b_ps:
        b_.excl = True

    def dma(eng, out_ap, in_ap, reads, writes, sem_buf):
        return P.op(eng, lambda e: e.dma_start(out=out_ap, in_=in_ap), reads, writes, dma_buf=sem_buf)

    idt = sb("idt", [128, 128], BF16)
    Rm = sb("Rm", [128, 128], BF16)
    mhalf = sb("mhalf", [128, 1], F32)
    sb_persist[0] = sb_off[0]
    b_idt = Buf("idt"); b_idt.sem = P.new_sem("d_idt")
    b_Rm = Buf("Rm")
    b_mhalf = Buf("mhalf")
    dma("sp", idt[:], ident_d, [], [b_idt], b_idt)
    P.op("pool", lambda e: e.memset(mhalf[:], -0.5), [], [b_mhalf])
    P.op("pool", lambda e: e.memset(Rm[:], 0.0), [], [b_Rm])
    P.op("dve", lambda e: e.tensor_copy(out=Rm[0:64, 64:128], in_=idt[0:64, 0:64]), [b_idt], [b_Rm])
    P.op("dve", lambda e: e.tensor_copy(out=Rm[64:128, 0:64], in_=idt[64:128, 64:128]), [b_idt], [b_Rm])

    cslot = [Buf(f"cslot{i}") for i in range(4)]
    for cb in cslot:
        cb.sem = P.new_sem("cslot")
    cast_list = []
    for j in range(20):
        cast_list.append((wi_bf[j], w_in[:, j * 512:(j + 1) * 512].rearrange("(k p) n -> p k n", p=128), ("wi", j)))
    for j in range(4):
        cast_list.append((wab_bf[j], w_ab[:, j * 512:(j + 1) * 512].rearrange("(k p) n -> p k n", p=128), ("wab", j)))
    for j in range(4):
        cast_list.append((wout_bf[j], w_out[:, j * 512:(j + 1) * 512].rearrange("(k p) n -> p k n", p=128), ("wout", j)))
    for j in range(16):
        cast_list.append((wup_bf[j], w_up[:, j * 512:(j + 1) * 512].rearrange("(k p) n -> p k n", p=128), ("wup", j)))
    for j in range(16):
        cast_list.append((wdn_bf[j], w_dn[(j % 4) * 2048:(j % 4 + 1) * 2048, (j // 4) * 512:(j // 4 + 1) * 512]
                          .rearrange("(k p) n -> p k n", p=128), ("wdn", j)))
    cast_pos = [0]

    def issue_casts(n):
        for _ in range(n):
            if cast_pos[0] >= len(cast_list):
                return
            dst, src, key = cast_list[cast_pos[0]]
            sl = cslot[cast_pos[0] % 4]
            cast_pos[0] += 1
            dma("pool", dst, src, [], [sl, dbuf(key)], sl)


    def rstd_from_ss(ss, v, rstd, b_ss, b_v, b_rstd, n):
        P.op("dve", lambda e: e.tensor_scalar(out=v[:], in0=ss[:], scalar1=1.0 / n, scalar2=EPS,
                                              op0=ALU.mult, op1=ALU.add), [b_ss], [b_v])
        P.op("pool", lambda e: e.tensor_tensor(out=rstd[:], in0=v[:], in1=mhalf[:], op=ALU.pow),
             [b_v, b_mhalf], [b_rstd])

    def norm_tile(xt, b_xt, gbc, b_g, junk, b_junk, ss, b_ss, v, b_v, rstd, b_rstd, hb, b_hb):
        P.op("act", lambda e: e.activation(out=junk[:], in_=xt[:], func=AF.Square, accum_out=ss[:]),
             [b_xt], [b_junk, b_ss])
        rstd_from_ss(ss, v, rstd, b_ss, b_v, b_rstd, D)
        P.op("dve", lambda e: e.scalar_tensor_tensor(out=hb[:], in0=xt[:], scalar=rstd[:], in1=gbc[:],
                                                     op0=ALU.mult, op1=ALU.mult),
             [b_xt, b_rstd, b_g], [b_hb])

    def transpose_tile(hb, b_hb, hTb, b_hTb, tt, cp_engs=("act", "dve")):
        for half in range(2):
            bank = 6 + half

            def tr(e, half=half, bank=bank):
                i = None
                for k in range(8):
                    kc = half * 8 + k
                    i = e.transpose(out=psb[bank][:, k * 128:(k + 1) * 128],
                                    in_=hb[:, kc * 128:(kc + 1) * 128], identity=idt[:])
                return i
            P.op("pe", tr, [b_hb, b_idt], [b_ps[bank]])
            dst = hTb[:, half * 8:(half + 1) * 8, tt * 128:(tt + 1) * 128]
            src = psb[bank].rearrange("p (k n) -> p k n", k=8)
            if cp_engs[half] == "act":
                P.op("act", lambda e, dst=dst, src=src: e.copy(out=dst, in_=src), [b_ps[bank]], [b_hTb])
            else:
                P.op("dve", lambda e, dst=dst, src=src: e.tensor_copy(out=dst, in_=src), [b_ps[bank]], [b_hTb])

    if stop_after == 0:
        issue_casts(1000)
        P.op("sp", lambda e: e.nop(), list(dram.values()), [])
        P.emit()
        return nc

    phase_start()
    xin = [sb("xin", [128, D], F32) for _ in range(2)]
    b_xin = [P.buf("xin", dma=True) for _ in range(2)]
    gpre = sb("gpre", [128, D], F32); b_gpre = P.buf("gpre", dma=True)
    junk = sb("junk", [128, D], BF16); b_junk = P.buf("junk")
    hbf = [sb("hbf", [128, D], BF16) for _ in range(4)]
    b_hbf = [P.buf("hbf") for _ in range(4)]
    hT = [sb("hT", [128, 16, 512], BF16) for _ in range(2)]
    b_hT = [P.buf("hT") for _ in range(2)]
    wr = [sb("wr", [128, 16, 512], BF16) for _ in range(3)]
    b_wr = [P.buf("wr", dma=True) for _ in range(3)]
    cst = [sb("cs", [128, 2, 512], F32) for _ in range(2)]
    b_cst = [P.buf("cs", dma=True) for _ in range(2)]
    qsb = [sb("qsb", [128, 512], BF16) for _ in range(2)]
    b_qsb = [P.buf("qsb") for _ in range(2)]
    t1 = [sb("t1", [128, 512], F32) for _ in range(2)]
    b_t1 = [P.buf("t1") for _ in range(2)]
    t2 = [sb("t2", [128, 512], F32) for _ in range(2)]
    b_t2 = [P.buf("t2") for _ in range(2)]
    stage = [sb("stage", [128, 4, 512], BF16) for _ in range(3)]
    b_stage = [P.buf("stage", dma=True) for _ in range(3)]
    sm = [[sb("sm", [128, 1], F32) for _ in range(3)] for _ in range(2)]
    b_sm = [[P.buf("sm") for _ in range(3)] for _ in range(2)]

    dma("sp", gpre[:], gains[0:1, :].partition_broadcast(128), [], [b_gpre], b_gpre)

    blocks = [("own", b) for b in range(NB)] + [("oth", b) for b in range(NB)]

    def tiles_of(kind, b):
        if kind == "own":
            return list(range(20))
        tl = [2, 3, 4, 5]
        if b == 0 or b == NB - 1:
            tl += [8, 9, 10, 11]
        return tl

    items = []
    for bi, (kind, b) in enumerate(blocks):
        tl = tiles_of(kind, b)
        for k_, j in enumerate(tl):
            items.append((bi, kind, b, j, k_, len(tl)))
    p1 = {"wl": 0, "bank": 0, "rope": 0, "ntile": 0, "stores": []}

    def ensure_wloads(upto):
        while p1["wl"] < min(upto, len(items)):
            n = p1["wl"]
            j = items[n][3]
            ws = n % 3
            dma("sp", wr[ws][:], wi_bf[j], [dbuf(("wi", j))], [b_wr[ws]], b_wr[ws])
            p1["wl"] += 1
            if cast_pos[0] < 20:
                issue_casts(1)

    def prologue_nonpe(bi):
        kind, b = blocks[bi]
        xsrc = x_own if kind == "own" else x_oth
        cs_src = cs_own if kind == "own" else cs_oth
        for tt in range(4):
            i = p1["ntile"] % 2
            p1["ntile"] += 1
            tok0 = b * 512 + tt * 128
            dma("sp", xin[i][:], xsrc[tok0:tok0 + 128, :], [], [b_xin[i]], b_xin[i])
            norm_tile(xin[i], b_xin[i], gpre, b_gpre, junk, b_junk, sm[i][0], b_sm[i][0], sm[i][1], b_sm[i][1],
                      sm[i][2], b_sm[i][2], hbf[tt], b_hbf[tt])
        csl = bi % 2
        dma("sp", cst[csl][:], cs_src[:, :, b * 512:(b + 1) * 512], [], [b_cst[csl]], b_cst[csl])

    def prologue_pe(bi):
        for tt in range(4):
            transpose_tile(hbf[tt], b_hbf[tt], hT[bi % 2], b_hT[bi % 2], tt)

    def flush_stores():
        for fn in p1["stores"]:
            fn()
        p1["stores"] = []

    issue_casts(4)
    prologue_nonpe(0)
    prologue_pe(0)
    for n, (bi, kind, b, j, kidx, ntl) in enumerate(items):
        ensure_wloads(n + 3)
        hTb, b_hTb = hT[bi % 2], b_hT[bi % 2]
        koff = 0 if kind == "own" else T
        csl = bi % 2
        ws = n % 3
        st = n % 3
        stg, b_stg = stage[st], b_stage[st]
        typ = ["qa", "qa", "ka", "ka", "va", "va", "qn", "qn", "kn", "kn", "vn", "vn"][j] if j < 12 else "gate"
        new_stores = []

        def store(dst, src, key, stg=stg, b_stg=b_stg):
            new_stores.append(lambda: dma("sp", dst, src, [b_stg], [dbuf(key)], b_stg))

        if typ in ("qa", "ka", "qn", "kn", "gate"):
            for ct in range(4):
                bank = p1["bank"] % 4
                p1["bank"] += 1

                def mm(e, ws=ws, ct=ct, bank=bank, hTb=hTb):
                    i_ = None
                    for kc in range(16):
                        i_ = e.matmul(ps[bank][:], lhsT=wr[ws][:, kc, ct * 128:(ct + 1) * 128], rhs=hTb[:, kc, :],
                                      start=(kc == 0), stop=(kc == 15))
                    return i_
                P.op("pe", mm, [b_wr[ws], b_hTb], [b_ps[bank]])
                if typ in ("qa", "ka"):
                    r = p1["rope"] % 2
                    p1["rope"] += 1
                    rb = 4 + r
                    P.op("act", lambda e, r=r, bank=bank: e.copy(out=qsb[r][:], in_=ps[bank][:]),
                         [b_ps[bank]], [b_qsb[r]])
                    P.op("pe", lambda e, r=r, rb=rb: e.matmul(ps[rb][:], lhsT=Rm[:], rhs=qsb[r][:], start=True, stop=True),
                         [b_qsb[r], b_Rm], [b_ps[rb]])
                    P.op("dve", lambda e, r=r, bank=bank, csl=csl: e.tensor_tensor(
                        out=t1[r][:], in0=ps[bank][:], in1=cst[csl][:, 0, :], op=ALU.mult),
                        [b_ps[bank], b_cst[csl]], [b_t1[r]])
                    P.op("dve", lambda e, r=r, rb=rb, csl=csl: e.tensor_tensor(
                        out=t2[r][:], in0=ps[rb][:], in1=cst[csl][:, 1, :], op=ALU.mult),
                        [b_ps[rb], b_cst[csl]], [b_t2[r]])
                    P.op("pool", lambda e, r=r, stg=stg, ct=ct: e.tensor_tensor(
                        out=stg[:, ct, :], in0=t1[r][:], in1=t2[r][:], op=ALU.add),
                        [b_t1[r], b_t2[r]], [b_stg])
                elif typ == "gate":
                    P.op("act", lambda e, bank=bank, stg=stg, ct=ct: e.activation(
                        out=stg[:, ct, :], in_=ps[bank][:], func=AF.Sigmoid), [b_ps[bank]], [b_stg])
                else:
                    P.op("act", lambda e, bank=bank, stg=stg, ct=ct: e.copy(out=stg[:, ct, :], in_=ps[bank][:]),
                         [b_ps[bank]], [b_stg])
            if typ == "qa":
                c0 = (j - 0) * 4
                store(QaT[c0:c0 + 4, :, b * 512:(b + 1) * 512].rearrange("c p n -> p c n"), stg[:], ("QaT", c0, b))
            elif typ == "ka":
                c0 = (j - 2) * 4
                store(KaT[c0:c0 + 4, :, koff + b * 512:koff + (b + 1) * 512].rearrange("c p n -> p c n"), stg[:],
                      ("KaT", c0, kind, b))
            elif typ == "qn":
                c0 = (j - 6) * 4
                store(QnT[c0:c0 + 4, :, b * 512:(b + 1) * 512].rearrange("c p n -> p c n"), stg[:], ("QnT", c0, b))
            elif typ == "kn":
                c0 = (j - 8) * 4
                if kind == "own":
                    e0 = 256 + b * 512
                    store(KnT[c0:c0 + 4, :, e0:e0 + 512].rearrange("c p n -> p c n"), stg[:], ("KnT", c0, kind, b))
                else:
                    if b == NB - 1:
                        store(KnT[c0:c0 + 4, :, 0:256].rearrange("c p n -> p c n"), stg[:, :, 256:512],
                              ("KnT", c0, kind, b, 0))
                    if b == 0:
                        e0 = (NQT + 2) * 128
                        store(KnT[c0:c0 + 4, :, e0:e0 + 256].rearrange("c p n -> p c n"), stg[:, :, 0:256],
                              ("KnT", c0, kind, b, 1))
            else:
                c0 = (j - 12) * 4
                store(gT[c0:c0 + 4, :, b * 512:(b + 1) * 512].rearrange("c p n -> p c n"), stg[:], ("gT", c0, b))
        else:
            for tt in range(4):
                bank = p1["bank"] % 4
                p1["bank"] += 1

                def mmv(e, ws=ws, tt=tt, bank=bank, hTb=hTb):
                    i_ = None
                    for kc in range(16):
                        i_ = e.matmul(ps[bank][:], lhsT=hTb[:, kc, tt * 128:(tt + 1) * 128], rhs=wr[ws][:, kc, :],
                                      start=(kc == 0), stop=(kc == 15))
                    return i_
                P.op("pe", mmv, [b_wr[ws], b_hTb], [b_ps[bank]])
                if tt % 2 == 0:
                    P.op("dve", lambda e, bank=bank, stg=stg, tt=tt: e.tensor_copy(out=stg[:, tt, :], in_=ps[bank][:]),
                         [b_ps[bank]], [b_stg])
                else:
                    P.op("act", lambda e, bank=bank, stg=stg, tt=tt: e.copy(out=stg[:, tt, :], in_=ps[bank][:]),
                         [b_ps[bank]], [b_stg])
            if typ == "va":
                for hh in range(2):
                    h = (j - 4) * 2 + hh
                    store(Va[h, koff + b * 512:koff + (b + 1) * 512, :].rearrange("(t p) e -> p t e", p=128),
                          stg[:, :, hh * 256:(hh + 1) * 256], ("Va", h, kind, b))
            else:
                for hh in range(4):
                    h = (j - 10) * 4 + hh
                    if kind == "own":
                        e0 = 256 + b * 512
                        store(Vn[h, e0:e0 + 512, :].rearrange("(t p) e -> p t e", p=128),
                              stg[:, :, hh * 128:(hh + 1) * 128], ("Vn", h, kind, b))
                    else:
                        if b == NB - 1:
                            store(Vn[h, 0:256, :].rearrange("(t p) e -> p t e", p=128),
                                  stg[:, 2:4, hh * 128:(hh + 1) * 128], ("Vn", h, kind, b, 0))
                        if b == 0:
                            e0 = (NQT + 2) * 128
                            store(Vn[h, e0:e0 + 256, :].rearrange("(t p) e -> p t e", p=128),
                                  stg[:, 0:2, hh * 128:(hh + 1) * 128], ("Vn", h, kind, b, 1))
        flush_stores()
        p1["stores"] = new_stores
        if kidx == min(1, ntl - 1) and bi + 1 < len(blocks):
            prologue_nonpe(bi + 1)
        if kidx == ntl - 1 and bi + 1 < len(blocks):
            prologue_pe(bi + 1)
    flush_stores()

    issue_casts(20 - cast_pos[0])

    def dkeys(prefix):
        return [v for k, v in dram.items() if isinstance(k, tuple) and k[0] == prefix]

    def finish():
        P.op("sp", lambda e: e.nop(), list(dram.values()), [])
        P.emit()
        return nc

    if stop_after == 1:
        return finish()

    phase_start()
    KTs = [[sb("KT", [128, 2 * T], BF16) for _ in range(2)] for _ in range(2)]
    b_KTs = [[P.buf("KT", dma=True) for _ in range(2)] for _ in range(2)]
    QTs = [[sb("QT", [128, T], BF16) for _ in range(2)] for _ in range(2)]
    b_QTs = [[P.buf("QT", dma=True) for _ in range(2)] for _ in range(2)]
    Vts = [sb("Vt", [128, NKT, 258], BF16) for _ in range(2)]
    NVP = 4
    b_Vts = [[P.buf("Vt", dma=True) for _ in range(NVP)] for _ in range(2)]
    b_Vones = [P.buf("Vones") for _ in range(2)]
    raw = [sb("raw", [128, 257], F32) for _ in range(8)]
    b_raw = [P.buf("raw") for _ in range(8)]
    ET = [sb("ET", [128, 512], BF16) for _ in range(3)]
    b_ET = [[P.buf("ET") for _ in range(2)] for _ in range(3)]
    o0 = [sb("o0", [128, 256], F32) for _ in range(4)]
    b_o0 = [P.buf("o0") for _ in range(4)]
    osb = [sb("osb", [128, 256], F32) for _ in range(2)]
    b_osb = [P.buf("osb") for _ in range(2)]
    ojunk = sb("ojunk", [128, 256], BF16); b_ojunk = P.buf("ojunk")
    oabf = [sb("oabf", [128, 256], BF16) for _ in range(4)]
    b_oabf = [P.buf("oabf") for _ in range(4)]
    sw8 = sb("sw8", [128, 256], F32); b_sw8 = P.buf("sw8", dma=True)
    oast = [sb("oast", [128, 2, 512], BF16) for _ in range(2)]
    b_oast = [P.buf("oast", dma=True) for _ in range(2)]
    lt = sb("lt", [128, 512], F32); b_lt = P.buf("lt", dma=True)
    lprod = sb("lprod", [128, 256], F32); b_lprod = P.buf("lprod")
    lsm = [sb("lsm", [128, 1], F32) for _ in range(6)]
    b_lsm = [P.buf("lsm") for _ in range(6)]
    dsm = [[sb("dsm", [128, 1], F32) for _ in range(6)] for _ in range(2)]
    b_dsm = [[P.buf("dsm") for _ in range(6)] for _ in range(2)]
    rz0 = [sb("rz0", [128, 1], F32) for _ in range(4)]
    b_rz0 = [P.buf("rz0") for _ in range(4)]

    dma("sp", lt[:], lamv.partition_broadcast(128), [], [b_lt], b_lt)
    dma("sp", sw8[:], subw.partition_broadcast(128), [], [b_sw8], b_sw8)
    P.op("dve", lambda e: e.tensor_scalar(out=sw8[:], in0=sw8[:], scalar1=float(1.0 - LAMBDA_INIT), scalar2=None,
                                          op0=ALU.mult), [b_sw8], [b_sw8])
    P.op("dve", lambda e: e.tensor_tensor(out=lprod[:, 0:128], in0=lt[:, 0:128], in1=lt[:, 128:256], op=ALU.mult),
         [b_lt], [b_lprod])
    P.op("dve", lambda e: e.tensor_tensor(out=lprod[:, 128:256], in0=lt[:, 256:384], in1=lt[:, 384:512], op=ALU.mult),
         [b_lt], [b_lprod])
    P.op("dve", lambda e: e.reduce_sum(out=lsm[0][:], in_=lprod[:, 0:128], axis=AX.X), [b_lprod], [b_lsm[0]])
    P.op("dve", lambda e: e.reduce_sum(out=lsm[1][:], in_=lprod[:, 128:256], axis=AX.X), [b_lprod], [b_lsm[1]])
    P.op("act", lambda e: e.activation(out=lsm[2][:], in_=lsm[0][:], func=AF.Exp), [b_lsm[0]], [b_lsm[2]])
    P.op("act", lambda e: e.activation(out=lsm[3][:], in_=lsm[1][:], func=AF.Exp), [b_lsm[1]], [b_lsm[3]])
    P.op("dve", lambda e: e.tensor_tensor(out=lsm[4][:], in0=lsm[2][:], in1=lsm[3][:], op=ALU.subtract),
         [b_lsm[2], b_lsm[3]], [b_lsm[4]])
    nlam, b_nlam = lsm[5], b_lsm[5]
    P.op("dve", lambda e: e.tensor_scalar(out=nlam[:], in0=lsm[4][:], scalar1=float(LAMBDA_INIT), scalar2=-1.0,
                                          op0=ALU.add, op1=ALU.mult), [b_lsm[4]], [b_nlam])
    for hs_ in range(2):
        P.op("pool", lambda e, hs_=hs_: e.memset(Vts[hs_][:, :, 256:258], 1.0), [], [b_Vones[hs_]])

    NG = T // 512
    OB = [2, 3, 4, 5, 6]
    state = {"step": 0, "ob": 0, "pend": None, "defer": []}

    def da_evac(h, g, c, banks):
        for i in range(4):
            B = banks[i]
            rw = c * 4 + i
            P.op("dve", lambda e, rw=rw, B=B: e.tensor_copy(out=raw[rw][:], in_=ps[B][:, 0:257]),
                 [b_ps[B]], [b_raw[rw]])
        for i in range(4):
            rw = c * 4 + i
            if c == 0:
                P.op("dve", lambda e, i=i, rw=rw: e.reciprocal(out=rz0[i][:], in_=raw[rw][:, 256:257]),
                     [b_raw[rw]], [b_rz0[i]])
                P.op("dve", lambda e, i=i, rw=rw: e.tensor_scalar(out=o0[i][:], in0=raw[rw][:, 0:256], scalar1=rz0[i][:],
                                                                  scalar2=None, op0=ALU.mult),
                     [b_raw[rw], b_rz0[i]], [b_o0[i]])
            else:
                k = i % 2
                d, bd = dsm[k], b_dsm[k]
                P.op("dve", lambda e, d=d, rw=rw: e.reciprocal(out=d[0][:], in_=raw[rw][:, 256:257]), [b_raw[rw]], [bd[0]])
                P.op("dve", lambda e, d=d: e.tensor_tensor(out=d[1][:], in0=d[0][:], in1=nlam[:], op=ALU.mult),
                     [bd[0], b_nlam], [bd[1]])
                P.op("dve", lambda e, d=d, rw=rw, i=i, k=k: e.scalar_tensor_tensor(
                    out=osb[k][:], in0=raw[rw][:, 0:256], scalar=d[1][:], in1=o0[i][:], op0=ALU.mult, op1=ALU.add),
                    [b_raw[rw], bd[1], b_o0[i]], [b_osb[k]])
                P.op("dve", lambda e, d=d, k=k: e.scalar_tensor_tensor(
                    out=ojunk[:], in0=osb[k][:], scalar=1.0, in1=osb[k][:], op0=ALU.mult, op1=ALU.mult,
                    accum_out=d[2][:]), [b_osb[k]], [b_ojunk, bd[2]])
                rstd_from_ss(d[2], d[3], d[4], bd[2], bd[3], bd[4], 256)
                P.op("dve", lambda e, d=d, k=k, i=i: e.scalar_tensor_tensor(
                    out=oabf[i][:], in0=osb[k][:], scalar=d[4][:], in1=sw8[:], op0=ALU.mult, op1=ALU.mult),
                    [b_osb[k], bd[4], b_sw8], [b_oabf[i]])

                def trf(i=i, k=k):
                    def tr(e):
                        i_ = None
                        for jj in range(2):
                            i_ = e.transpose(out=psb[7][:, jj * 512 + i * 128: jj * 512 + (i + 1) * 128],
                                             in_=oabf[i][:, jj * 128:(jj + 1) * 128], identity=idt[:])
                        return i_
                    P.op("pe", tr, [b_oabf[i], b_idt], [b_ps[7]])
                state["defer"].append((state["step"] + 6 + 2 * i, trf))
        if c == 1:
            def fin(h=h, g=g):
                s = (h * NG + g) % 2
                P.op("dve", lambda e, s=s: e.tensor_copy(out=oast[s][:].rearrange("p c n -> p (c n)"), in_=psb[7]),
                     [b_ps[7]], [b_oast[s]])
                dst = oaT[2 * h:2 * h + 2, :, g * 512:(g + 1) * 512].rearrange("c p n -> p c n")
                dma("sp", dst, oast[s][:], [b_oast[s]], [dbuf(("oaT", h, g))], b_oast[s])
            state["defer"].append((state["step"] + 14, fin))

    def run_deferred(force=False):
        keep = []
        for (at, fn) in state["defer"]:
            if force or at <= state["step"]:
                fn()
            else:
                keep.append((at, fn))
        state["defer"] = keep

    def emit_pv(pend):
        h, g, c, kt, banks, es = pend
        hs = h % 2

        for hf_ in range(2):
            def pv(e, hf_=hf_):
                i_ = None
                for i in (2 * hf_, 2 * hf_ + 1):
                    i_ = e.matmul(ps[banks[i]][:, 0:257], lhsT=ET[es][:, i * 128:(i + 1) * 128],
                                  rhs=Vts[hs][:, kt, 0:257], start=(kt == 0), stop=(kt == NKT - 1))
                return i_
            P.op("pe", pv, [b_ET[es][hf_], b_Vts[hs][kt * NVP // NKT], b_Vones[hs]],
                 [b_ps[banks[2 * hf_]], b_ps[banks[2 * hf_ + 1]]])
        if kt == NKT - 1:
            da_evac(h, g, c, banks)

    def da_loads(h):
        hs = h % 2
        for c in range(2):
            dma("sp", KTs[hs][c][:], KaT[2 * h + c], dkeys("KaT"), [b_KTs[hs][c]], b_KTs[hs][c])
            dma("sp", QTs[hs][c][:], QaT[2 * h + c], dkeys("QaT"), [b_QTs[hs][c]], b_QTs[hs][c])
        for vp in range(NVP):
            k0 = vp * NKT // NVP
            k1 = (vp + 1) * NKT // NVP
            src = Va[h, k0 * 128:k1 * 128, :].rearrange("(t p) e -> p t e", p=128)
            dma("sp", Vts[hs][:, k0:k1, 0:256], src, dkeys("Va"), [b_Vts[hs][vp]], b_Vts[hs][vp])

    da_loads(0)
    for h in range(4):
        hs = h % 2
        KT, b_KT, QT, b_QT = KTs[hs], b_KTs[hs], QTs[hs], b_QTs[hs]
        for g in range(NG):
            if g == 1 and h + 1 < 4:
                da_loads(h + 1)
            for c in range(2):
                banks = [OB[(state["ob"] + i) % 5] for i in range(4)]
                state["ob"] += 4
                for kt in range(NKT):
                    sbk = state["step"] % 2
                    es = state["step"] % 3
                    P.op("pe", lambda e, sbk=sbk, c=c, kt=kt, g=g, KT=KT, QT=QT: e.matmul(
                        ps[sbk][:], lhsT=KT[c][:, kt * 128:(kt + 1) * 128], rhs=QT[c][:, g * 512:(g + 1) * 512],
                        start=True, stop=True), [b_KT[c], b_QT[c]], [b_ps[sbk]])
                    for hf_ in range(2):
                        P.op("act", lambda e, sbk=sbk, es=es, hf_=hf_: e.activation(
                            out=ET[es][:, hf_ * 256:(hf_ + 1) * 256], in_=ps[sbk][:, hf_ * 256:(hf_ + 1) * 256],
                            func=AF.Exp, scale=float(SCALE)), [b_ps[sbk]], [b_ET[es][hf_]])
                    if state["pend"] is not None:
                        emit_pv(state["pend"])
                    state["pend"] = (h, g, c, kt, banks, es)
                    state["step"] += 1
                    run_deferred()
                    if state["step"] % 48 == 0:
                        issue_casts(1)
    emit_pv(state["pend"])
    state["pend"] = None
    run_deferred(force=True)

    issue_casts(1000)
    if stop_after == 2:
        return finish()

    phase_start()
    Qn = [sb("Qn", [128, T], BF16) for _ in range(2)]
    b_Qn = [P.buf("Qn", dma=True) for _ in range(2)]
    Kn = [sb("Kn", [128, NEXT * 128], BF16) for _ in range(2)]
    b_Kn = [P.buf("Kn", dma=True) for _ in range(2)]
    Vnt = [sb("Vnt", [128, NEXT, 130], BF16) for _ in range(2)]
    b_Vnt = [P.buf("Vnt", dma=True) for _ in range(2)]
    b_Vn1 = [P.buf("Vn1") for _ in range(2)]
    nbt = [sb("nbt", [128, 27, 128], F32) for _ in range(2)]
    b_nbt = [P.buf("nbt", dma=True) for _ in range(2)]
    ssb = [sb("ssb", [128, 768], F32) for _ in range(3)]
    b_ssb = [P.buf("ssb") for _ in range(3)]
    ETn = [sb("ETn", [128, 768], BF16) for _ in range(3)]
    b_ETn = [P.buf("ETn") for _ in range(3)]
    rzn = [sb("rzn", [128, 1], F32) for _ in range(3)]
    b_rzn = [P.buf("rzn") for _ in range(3)]
    onbf = [sb("onbf", [128, 128], BF16) for _ in range(3)]
    b_onbf = [P.buf("onbf") for _ in range(3)]
    onst = [sb("onst", [128, T], BF16) for _ in range(2)]
    b_onst = [P.buf("onst", dma=True) for _ in range(2)]
    for s_ in range(2):
        P.op("pool", lambda e, s_=s_: e.memset(Vnt[s_][:, :, 128:130], 1.0), [], [b_Vn1[s_]])

    def na_tiles(r):
        if r == 0:
            return list(range(0, 6)), 5
        if r == 1:
            return list(range(1, 6)), 11
        if r == NQT - 2:
            return list(range(r, r + 5)), 16
        if r == NQT - 1:
            return list(range(r - 1, r + 5)), 21
        return list(range(r, r + 5)), 0

    na_state = {"pend": [], "cnt": 0}
    NA_LAG = 2

    def na_pv(pend):
        h, s, r, tl, ws, ob = pend
        n = len(tl)

        def pv(e):
            i_ = None
            for i, et in enumerate(tl):
                i_ = e.matmul(ps[ob][:, 0:129], lhsT=ETn[ws][:, i * 128:(i + 1) * 128], rhs=Vnt[s][:, et, 0:129],
                              start=(i == 0), stop=(i == n - 1))
            return i_
        P.op("pe", pv, [b_ETn[ws], b_Vnt[s], b_Vn1[s]], [b_ps[ob]])
        P.op("dve", lambda e: e.reciprocal(out=rzn[ws][:], in_=ps[ob][:, 128:129]), [b_ps[ob]], [b_rzn[ws]])
        P.op("dve", lambda e: e.tensor_scalar(out=onbf[ws][:], in0=ps[ob][:, 0:128], scalar1=rzn[ws][:], scalar2=None,
                                              op0=ALU.mult), [b_ps[ob], b_rzn[ws]], [b_onbf[ws]])
        prev_t = na_state.get("tq")

        def tpart():
            tb = 6 + (r // 8) % 2
            P.op("pe", lambda e: e.transpose(out=psb[tb][:, (r % 8) * 128:(r % 8 + 1) * 128], in_=onbf[ws][:],
                                             identity=idt[:]), [b_onbf[ws], b_idt], [b_ps[tb]])
            if r % 8 == 7:
                r0 = r - 7
                P.op("dve", lambda e: e.tensor_copy(out=onst[s][:, r0 * 128:(r0 + 8) * 128], in_=psb[tb]),
                     [b_ps[tb]], [b_onst[s]])
            if r == NQT - 1:
                dma("sp", onT[h], onst[s][:], [b_onst[s]], [dbuf(("onT", h))], b_onst[s])
        na_state["tq"] = tpart
        if prev_t is not None:
            prev_t()

    def na_loads(h):
        s = h % 2
        dma("sp", Qn[s][:], QnT[h], dkeys("QnT"), [b_Qn[s]], b_Qn[s])
        dma("sp", Kn[s][:], KnT[h], dkeys("KnT"), [b_Kn[s]], b_Kn[s])
        dma("sp", Vnt[s][:, :, 0:128], Vn[h].rearrange("(t p) e -> p t e", p=128), dkeys("Vn"), [b_Vnt[s]], b_Vnt[s])
        dma("sp", nbt[s][:], nbias[h], [], [b_nbt[s]], b_nbt[s])

    na_loads(0)
    for h in range(8):
        s = h % 2
        if h + 1 < 8:
            while na_state["pend"] and na_state["pend"][0][0] < h:
                na_pv(na_state["pend"].pop(0))
            na_loads(h + 1)
        for r in range(NQT):
            tl, bi0 = na_tiles(r)
            n = len(tl)
            cnt = na_state["cnt"]
            na_state["cnt"] += 1
            ws = cnt % 3
            sset = cnt % 2
            sb0, sb1 = 2 * sset, 2 * sset + 1
            ob = 4 + sset

            def smm(e, tl=tl, s=s, r=r, sb0=sb0, sb1=sb1):
                i_ = None
                for i, et in enumerate(tl):
                    bk = sb0 if i < 4 else sb1
                    i_ = e.matmul(ps[bk][:, (i % 4) * 128:(i % 4 + 1) * 128], lhsT=Kn[s][:, et * 128:(et + 1) * 128],
                                  rhs=Qn[s][:, r * 128:(r + 1) * 128], start=True, stop=True)
                return i_
            P.op("pe", smm, [b_Kn[s], b_Qn[s]], [b_ps[sb0], b_ps[sb1]])
            n0 = min(n, 4)
            P.op("dve", lambda e, ws=ws, sb0=sb0, n0=n0, s=s, bi0=bi0: e.scalar_tensor_tensor(
                out=ssb[ws][:, 0:n0 * 128], in0=ps[sb0][:, 0:n0 * 128], scalar=float(SCALE),
                in1=nbt[s][:, bi0:bi0 + n0, :].rearrange("p a b -> p (a b)"), op0=ALU.mult, op1=ALU.add),
                [b_ps[sb0], b_nbt[s]], [b_ssb[ws]])
            if n > 4:
                n1 = n - 4
                P.op("dve", lambda e, ws=ws, sb1=sb1, n1=n1, s=s, bi0=bi0: e.scalar_tensor_tensor(
                    out=ssb[ws][:, 512:512 + n1 * 128], in0=ps[sb1][:, 0:n1 * 128], scalar=float(SCALE),
                    in1=nbt[s][:, bi0 + 4:bi0 + 4 + n1, :].rearrange("p a b -> p (a b)"), op0=ALU.mult, op1=ALU.add),
                    [b_ps[sb1], b_nbt[s]], [b_ssb[ws]])
            P.op("act", lambda e, ws=ws, n=n: e.activation(out=ETn[ws][:, 0:n * 128], in_=ssb[ws][:, 0:n * 128],
                                                           func=AF.Exp), [b_ssb[ws]], [b_ETn[ws]])
            na_state["pend"].append((h, s, r, tl, ws, ob))
            if len(na_state["pend"]) > NA_LAG:
                na_pv(na_state["pend"].pop(0))
    while na_state["pend"]:
        na_pv(na_state["pend"].pop(0))
    if na_state.get("tq") is not None:
        na_state["tq"]()
        na_state["tq"] = None

    if stop_after == 3:
        return finish()

    phase_start()
    wr4 = [sb("wr4", [128, 16, 512], BF16) for _ in range(3)]
    b_wr4 = [P.buf("wr4", dma=True) for _ in range(3)]
    gta = [sb("gta", [128, 4, 512], BF16) for _ in range(3)]
    b_gta = [P.buf("gta", dma=True) for _ in range(3)]
    gtb = [sb("gtb", [128, 4, 512], BF16) for _ in range(3)]
    b_gtb = [P.buf("gtb", dma=True) for _ in range(3)]
    oab = [sb("oab", [128, 8, 512], BF16) for _ in range(2)]
    b_oab = [P.buf("oab", dma=True) for _ in range(2)]
    onb = [sb("onb", [128, 8, 512], BF16) for _ in range(2)]
    b_onb = [P.buf("onb", dma=True) for _ in range(2)]
    mixT = sb("mixT", [128, 16, 512], BF16); b_mixT = P.buf("mixT")
    ta = [sb("ta", [128, 512], F32) for _ in range(2)]
    b_ta = [P.buf("ta") for _ in range(2)]
    tb_ = [sb("tb", [128, 512], F32) for _ in range(2)]
    b_tb = [P.buf("tb") for _ in range(2)]
    ysb = [sb("ysb", [128, D], F32) for _ in range(4)]
    b_ysb = [P.buf("ysb") for _ in range(4)]
    xin4 = [sb("xin4", [128, D], F32) for _ in range(2)]
    b_xin4 = [P.buf("xin4", dma=True) for _ in range(2)]
    gpost = sb("gpost", [128, D], F32); b_gpost = P.buf("gpost", dma=True)
    junk4 = sb("junk4", [128, D], BF16); b_junk4 = P.buf("junk4")
    sm4 = [[sb("sm4", [128, 1], F32) for _ in range(3)] for _ in range(2)]
    b_sm4 = [[P.buf("sm4") for _ in range(3)] for _ in range(2)]
    dma("sp", gpost[:], gains[1:2, :].partition_broadcast(128), [], [b_gpost], b_gpost)

    items4 = []
    for b in range(NB):
        for ctg in range(4):
            items4.append((b, "ab", ctg))
        for cg in range(4):
            items4.append((b, "out", cg))
    p4 = {"wl": 0, "bk": 0, "tc": 0, "xc": 0, "tail": [], "st": None}

    def ensure_loads4(upto):
        while p4["wl"] < min(upto, len(items4)):
            n = p4["wl"]
            b, typ, idx = items4[n]
            ws = n % 3
            if typ == "ab":
                gs = (n // 8 * 4 + idx) % 3
                dma("sp", wr4[ws][:], wab_bf[idx], [dbuf(("wab", idx))], [b_wr4[ws]], b_wr4[ws])
                dma("sp", gta[gs][:], gT[idx * 4:idx * 4 + 4, :, b * 512:(b + 1) * 512].rearrange("c p n -> p c n"),
                    dkeys("gT"), [b_gta[gs]], b_gta[gs])
                dma("sp", gtb[gs][:],
                    gT[16 + idx * 4:16 + idx * 4 + 4, :, b * 512:(b + 1) * 512].rearrange("c p n -> p c n"),
                    dkeys("gT"), [b_gtb[gs]], b_gtb[gs])
            else:
                dma("sp", wr4[ws][:], wout_bf[idx], [dbuf(("wout", idx))], [b_wr4[ws]], b_wr4[ws])
            p4["wl"] += 1

    def load_blk4(b):
        s_ = b % 2
        dma("sp", oab[s_][:], oaT[:, :, b * 512:(b + 1) * 512].rearrange("c p n -> p c n"), dkeys("oaT"),
            [b_oab[s_]], b_oab[s_])
        dma("sp", onb[s_][:], onT[:, :, b * 512:(b + 1) * 512].rearrange("c p n -> p c n"), dkeys("onT"),
            [b_onb[s_]], b_onb[s_])

    load_blk4(0)
    for n, (b, typ, idx) in enumerate(items4):
        ensure_loads4(n + 3)
        s = b % 2
        ws = n % 3
        if typ == "ab":
            ctg = idx
            gs = (n // 8 * 4 + idx) % 3
            if idx == 0 and b + 1 < NB:
                load_blk4(b + 1)
            for ct in range(4):
                bA = p4["bk"] % 4
                bB = (p4["bk"] + 1) % 4
                p4["bk"] += 2

                def mma(e, ws=ws, ct=ct, bA=bA, s=s):
                    i_ = None
                    for kc in range(8):
                        i_ = e.matmul(ps[bA][:], lhsT=wr4[ws][:, kc, ct * 128:(ct + 1) * 128], rhs=oab[s][:, kc, :],
                                      start=(kc == 0), stop=(kc == 7))
                    return i_

                def mmb(e, ws=ws, ct=ct, bB=bB, s=s):
                    i_ = None
                    for kc in range(8):
                        i_ = e.matmul(ps[bB][:], lhsT=wr4[ws][:, 8 + kc, ct * 128:(ct + 1) * 128], rhs=onb[s][:, kc, :],
                                      start=(kc == 0), stop=(kc == 7))
                    return i_
                P.op("pe", mma, [b_wr4[ws], b_oab[s]], [b_ps[bA]])
                P.op("pe", mmb, [b_wr4[ws], b_onb[s]], [b_ps[bB]])
                k = p4["tc"] % 2
                p4["tc"] += 1
                P.op("dve", lambda e, k=k, bA=bA, gs=gs, ct=ct: e.tensor_tensor(
                    out=ta[k][:], in0=ps[bA][:], in1=gta[gs][:, ct, :], op=ALU.mult), [b_ps[bA], b_gta[gs]], [b_ta[k]])
                P.op("dve", lambda e, k=k, bB=bB, gs=gs, ct=ct: e.tensor_tensor(
                    out=tb_[k][:], in0=ps[bB][:], in1=gtb[gs][:, ct, :], op=ALU.mult), [b_ps[bB], b_gtb[gs]], [b_tb[k]])
                P.op("pool", lambda e, k=k, ctg=ctg, ct=ct: e.tensor_tensor(
                    out=mixT[:, ctg * 4 + ct, :], in0=ta[k][:], in1=tb_[k][:], op=ALU.add),
                    [b_ta[k], b_tb[k]], [b_mixT])
        else:
            cg = idx
            for tt in range(4):
                bY = p4["bk"] % 4
                p4["bk"] += 1

                def mmy(e, ws=ws, tt=tt, bY=bY):
                    i_ = None
                    for kc in range(16):
                        i_ = e.matmul(ps[bY][:], lhsT=mixT[:, kc, tt * 128:(tt + 1) * 128], rhs=wr4[ws][:, kc, :],
                                      start=(kc == 0), stop=(kc == 15))
                    return i_
                P.op("pe", mmy, [b_wr4[ws], b_mixT], [b_ps[bY]])
                P.op("act", lambda e, tt=tt, cg=cg, bY=bY: e.copy(out=ysb[tt][:, cg * 512:(cg + 1) * 512], in_=ps[bY][:]),
                     [b_ps[bY]], [b_ysb[tt]])
            if cg == 3:
                for tt in range(4):
                    def tail(b=b, tt=tt):
                        if p4["st"] is not None:
                            p4["st"]()
                            p4["st"] = None
                        i = p4["xc"] % 2
                        p4["xc"] += 1
                        tok0 = b * 512 + tt * 128
                        dma("sp", xin4[i][:], x_own[tok0:tok0 + 128, :], [], [b_xin4[i]], b_xin4[i])
                        P.op("act", lambda e, tt=tt, i=i: e.activation(out=junk4[:], in_=ysb[tt][:], func=AF.Square,
                                                                      accum_out=sm4[i][0][:]),
                             [b_ysb[tt]], [b_junk4, b_sm4[i][0]])
                        rstd_from_ss(sm4[i][0], sm4[i][1], sm4[i][2], b_sm4[i][0], b_sm4[i][1], b_sm4[i][2], D)
                        P.op("dve", lambda e, tt=tt, i=i: e.scalar_tensor_tensor(
                            out=ysb[tt][:], in0=ysb[tt][:], scalar=sm4[i][2][:], in1=gpost[:], op0=ALU.mult,
                            op1=ALU.mult), [b_sm4[i][2], b_gpost], [b_ysb[tt]])
                        P.op("dve", lambda e, tt=tt, i=i: e.tensor_tensor(out=xin4[i][:], in0=xin4[i][:], in1=ysb[tt][:],
                                                                          op=ALU.add), [b_ysb[tt]], [b_xin4[i]])
                        p4["st"] = lambda: dma("sp", x1[tok0:tok0 + 128, :], xin4[i][:], [b_xin4[i]],
                                               [dbuf(("x1", b, tt))], b_xin4[i])
                    p4["tail"].append(tail)
        if typ == "ab" and p4["tail"] and (idx < 3 or True):
            p4["tail"].pop(0)()
    while p4["tail"]:
        p4["tail"].pop(0)()
    if p4["st"] is not None:
        p4["st"]()
        p4["st"] = None

    if stop_after == 4:
        return finish()

    phase_start()
    wr5 = [sb("wr5", [128, 16, 512], BF16) for _ in range(2)]
    b_wr5 = [P.buf("wr5", dma=True) for _ in range(2)]
    uT = sb("uT", [128, 64, 512], BF16); b_uT = P.buf("uT")
    h2T = sb("h2T", [128, 16, 512], BF16); b_h2T = P.buf("h2T")
    xin5 = [sb("xin5", [128, D], F32) for _ in range(2)]
    b_xin5 = [P.buf("xin5", dma=True) for _ in range(2)]
    zsb = [sb("zsb", [128, D], F32) for _ in range(4)]
    b_zsb = [P.buf("zsb") for _ in range(4)]
    gm1 = sb("gm1", [128, D], F32); b_gm1 = P.buf("gm1", dma=True)
    gm2 = sb("gm2", [128, D], F32); b_gm2 = P.buf("gm2", dma=True)
    hb5 = [sb("hb5", [128, D], BF16) for _ in range(4)]
    b_hb5 = [P.buf("hb5") for _ in range(4)]
    junk5 = sb("junk5", [128, D], BF16); b_junk5 = P.buf("junk5")
    rr = [sb("rr", [128, 512], F32) for _ in range(2)]
    b_rr = [P.buf("rr") for _ in range(2)]
    sm5 = [[sb("sm5", [128, 1], F32) for _ in range(3)] for _ in range(2)]
    b_sm5 = [[P.buf("sm5") for _ in range(3)] for _ in range(2)]
    dma("sp", gm1[:], gains[2:3, :].partition_broadcast(128), [], [b_gm1], b_gm1)
    dma("sp", gm2[:], gains[3:4, :].partition_broadcast(128), [], [b_gm2], b_gm2)
    out_bufs = []
    p5 = {"wc": 0, "xc": 0, "bk": 0, "rc": 0, "tail": [], "st": None}

    def flush_st5():
        if p5["st"] is not None:
            p5["st"]()
            p5["st"] = None

    def prologue5_nonpe(b):
        flush_st5()
        for tt in range(4):
            i = p5["xc"] % 2
            p5["xc"] += 1
            tok0 = b * 512 + tt * 128
            dma("sp", xin5[i][:], x1[tok0:tok0 + 128, :], [dbuf(("x1", b, tt))], [b_xin5[i]], b_xin5[i])
            norm_tile(xin5[i], b_xin5[i], gm1, b_gm1, junk5, b_junk5, sm5[i][0], b_sm5[i][0], sm5[i][1], b_sm5[i][1],
                      sm5[i][2], b_sm5[i][2], hb5[tt], b_hb5[tt])

    def prologue5_pe():
        for tt in range(4):
            transpose_tile(hb5[tt], b_hb5[tt], h2T, b_h2T, tt)

    def make_tail5(b, tt):
        def tail():
            flush_st5()
            i = p5["xc"] % 2
            p5["xc"] += 1
            tok0 = b * 512 + tt * 128
            dma("sp", xin5[i][:], x1[tok0:tok0 + 128, :], [dbuf(("x1", b, tt))], [b_xin5[i]], b_xin5[i])
            P.op("act", lambda e: e.activation(out=junk5[:], in_=zsb[tt][:], func=AF.Square,
                                               accum_out=sm5[i][0][:]), [b_zsb[tt]], [b_junk5, b_sm5[i][0]])
            rstd_from_ss(sm5[i][0], sm5[i][1], sm5[i][2], b_sm5[i][0], b_sm5[i][1], b_sm5[i][2], D)
            P.op("dve", lambda e: e.scalar_tensor_tensor(
                out=zsb[tt][:], in0=zsb[tt][:], scalar=sm5[i][2][:], in1=gm2[:], op0=ALU.mult, op1=ALU.mult),
                [b_sm5[i][2], b_gm2], [b_zsb[tt]])
            P.op("dve", lambda e: e.tensor_tensor(out=xin5[i][:], in0=xin5[i][:], in1=zsb[tt][:], op=ALU.add),
                 [b_zsb[tt]], [b_xin5[i]])
            ob_ = dbuf(("out", b, tt))
            out_bufs.append(ob_)
            p5["st"] = lambda: dma("sp", out[tok0:tok0 + 128, :], xin5[i][:], [b_xin5[i]], [ob_], b_xin5[i])
        return tail

    prologue5_nonpe(0)
    prologue5_pe()
    for b in range(NB):
        for fg in range(16):
            ws = p5["wc"] % 2
            p5["wc"] += 1
            dma("sp", wr5[ws][:], wup_bf[fg], [dbuf(("wup", fg))], [b_wr5[ws]], b_wr5[ws])
            for fc in range(4):
                bU = p5["bk"] % 2
                p5["bk"] += 1

                def mmu(e, ws=ws, fc=fc, bU=bU):
                    i_ = None
                    for kc in range(16):
                        i_ = e.matmul(ps[bU][:], lhsT=wr5[ws][:, kc, fc * 128:(fc + 1) * 128], rhs=h2T[:, kc, :],
                                      start=(kc == 0), stop=(kc == 15))
                    return i_
                P.op("pe", mmu, [b_wr5[ws], b_h2T], [b_ps[bU]])
                k = p5["rc"] % 2
                p5["rc"] += 1
                P.op("act", lambda e, k=k, bU=bU: e.activation(out=rr[k][:], in_=ps[bU][:], func=AF.Relu),
                     [b_ps[bU]], [b_rr[k]])
                P.op("pool", lambda e, k=k, fg=fg, fc=fc: e.tensor_tensor(
                    out=uT[:, fg * 4 + fc, :], in0=rr[k][:], in1=rr[k][:], op=ALU.mult), [b_rr[k]], [b_uT])
            if p5["tail"] and fg % 2 == 1:
                p5["tail"].pop(0)()
            if fg == 10 and b + 1 < NB:
                prologue5_nonpe(b + 1)
        if b + 1 < NB:
            prologue5_pe()
        for cg in range(4):
            for fs in range(4):
                ws = p5["wc"] % 2
                p5["wc"] += 1
                dma("sp", wr5[ws][:], wdn_bf[cg * 4 + fs], [dbuf(("wdn", cg * 4 + fs))], [b_wr5[ws]], b_wr5[ws])
                for tt in range(4):
                    bZ = 2 + tt

                    def mmz(e, ws=ws, tt=tt, bZ=bZ, fs=fs):
                        i_ = None
                        for fc in range(16):
                            i_ = e.matmul(ps[bZ][:], lhsT=uT[:, fs * 16 + fc, tt * 128:(tt + 1) * 128],
                                          rhs=wr5[ws][:, fc, :], start=(fs == 0 and fc == 0),
                                          stop=(fs == 3 and fc == 15))
                        return i_
                    P.op("pe", mmz, [b_wr5[ws], b_uT], [b_ps[bZ]])
            for tt in range(4):
                bZ = 2 + tt
                P.op("act", lambda e, tt=tt, cg=cg, bZ=bZ: e.copy(out=zsb[tt][:, cg * 512:(cg + 1) * 512], in_=ps[bZ][:]),
                     [b_ps[bZ]], [b_zsb[tt]])
        for tt in range(4):
            p5["tail"].append(make_tail5(b, tt))
    while p5["tail"]:
        p5["tail"].pop(0)()
    flush_st5()

    P.op("sp", lambda e: e.nop(), out_bufs, [])
    P.emit()
    return nc


def _rope_tables(pos):
    inv = (1.0 / (np.float32(10000.0) ** (np.arange(0, 128, 2, dtype=np.float32) / np.float32(128)))).astype(np.float32)
    ang = (pos.astype(np.float32)[:, None] * inv[None, :]).astype(np.float32)
    c = np.cos(ang).astype(np.float32).T
    s = np.sin(ang).astype(np.float32).T
    out = np.empty((128, 2, len(pos)), np.float32)
    out[0:64, 0] = c
    out[64:128, 0] = c
    out[0:64, 1] = -s
    out[64:128, 1] = s
    return out


def _na_bias_layout(rpb, S, half):
    T = S // 2
    NQT = T // 128
    rows = S // GRID_W
    kh = min(8, rows)
    H = rpb.shape[0]
    out = np.full((H, 128, 27, 128), NEG, np.float32)

    def ext_to_global(e):
        if e < 2:
            return (1 - half) * NQT + (NQT - 2 + e)
        if e < NQT + 2:
            return half * NQT + (e - 2)
        return (1 - half) * NQT + (e - NQT - 2)

    pats = [(2, list(range(2, 7)), 0), (0, list(range(0, 6)), 5), (1, list(range(1, 6)), 11),
            (NQT - 2, list(range(NQT - 2, NQT + 3)), 16), (NQT - 1, list(range(NQT - 2, NQT + 4)), 21)]
    kk = np.arange(128)
    ka, kc = kk // 64, kk % 64
    for (r, tl, base) in pats:
        Rg = half * NQT + r
        qrow = 2 * Rg + ka
        qcol = kc
        rs = np.clip(qrow - kh // 2, 0, rows - kh)
        cstart = np.clip(qcol - 8, 0, GRID_W - 16)
        for i, e in enumerate(tl):
            Kg = ext_to_global(e)
            krow = 2 * Kg + ka
            kcol = kc
            valid = ((krow[:, None] >= rs[None, :]) & (krow[:, None] < rs[None, :] + kh) &
                     (kcol[:, None] >= cstart[None, :]) & (kcol[:, None] < cstart[None, :] + 16))
            dr = np.clip(krow[:, None] - qrow[None, :] + 7, 0, 14)
            dc = np.clip(kcol[:, None] - qcol[None, :] + 15, 0, 30)
            g = rpb[:, dr, dc]
            out[:, :, base + i, :] = np.where(valid[None], g, np.float32(NEG))
    return out


_PROG_CACHE = {}


def run_layer(inputs, debug=False):
    x = np.asarray(inputs["x"], np.float32)
    B, S, _ = x.shape
    T = S // 2
    ncores = 2 * B
    key = (T, debug)
    if key not in _PROG_CACHE:
        _PROG_CACHE[key] = build_program(T, debug)
    nc = _PROG_CACHE[key]
    f = lambda k: np.ascontiguousarray(np.asarray(inputs[k], np.float32)[0])
    w_in = f("w_in")
    w_ab = np.ascontiguousarray(np.concatenate([f("w_branch_a"), f("w_branch_b")], axis=0))
    w_out = f("w_out")
    w_up = f("w_up")
    w_dn = f("w_down")
    gains = np.ascontiguousarray(np.stack([f("norm_mix_pre"), f("norm_mix_post"), f("norm_mlp_pre"), f("norm_mlp_post")]))
    lamv = np.ascontiguousarray(np.concatenate([f("lam_q1"), f("lam_k1"), f("lam_q2"), f("lam_k2")])[None, :])
    subw = f("subln_w")[None, :]
    rpb = f("na_rpb")
    ident = np.eye(128, dtype=np.float32).astype(ml_dtypes.bfloat16)
    nb_half = [_na_bias_layout(rpb, S, hf) for hf in range(2)]
    cs_half = [_rope_tables(np.arange(hf * T, (hf + 1) * T)) for hf in range(2)]
    in_maps = []
    for c in range(ncores):
        b, hf = c // 2, c % 2
        in_maps.append({
            "x_own": np.ascontiguousarray(x[b, hf * T:(hf + 1) * T]),
            "x_oth": np.ascontiguousarray(x[b, (1 - hf) * T:(2 - hf) * T]),
            "w_in": w_in, "w_ab": w_ab, "w_out": w_out, "w_up": w_up, "w_dn": w_dn,
            "gains": gains, "lamv": lamv, "subw": subw, "nbias": nb_half[hf],
            "cs_own": cs_half[hf], "cs_oth": cs_half[1 - hf], "ident": ident,
        })
    res = run_bass_kernel_spmd(nc, in_maps, core_ids=list(range(ncores)))
    outp = np.empty((B, S, D), np.float32)
    for c in range(ncores):
        b, hf = c // 2, c % 2
        outp[b, hf * T:(hf + 1) * T] = res.results[c]["out"]
    if debug:
        return outp, res.results
    return outp


def kernel(**inputs):
    return run_layer(inputs)
```
